# Optimizing a Trainium2 kernel written in Bass

```python
import math
import jax, jax.numpy as jnp
from jax import lax
import numpy as np

D_MODEL = 2048
BATCH = 2
SEQ = 4096
DEPTH = 2

N_MIXERS = 2
N_POOL_LAYERS = (DEPTH + 1) // 2
N_ATTN_LAYERS = DEPTH // 2

POOL_WINDOWS = (2, 4, 8, 16)
N_POOL_GROUPS = len(POOL_WINDOWS)
POOL_GROUP = D_MODEL // N_POOL_GROUPS

DIFF_HEAD_DIM = 128
N_DIFF_HEADS = D_MODEL // (2 * DIFF_HEAD_DIM)
V_HEAD_DIM = 2 * DIFF_HEAD_DIM
Q_BLOCK = 128

D_FF = int(math.ceil((8 * D_MODEL / 3) / 256) * 256)

RMS_EPS = 1e-6

kernel_name = "hybrid_pool_diffattn_alibi_swiglu"


def rmsnorm(x, g, eps=RMS_EPS):
    xf = x.astype(jnp.float32)
    y = xf * lax.rsqrt(jnp.mean(xf * xf, axis=-1, keepdims=True) + eps)
    return (y * g.astype(jnp.float32)).astype(x.dtype)


def alibi_slopes(n_heads):
    return jnp.asarray([2.0 ** (-8.0 * (i + 1) / n_heads) for i in range(n_heads)], dtype=jnp.float32)


def causal_trailing_mean(h, w):
    S = h.shape[1]
    c = jnp.cumsum(h.astype(jnp.float32), axis=1)
    shifted = jnp.pad(c, ((0, 0), (w, 0), (0, 0)))[:, :S]
    cnt = jnp.minimum(jnp.arange(1, S + 1), w).astype(jnp.float32)
    return ((c - shifted) / cnt[None, :, None]).astype(h.dtype)


def pool_mixer(h, w_grp, scale):
    B, S, D = h.shape
    hg = h.reshape(B, S, N_POOL_GROUPS, POOL_GROUP)
    pooled = jnp.stack([causal_trailing_mean(hg[:, :, g], POOL_WINDOWS[g]) for g in range(N_POOL_GROUPS)], axis=2)
    y = jnp.einsum('bsgc,gcd->bsgd', pooled - hg, w_grp)
    return y.reshape(B, S, D) * scale


def diff_attention(h, w_qkv, lq1, lk1, lq2, lk2, subln_g, w_o, lambda_init):
    B, S, D = h.shape
    H, d, e = N_DIFF_HEADS, DIFF_HEAD_DIM, V_HEAD_DIM
    qkv = h @ w_qkv
    q = qkv[..., :D].reshape(B, S, H, 2, d)
    k = qkv[..., D:2 * D].reshape(B, S, H, 2, d)
    v = qkv[..., 2 * D:].reshape(B, S, H, e)
    lam = (jnp.exp(jnp.sum(lq1.astype(jnp.float32) * lk1.astype(jnp.float32)))
           - jnp.exp(jnp.sum(lq2.astype(jnp.float32) * lk2.astype(jnp.float32)))
           + lambda_init)
    slopes = alibi_slopes(H)
    scale = d ** -0.5
    nb = S // Q_BLOCK
    q_blocks = q.reshape(B, nb, Q_BLOCK, H, 2, d).transpose(1, 0, 2, 3, 4, 5)
    kpos = jnp.arange(S)

    def one_block(args):
        qb, bi = args
        qpos = bi * Q_BLOCK + jnp.arange(Q_BLOCK)
        s = jnp.einsum('bqhcd,bkhcd->bhcqk', qb, k, preferred_element_type=jnp.float32) * scale
        dist = (qpos[:, None] - kpos[None, :]).astype(jnp.float32)
        bias = -slopes[:, None, None] * dist[None]
        s = jnp.where((dist >= 0)[None, None, None], s + bias[None, :, None], -jnp.inf)
        p = jax.nn.softmax(s, axis=-1)
        a = p[:, :, 0] - lam * p[:, :, 1]
        return jnp.einsum('bhqk,bkhe->bqhe', a.astype(v.dtype), v)

    o = lax.map(one_block, (q_blocks, jnp.arange(nb)))
    o = o.transpose(1, 0, 2, 3, 4).reshape(B, S, H, e)
    o = rmsnorm(o, subln_g) * (1.0 - lambda_init)
    return o.reshape(B, S, H * e) @ w_o


def swiglu(h, wg, wu, wd):
    return (jax.nn.silu(h @ wg) * (h @ wu)) @ wd


def setup_inputs(seed: int = 0) -> dict:
    key = jax.random.key(seed)
    ks = jax.random.split(key, 20)
    f32 = jnp.float32
    D = D_MODEL

    def nrm(k, shape, fan_in):
        return jax.random.normal(k, shape, f32) * (fan_in ** -0.5)

    def gain(k, shape):
        return 1.0 + 0.02 * jax.random.normal(k, shape, f32)

    return {
        "x": jax.random.normal(ks[0], (BATCH, SEQ, D), f32),
        "norm_mix": gain(ks[1], (DEPTH, D)),
        "norm_ffn": gain(ks[2], (DEPTH, D)),
        "pool_w": nrm(ks[3], (N_POOL_LAYERS, N_POOL_GROUPS, POOL_GROUP, POOL_GROUP), POOL_GROUP),
        "pool_scale": gain(ks[4], (N_POOL_LAYERS, D)),
        "w_qkv": nrm(ks[5], (N_ATTN_LAYERS, D, 3 * D), D),
        "lambda_q1": 0.1 * jax.random.normal(ks[6], (N_ATTN_LAYERS, DIFF_HEAD_DIM), f32),
        "lambda_k1": 0.1 * jax.random.normal(ks[7], (N_ATTN_LAYERS, DIFF_HEAD_DIM), f32),
        "lambda_q2": 0.1 * jax.random.normal(ks[8], (N_ATTN_LAYERS, DIFF_HEAD_DIM), f32),
        "lambda_k2": 0.1 * jax.random.normal(ks[9], (N_ATTN_LAYERS, DIFF_HEAD_DIM), f32),
        "subln_g": gain(ks[10], (N_ATTN_LAYERS, V_HEAD_DIM)),
        "w_o": nrm(ks[11], (N_ATTN_LAYERS, D, D), D),
        "w_gate": nrm(ks[12], (DEPTH, D, D_FF), D),
        "w_up": nrm(ks[13], (DEPTH, D, D_FF), D),
        "w_down": nrm(ks[14], (DEPTH, D_FF, D), D_FF),
        "final_norm": gain(ks[15], (D,)),
    }


def reference(x, norm_mix, norm_ffn, pool_w, pool_scale, w_qkv, lambda_q1, lambda_k1,
              lambda_q2, lambda_k2, subln_g, w_o, w_gate, w_up, w_down, final_norm):
    h = x
    for i in range(DEPTH):
        j = i // N_MIXERS
        hn = rmsnorm(h, norm_mix[i])
        if i % N_MIXERS == 0:
            mix = pool_mixer(hn, pool_w[j], pool_scale[j])
        else:
            lambda_init = 0.8 - 0.6 * math.exp(-0.3 * i)
            mix = diff_attention(hn, w_qkv[j], lambda_q1[j], lambda_k1[j], lambda_q2[j],
                                 lambda_k2[j], subln_g[j], w_o[j], lambda_init)
        h = h + mix
        h = h + swiglu(rmsnorm(h, norm_ffn[i]), w_gate[i], w_up[i], w_down[i])
    return rmsnorm(h, final_norm)
```

```python
import math
import numpy as np
import ml_dtypes
import concourse.bass as bass
import concourse.mybir as mybir
from concourse.bass_utils import run_bass_kernel_spmd

F32 = mybir.dt.float32
BF16 = mybir.dt.bfloat16
AF = mybir.ActivationFunctionType
ALU = mybir.AluOpType

NCORES = 8
D = 2048
T = 1024
HALO = 16
TE = T + HALO
KC = 16
FF = 5632
FC = 44
NFB = 4
FCB = FC // NFB
NH = 8
WS = 8192
NSLOT = 3
EPS = 1e-6
LAMBDA_INIT = 0.8 - 0.6 * math.exp(-0.3 * 1)
SCALE = 128 ** -0.5
SLOPES = [2.0 ** (-8.0 * (i + 1) / NH) for i in range(NH)]
QBH = [128, 256, 512, 512, 512, 512, 512, 512]
NEG = -30000.0
GROUPS = [[0, 1, 2, 3], [4, 5, 6, 7]]

C_NMIX0, C_NFFN0, C_PSCALE, C_NMIX1, C_NFFN1, C_FINAL = [i * 16 for i in range(6)]
NCOLS = 96


def weight_tiles():
    tiles = [("pool", (), 8192)]

    def ffn(l):
        out = []
        for blk in range(NFB):
            j0 = blk * FCB
            jl = 0
            while jl < FCB:
                nj = min(2, FCB - jl)
                out.append(("gu", (l, j0 + jl, nj), nj * 4096))
                jl += nj
            for dg in range(4):
                out.append(("down", (l, blk, dg), FCB * 512))
        return out

    tiles += ffn(0)
    for t in range(4):
        tiles.append(("k", (t,), 8192))
    for t in range(4):
        tiles.append(("v", (t,), 8192))
    for t in range(4):
        tiles.append(("q", (t,), 8192))
    for t in range(4):
        tiles.append(("o", (t,), 8192))
    tiles += ffn(1)
    return tiles


def pack_weights(inp):
    tiles = weight_tiles()
    w = np.zeros((len(tiles), 128, WS), np.float32)
    wqkv = inp["w_qkv"][0]
    for i, (kind, a, n) in enumerate(tiles):
        if kind == "pool":
            pw = inp["pool_w"][0].reshape(4, 4, 128, 4, 128).transpose(2, 0, 1, 3, 4)
            w[i, :, :n] = pw.reshape(128, n)
        elif kind == "gu":
            l, j0, nj = a
            for jj in range(nj):
                for gu, W in enumerate((inp["w_gate"][l], inp["w_up"][l])):
                    sub = W[:, (j0 + jj) * 128:(j0 + jj + 1) * 128].reshape(16, 128, 128).transpose(1, 0, 2)
                    o = (jj * 2 + gu) * 2048
                    w[i, :, o:o + 2048] = sub.reshape(128, 2048)
        elif kind == "down":
            l, blk, dg = a
            sub = inp["w_down"][l][blk * FCB * 128:(blk + 1) * FCB * 128, dg * 512:(dg + 1) * 512]
            sub = sub.reshape(FCB, 128, 4, 128).transpose(1, 0, 2, 3)
            w[i, :, :n] = sub.reshape(128, n)
        elif kind in ("k", "q", "o"):
            t = a[0]
            if kind == "q":
                W = wqkv[:, 0:2048]
            elif kind == "k":
                W = wqkv[:, 2048:4096]
            else:
                W = inp["w_o"][0]
            sub = W[:, t * 512:(t + 1) * 512].reshape(16, 128, 4, 128).transpose(1, 2, 0, 3)
            w[i, :, :n] = sub.reshape(128, n)
        elif kind == "v":
            t = a[0]
            sub = wqkv[:, 4096 + t * 512:4096 + (t + 1) * 512].reshape(16, 128, 512).transpose(1, 0, 2)
            w[i, :, :n] = sub.reshape(128, n)
    return w


def bias_index():
    idx = {}
    n = 0
    for h in range(NH):
        ns = 512 // QBH[h]
        for blk in range(4):
            for qb in range(2):
                for kt in range(8):
                    for sub in range(ns):
                        idx[(h, blk, qb, kt, sub)] = n
                        n += 1
    return idx, n


BIDX, NBIAS = bias_index()


def make_bias(r):
    b = np.zeros((128, NBIAS), np.float32)
    j = np.arange(128, dtype=np.float64)
    for (h, blk, qb, kt, sub), col in BIDX.items():
        s = r if blk == 3 else blk
        if blk != 3 and s >= r:
            b[:, col] = NEG
            continue
        kpos = s * 1024 + kt * 128 + j
        qref = r * 1024 + qb * 512 + sub * QBH[h] + QBH[h] // 2
        b[:, col] = (SLOPES[h] * (kpos - qref)).astype(np.float32)
    return b


class Tok:
    __slots__ = ("name", "w", "r")

    def __init__(self, name):
        self.name = name
        self.w = None
        self.r = {}


class Prog:
    ENG = ("pe", "act", "dve", "pool", "sp")
    LIMIT = 240

    def __init__(self):
        self.q = {e: [] for e in self.ENG}
        self.cnt = {}
        self.known = {e: {} for e in self.ENG}
        self.semkeys = []
        self.epoch = {}
        self.keysrc = {}
        self.basekeys = {}

    def _key(self, base, inc, src):
        if base not in self.epoch:
            self.epoch[base] = 0
            self.basekeys[base] = []
            k = f"{base}.0"
            self.cnt[k] = 0
            self.semkeys.append(k)
            self.keysrc[k] = src
            self.basekeys[base].append(k)
        k = f"{base}.{self.epoch[base]}"
        if self.cnt[k] + inc > self.LIMIT:
            self.epoch[base] += 1
            k = f"{base}.{self.epoch[base]}"
            self.cnt[k] = 0
            self.semkeys.append(k)
            self.keysrc[k] = src
            self.basekeys[base].append(k)
        return k

    def total(self, base):
        return sum(self.cnt[k] for k in self.basekeys.get(base, []))

    def op(self, eng, fn, reads=(), writes=(), signal=True, dma=None, inc=16):
        src = "dma" if dma else eng
        need = {}

        def consider(ev, kind):
            key, val, s = ev
            if s == src and s != "dma":
                if s == "pe" or kind != "raw":
                    return
            if self.known[eng].get(key, 0) >= val:
                return
            if need.get(key, 0) < val:
                need[key] = val

        for t in reads:
            if t.w is not None:
                consider(t.w, "raw")
        for t in writes:
            if t.w is not None:
                consider(t.w, "waw")
            for key, (val, s) in t.r.items():
                consider((key, val, s), "war")
        for key, val in need.items():
            self.known[eng][key] = val
            self.q[eng].append(("wait", key, val))
        if dma:
            key = self._key(dma, inc, "dma")
            self.cnt[key] += inc
            ev = (key, self.cnt[key], "dma")
            self.q[eng].append(("dma", fn, key, inc))
        else:
            key = self._key(eng, 1, eng)
            if signal:
                self.cnt[key] += 1
                ev = (key, self.cnt[key], eng)
                self.q[eng].append(("sig", fn, key))
            else:
                ev = (key, self.cnt[key] + 1, eng)
                self.q[eng].append(("nosig", fn))
        for t in reads:
            old = t.r.get(ev[0])
            if old is None or old[0] < ev[1]:
                t.r[ev[0]] = (ev[1], ev[2])
        for t in writes:
            t.w = ev
            t.r = {}
        return ev

    def last_event(self, base):
        k = self.basekeys[base][-1]
        return (k, self.cnt[k], self.keysrc[k])

    def barrier(self):
        for e in self.ENG:
            for base, keys in self.basekeys.items():
                src = self.keysrc[keys[0]]
                if src == "dma":
                    wk = keys
                else:
                    if src == "pe" and e == "pe":
                        continue
                    wk = keys[-1:]
                    for k in keys[:-1]:
                        self.known[e][k] = self.cnt[k]
                for key in wk:
                    val = self.cnt[key]
                    if val and self.known[e].get(key, 0) < val:
                        self.known[e][key] = val
                        self.q[e].append(("wait", key, val))

    def final_wait(self, eng, bases):
        for base in bases:
            for key in self.basekeys.get(base, []):
                val = self.cnt[key]
                if val and self.known[eng].get(key, 0) < val:
                    self.known[eng][key] = val
                    self.q[eng].append(("wait", key, val))


NEED_TILES = {1: 1, 2: 41, 3: 57, 4: 97}


def build_program(debug=False, stop=4):
    nc = bass.Bass("TRN2", target_bir_lowering=False)
    P = Prog()
    tiles = weight_tiles()
    NT = min(len(tiles), NEED_TILES[stop])

    xT_d = nc.dram_tensor("xT", [128, KC * T], F32, kind="ExternalInput").ap()
    xh_d = nc.dram_tensor("xh", [128, KC * HALO], F32, kind="ExternalInput").ap()
    cols_d = nc.dram_tensor("cols", [128, NCOLS], F32, kind="ExternalInput").ap()
    bias_d = nc.dram_tensor("bias", [128, NBIAS], F32, kind="ExternalInput").ap()
    inv16_d = nc.dram_tensor("inv16", [128, 64], F32, kind="ExternalInput").ap()
    tri_d = nc.dram_tensor("tri", [128, 128], BF16, kind="ExternalInput").ap()
    ident_d = nc.dram_tensor("ident", [128, 128], BF16, kind="ExternalInput").ap()
    lamw_d = nc.dram_tensor("lamw", [128, 512], F32, kind="ExternalInput").ap()
    subg_d = nc.dram_tensor("subg", [128, 256], F32, kind="ExternalInput").ap()
    w_d = nc.dram_tensor("w", [NT, 128, WS], F32, kind="ExternalInput").ap()
    out_d = nc.dram_tensor("out", [128, KC * T], F32, kind="ExternalOutput").ap()
    ccik = [nc.dram_tensor(f"ccik{h}", [256, 1024], BF16, kind="Internal", addr_space="Local").ap() for h in range(NH)]
    ccok = [nc.dram_tensor(f"ccok{h}", [4 * 256, 1024], BF16, kind="Internal", addr_space="Local").ap() for h in range(NH)]
    cciv = [nc.dram_tensor(f"cciv{h}", [1024, 256], BF16, kind="Internal", addr_space="Local").ap() for h in range(NH)]
    ccov = [nc.dram_tensor(f"ccov{h}", [4 * 1024, 256], BF16, kind="Internal", addr_space="Local").ap() for h in range(NH)]
    qT_d = nc.dram_tensor("qT_s", [2048, 1024], BF16, kind="Internal", addr_space="Local").ap()
    dbg_d = None
    if debug:
        dbg_d = nc.dram_tensor("dbg", [4, 128, KC * T], F32, kind="ExternalOutput").ap()

    cur = [16512]

    def alloc(name, n, dt, at=None):
        sz = n * (4 if dt == F32 else 2)
        sz = (sz + 31) // 32 * 32
        if at is None:
            off = cur[0]
            cur[0] += sz
        else:
            off = at
        assert off + sz <= 229344, (name, off, sz)
        return nc.alloc_sbuf_tensor_at(name, [128, n], dt, offset=off).ap(), off + sz

    H, _ = alloc("H", KC * T, F32)
    HH, _ = alloc("HH", KC * HALO, F32)
    A, _ = alloc("A", KC * TE, BF16)
    WSL, _ = alloc("WSL", NSLOT * WS, BF16)
    COLS, _ = alloc("COLS", NCOLS, F32)
    BIAS, _ = alloc("BIAS", NBIAS, F32)
    INV16, _ = alloc("INV16", 64, F32)
    TRI, _ = alloc("TRI", 128, BF16)
    IDENT, _ = alloc("IDENT", 128, BF16)
    ONES, _ = alloc("ONES", 128, F32)
    LAMW, _ = alloc("LAMW", 512, F32)
    SUBG, _ = alloc("SUBG", 256, F32)
    SM, _ = alloc("SM", 32, F32)
    EPSC, _ = alloc("EPSC", 8, F32)
    RSTD, _ = alloc("RSTD", TE, F32)
    SQ = []
    for b in range(2):
        t_, _ = alloc(f"SQ{b}", 528, F32)
        SQ.append(t_)
    S0 = cur[0]

    def layout(base, specs):
        res = {}
        off = base
        for name, n, dt in specs:
            res[name], off = alloc(name, n, dt, at=off)
        return res

    LP = layout(S0, [("SQH", 256, F32), ("E", TE, F32), ("S2", TE, F32), ("S4", TE, F32), ("S8", TE, F32),
                     ("S16", TE, F32), ("T16", 16, F32)])
    LF = layout(S0, [("ACTB", FCB * T, BF16), ("SG0", 512, F32), ("SG1", 512, F32)])
    LA = layout(S0, [("STG0", 1024, BF16), ("STG1", 1024, BF16)] +
                [(f"VB{i}", 8 * 257, BF16) for i in range(4)] +
                [(f"KB{i}", 1024, BF16) for i in range(4)] +
                [("QB0", 512, BF16), ("QB1", 512, BF16)] +
                [(f"P{i}", 512, BF16) for i in range(4)] +
                [("OC", 4 * 257, F32), ("OC2", 4 * 257, F32), ("OT", 256, F32)] +
                [(f"ON{i}", 256, BF16) for i in range(4)] +
                [("JUNK", 256, F32), ("SS", 16, F32)])
    LO = layout(S0, [("OUT0", T, F32), ("OUT1", T, F32)])

    Ht = [[Tok(f"H{k}_{hf}") for hf in range(2)] for k in range(KC)]
    HHt = Tok("HH")
    At = [[Tok(f"A{k}_{hf}") for hf in range(2)] for k in range(KC)]
    Aht = [Tok(f"Ah{k}") for k in range(KC)]
    Wt = [Tok(f"W{s}") for s in range(NSLOT)]
    CONSTt = Tok("const")
    ONESt = Tok("ones")
    RSTDt = Tok("rstd")
    SQt = [Tok("sq0"), Tok("sq1")]
    PSt = [Tok(f"ps{i}") for i in range(8)]
    SMt = Tok("sm")

    def Hk(k, t0=0, t1=T):
        return H[:, k * T + t0:k * T + t1]

    def Ak(k, t0=0, t1=T):
        return A[:, k * TE + HALO + t0:k * TE + HALO + t1]

    def col(c):
        return COLS[:, c:c + 1]

    def hslice(hf):
        return (hf * 512, (hf + 1) * 512)

    import contextlib
    es = contextlib.ExitStack()
    with es:
        PS = [es.enter_context(nc.psum_tensor(f"psb{i}", [128, 512], F32)) for i in range(8)]
        PSa = [p[:] for p in PS]
        bank_rr = [0]

        def nb():
            b = bank_rr[0]
            bank_rr[0] = (b + 1) % 8
            return b

        wstate = {"next_load": 0}

        def w_load(i):
            if i >= NT:
                return
            s = i % NSLOT
            n = tiles[i][2]
            P.op("pool", lambda e, i=i, s=s, n=n: e.dma_start(out=WSL[:, s * WS:s * WS + n], in_=w_d[i, :, 0:n]),
                 writes=[Wt[s]], dma=f"w{s}")

        def w_consume_done(i):
            w_load(i + NSLOT)

        def wslot(i, off, n=128):
            s = i % NSLOT
            return WSL[:, s * WS + off:s * WS + off + n]

        for g in range(4):
            P.op("sp", lambda e, g=g: e.dma_start(out=H[:, g * 4 * T:(g + 1) * 4 * T], in_=xT_d[:, g * 4 * T:(g + 1) * 4 * T]),
                 writes=[Ht[k][hf] for k in range(g * 4, g * 4 + 4) for hf in range(2)], dma=f"x{g}")
        P.op("sp", lambda e: e.dma_start(out=HH[:, :], in_=xh_d[:, :]), writes=[HHt], dma="xh")
        for dst, srcd in ((COLS, cols_d), (BIAS, bias_d), (INV16, inv16_d), (TRI, tri_d), (IDENT, ident_d),
                          (LAMW, lamw_d), (SUBG, subg_d)):
            P.op("sp", lambda e, dst=dst, srcd=srcd: e.dma_start(out=dst[:, :], in_=srcd[:, :]), writes=[CONSTt], dma="cst")
        CONSTt.w = P.last_event("cst")
        P.op("dve", lambda e: e.memset(ONES[:, :], 1.0), writes=[ONESt])
        P.op("dve", lambda e: e.memset(EPSC[:, :], EPS), writes=[ONESt])
        for i in range(NSLOT):
            w_load(i)
        wi = [0]

        def norm_stats(with_halo):
            banks = [nb(), nb()]
            n = 0
            for k in range(KC):
                for hf in range(2):
                    b = n % 2
                    n += 1
                    t0, t1 = hslice(hf)
                    P.op("act", lambda e, b=b, k=k, t0=t0, t1=t1: e.activation(out=SQ[b][:, 0:512], in_=Hk(k, t0, t1), func=AF.Square),
                         reads=[Ht[k][hf]], writes=[SQt[b]])
                    P.op("pe", lambda e, b=b, bk=banks[hf], k=k: e.matmul(PSa[bk][:, :], lhsT=ONES[:, :], rhs=SQ[b][:, 0:512],
                                                                           start=(k == 0), stop=(k == KC - 1)),
                         reads=[SQt[b], ONESt], writes=[PSt[banks[hf]]], signal=True)
            hb = None
            if with_halo:
                hb = nb()
                SQH = LP["SQH"]
                sqht = Tok("sqh")
                P.op("act", lambda e: e.activation(out=SQH[:, :], in_=HH[:, :], func=AF.Square), reads=[HHt], writes=[sqht])
                for k in range(KC):
                    P.op("pe", lambda e, k=k: e.matmul(PSa[hb][:, 0:HALO], lhsT=ONES[:, :], rhs=SQH[:, k * HALO:(k + 1) * HALO],
                                                        start=(k == 0), stop=(k == KC - 1)),
                         reads=[sqht, ONESt], writes=[PSt[hb]], signal=(k == KC - 1))
            for hf in range(2):
                t0, t1 = hslice(hf)
                P.op("act", lambda e, bk=banks[hf], t0=t0, t1=t1: e.activation(out=RSTD[:, HALO + t0:HALO + t1], in_=PSa[bk][:, :],
                                                                              func=AF.Sqrt, bias=EPSC[:, 0:1], scale=1.0 / D),
                     reads=[PSt[banks[hf]], ONESt], writes=[RSTDt])
            if with_halo:
                P.op("act", lambda e: e.activation(out=RSTD[:, 0:HALO], in_=PSa[hb][:, 0:HALO], func=AF.Sqrt, bias=EPSC[:, 0:1], scale=1.0 / D),
                     reads=[PSt[hb], ONESt], writes=[RSTDt])
            lo = 0 if with_halo else HALO
            P.op("dve", lambda e, lo=lo: e.reciprocal(out=RSTD[:, lo:TE], in_=RSTD[:, lo:TE]),
                 reads=[RSTDt], writes=[RSTDt])

        def norm_apply(cbase):
            for k in range(KC):
                P.op("dve", lambda e, k=k: e.scalar_tensor_tensor(out=Ak(k), in0=Hk(k), scalar=col(cbase + k), in1=RSTD[:, HALO:TE],
                                                                  op0=ALU.mult, op1=ALU.mult),
                     reads=[Ht[k][0], Ht[k][1], RSTDt, CONSTt], writes=[At[k][0], At[k][1]])

        def dump(i):
            if debug:
                P.barrier()
                P.op("sp", lambda e, i=i: e.dma_start(out=dbg_d[i], in_=H[:, :]),
                     reads=[Ht[k][hf] for k in range(KC) for hf in range(2)], dma="dbg")

        def pool_mixer():
            norm_stats(True)
            E, S2, S4, S8, S16, T16 = LP["E"], LP["S2"], LP["S4"], LP["S8"], LP["S16"], LP["T16"]
            Et, St = Tok("E"), [Tok("S2"), Tok("S4"), Tok("S8"), Tok("S16")]
            T16t = Tok("T16")
            Sb = [S2, S4, S8, S16]
            for k in range(KC):
                g = k // 4
                w = 2 << g
                P.op("dve", lambda e, k=k: e.scalar_tensor_tensor(out=E[:, 0:HALO], in0=HH[:, k * HALO:(k + 1) * HALO], scalar=col(C_NMIX0 + k),
                                                                  in1=RSTD[:, 0:HALO], op0=ALU.mult, op1=ALU.mult),
                     reads=[HHt, RSTDt, CONSTt], writes=[Et])
                P.op("dve", lambda e, k=k: e.scalar_tensor_tensor(out=E[:, HALO:TE], in0=Hk(k), scalar=col(C_NMIX0 + k),
                                                                  in1=RSTD[:, HALO:TE], op0=ALU.mult, op1=ALU.mult),
                     reads=[Ht[k][0], Ht[k][1], RSTDt, CONSTt], writes=[Et])
                prev, prevt = E, Et
                lo = 0
                for step in range(g + 1):
                    sh = 1 << step
                    lo2 = lo + sh
                    dst, dstt = Sb[step], St[step]
                    P.op("dve", lambda e, dst=dst, prev=prev, lo2=lo2, sh=sh: e.tensor_tensor(out=dst[:, lo2:TE], in0=prev[:, lo2:TE],
                                                                                           in1=prev[:, lo2 - sh:TE - sh], op=ALU.add),
                         reads=[prevt], writes=[dstt])
                    prev, prevt, lo = dst, dstt, lo2
                S, Stok = prev, prevt
                P.op("dve", lambda e, k=k, S=S, w=w: e.scalar_tensor_tensor(out=Ak(k), in0=S[:, HALO:TE], scalar=1.0 / w, in1=E[:, HALO:TE],
                                                                            op0=ALU.mult, op1=ALU.subtract),
                     reads=[Stok, Et], writes=[At[k][0], At[k][1]])
                P.op("dve", lambda e, S=S, g=g: e.tensor_tensor(out=T16[:, :], in0=S[:, HALO:2 * HALO], in1=INV16[:, g * 16:(g + 1) * 16], op=ALU.mult),
                     reads=[Stok, CONSTt], writes=[T16t])
                P.op("dve", lambda e, k=k: e.tensor_tensor(out=Ak(k, 0, HALO), in0=T16[:, :], in1=E[:, HALO:2 * HALO], op=ALU.subtract),
                     reads=[T16t, Et, At[k][0]], writes=[At[k][0]])
            ti = wi[0]
            for g in range(4):
                for oc in range(4):
                    c = g * 4 + oc
                    for hf in range(2):
                        t0, t1 = hslice(hf)
                        bk = nb()
                        for kk in range(4):
                            off = ((g * 4 + kk) * 4 + oc) * 128
                            last = (g == 3 and oc == 3 and hf == 1 and kk == 3)
                            P.op("pe", lambda e, bk=bk, off=off, kk=kk, g=g, t0=t0, t1=t1: e.matmul(
                                PSa[bk][:, :], lhsT=wslot(ti, off), rhs=Ak(g * 4 + kk, t0, t1), start=(kk == 0), stop=(kk == 3)),
                                 reads=[Wt[ti % NSLOT], At[g * 4 + kk][hf]], writes=[PSt[bk]], signal=(kk == 3))
                        P.op("dve", lambda e, bk=bk, c=c, t0=t0, t1=t1: e.scalar_tensor_tensor(
                            out=Hk(c, t0, t1), in0=PSa[bk][:, :], scalar=col(C_PSCALE + c), in1=Hk(c, t0, t1), op0=ALU.mult, op1=ALU.add),
                             reads=[PSt[bk], Ht[c][hf], CONSTt], writes=[Ht[c][hf]])
            w_consume_done(ti)
            wi[0] += 1

        def ffn(cnorm):
            norm_stats(False)
            norm_apply(cnorm)
            ACTB = LF["ACTB"]
            SG = [LF["SG0"], LF["SG1"]]
            SGt = [Tok("sg0"), Tok("sg1")]
            ACTt = [[Tok(f"act{j}_{hf}") for hf in range(2)] for j in range(FCB)]
            sgn = [0]
            for blk in range(NFB):
                jl = 0
                while jl < FCB:
                    nj = min(2, FCB - jl)
                    ti = wi[0]
                    for jj in range(nj):
                        j = jl + jj
                        banks = [nb() for _ in range(4)]
                        for k in range(KC):
                            for gu in range(2):
                                off = ((jj * 2 + gu) * 16 + k) * 128
                                for hf in range(2):
                                    t0, t1 = hslice(hf)
                                    bk = banks[gu * 2 + hf]
                                    P.op("pe", lambda e, bk=bk, off=off, k=k, t0=t0, t1=t1, ti=ti: e.matmul(
                                        PSa[bk][:, :], lhsT=wslot(ti, off), rhs=Ak(k, t0, t1), start=(k == 0), stop=(k == KC - 1)),
                                         reads=[Wt[ti % NSLOT], At[k][hf]], writes=[PSt[bk]], signal=(k == KC - 1))
                        for hf in range(2):
                            t0, t1 = hslice(hf)
                            b = sgn[0] % 2
                            sgn[0] += 1
                            P.op("act", lambda e, b=b, bk=banks[hf]: e.activation(out=SG[b][:, :], in_=PSa[bk][:, :], func=AF.Silu),
                                 reads=[PSt[banks[hf]]], writes=[SGt[b]])
                            P.op("dve", lambda e, b=b, bk=banks[2 + hf], j=j, t0=t0, t1=t1: e.tensor_tensor(
                                out=ACTB[:, j * T + t0:j * T + t1], in0=SG[b][:, :], in1=PSa[bk][:, :], op=ALU.mult),
                                 reads=[SGt[b], PSt[banks[2 + hf]]], writes=[ACTt[j][hf]])
                    w_consume_done(ti)
                    wi[0] += 1
                    jl += nj
                for dg in range(4):
                    ti = wi[0]
                    for dd in range(4):
                        dc = dg * 4 + dd
                        for hf in range(2):
                            t0, t1 = hslice(hf)
                            bk = nb()
                            for fc in range(FCB):
                                off = (fc * 4 + dd) * 128
                                P.op("pe", lambda e, bk=bk, off=off, fc=fc, t0=t0, t1=t1, ti=ti: e.matmul(
                                    PSa[bk][:, :], lhsT=wslot(ti, off), rhs=ACTB[:, fc * T + t0:fc * T + t1], start=(fc == 0), stop=(fc == FCB - 1)),
                                     reads=[Wt[ti % NSLOT], ACTt[fc][hf]], writes=[PSt[bk]], signal=(fc == FCB - 1))
                            P.op("dve", lambda e, bk=bk, dc=dc, t0=t0, t1=t1: e.tensor_tensor(
                                out=Hk(dc, t0, t1), in0=Hk(dc, t0, t1), in1=PSa[bk][:, :], op=ALU.add),
                                 reads=[PSt[bk], Ht[dc][hf]], writes=[Ht[dc][hf]])
                    w_consume_done(ti)
                    wi[0] += 1

        def attention():
            norm_stats(False)
            norm_apply(C_NMIX1)
            STG = [LA["STG0"], LA["STG1"]]
            STGt = [Tok("stg0"), Tok("stg1")]
            VB = [LA[f"VB{i}"] for i in range(4)]
            VBt = [Tok(f"vb{i}") for i in range(4)]
            KB = [LA[f"KB{i}"] for i in range(4)]
            KBt = [Tok(f"kb{i}") for i in range(4)]
            QB = [LA["QB0"], LA["QB1"]]
            QBt = [Tok("qb0"), Tok("qb1")]
            PT = [LA[f"P{i}"] for i in range(4)]
            PTt = [Tok(f"p{i}") for i in range(4)]
            OC, OT, JUNK, SS = LA["OC"], LA["OT"], LA["JUNK"], LA["SS"]
            OCt = [Tok(f"oc{i}") for i in range(4)]
            OTt, JUNKt, SSt = Tok("ot"), Tok("junk"), Tok("ss")
            Kd = [Tok(f"kd{c}") for c in range(KC)]
            Vd = [[Tok(f"vd{tt}_{h}") for h in range(NH)] for tt in range(8)]
            Qd = [Tok(f"qd{c}") for c in range(KC)]
            CCKt = [Tok(f"cck{h}") for h in range(NH)]
            CCVt = [Tok(f"ccv{h}") for h in range(NH)]
            stn = [0]

            LAM = SM[:, 0:1]
            P.op("dve", lambda e: e.scalar_tensor_tensor(out=JUNK[:, 0:128], in0=LAMW[:, 0:128], scalar=1.0, in1=LAMW[:, 128:256],
                                                         op0=ALU.mult, op1=ALU.mult, accum_out=SM[:, 1:2]),
                 reads=[CONSTt], writes=[JUNKt, SMt])
            P.op("dve", lambda e: e.scalar_tensor_tensor(out=JUNK[:, 128:256], in0=LAMW[:, 256:384], scalar=1.0, in1=LAMW[:, 384:512],
                                                         op0=ALU.mult, op1=ALU.mult, accum_out=SM[:, 2:3]),
                 reads=[CONSTt, SMt], writes=[JUNKt, SMt])
            P.op("act", lambda e: e.activation(out=SM[:, 3:5], in_=SM[:, 1:3], func=AF.Exp), reads=[SMt], writes=[SMt])
            P.op("dve", lambda e: e.tensor_tensor(out=SM[:, 5:6], in0=SM[:, 3:4], in1=SM[:, 4:5], op=ALU.subtract), reads=[SMt], writes=[SMt])
            P.op("dve", lambda e: e.tensor_scalar(out=LAM, in0=SM[:, 5:6], scalar1=float(LAMBDA_INIT), scalar2=None, op0=ALU.add),
                 reads=[SMt], writes=[SMt])
            P.op("dve", lambda e: e.tensor_scalar(out=SUBG[:, :], in0=SUBG[:, :], scalar1=float(1.0 - LAMBDA_INIT), scalar2=None, op0=ALU.mult),
                 reads=[CONSTt], writes=[CONSTt])
            for i in range(4):
                P.op("dve", lambda e, i=i: e.memset(VB[i].rearrange("p (k e) -> p k e", e=257)[:, :, 256:257], 1.0), writes=[VBt[i]])

            def evac_copy(n, out_ap, in_ap, reads, writes):
                if n % 2 == 0:
                    P.op("act", lambda e: e.activation(out=out_ap, in_=in_ap, func=AF.Copy), reads=reads, writes=writes)
                else:
                    P.op("dve", lambda e: e.tensor_copy(out=out_ap, in_=in_ap), reads=reads, writes=writes)

            def proj_fm(dst_rows, dtoks, after_tile=None):
                for t in range(4):
                    ti = wi[0]
                    for ocl in range(4):
                        oc = t * 4 + ocl
                        b = stn[0] % 2
                        stn[0] += 1
                        for hf in range(2):
                            t0, t1 = hslice(hf)
                            bk = nb()
                            for k in range(KC):
                                off = (ocl * 16 + k) * 128
                                P.op("pe", lambda e, bk=bk, off=off, k=k, t0=t0, t1=t1, ti=ti: e.matmul(
                                    PSa[bk][:, :], lhsT=wslot(ti, off), rhs=Ak(k, t0, t1), start=(k == 0), stop=(k == KC - 1)),
                                     reads=[Wt[ti % NSLOT], At[k][hf]], writes=[PSt[bk]], signal=(k == KC - 1))
                            evac_copy(hf, STG[b][:, t0:t1], PSa[bk][:, :], [PSt[bk]], [STGt[b]])
                        P.op("sp", lambda e, b=b, oc=oc: e.dma_start(out=dst_rows(oc), in_=STG[b][:, :]),
                             reads=[STGt[b]], writes=[dtoks[oc]], dma=f"st{b}")
                    w_consume_done(ti)
                    wi[0] += 1
                    if after_tile is not None:
                        after_tile(t)

            def gather_k(t):
                for h in (2 * t, 2 * t + 1):
                    P.op("pool", lambda e, h=h: e.collective_compute("AllGather", ALU.bypass, replica_groups=GROUPS, ins=[ccik[h]], outs=[ccok[h]]),
                         reads=[Kd[2 * h], Kd[2 * h + 1]], writes=[CCKt[h]], dma=f"cck{h}", inc=1)

            def gather_v(cg):
                for h in (2 * cg, 2 * cg + 1):
                    P.op("pool", lambda e, h=h: e.collective_compute("AllGather", ALU.bypass, replica_groups=GROUPS, ins=[cciv[h]], outs=[ccov[h]]),
                         reads=[Vd[tt][h] for tt in range(8)], writes=[CCVt[h]], dma=f"ccv{h}", inc=1)

            proj_fm(lambda oc: ccik[oc // 2][(oc % 2) * 128:(oc % 2 + 1) * 128, :], Kd, after_tile=gather_k)
            for cg in range(4):
                ti = wi[0]
                for tt in range(8):
                    b = stn[0] % 2
                    stn[0] += 1
                    bk = nb()
                    for k in range(KC):
                        P.op("pe", lambda e, bk=bk, k=k, tt=tt, ti=ti: e.matmul(
                            PSa[bk][:, :], lhsT=Ak(k, tt * 128, (tt + 1) * 128), rhs=wslot(ti, k * 512, 512), start=(k == 0), stop=(k == KC - 1)),
                             reads=[Wt[ti % NSLOT], At[k][tt // 4]], writes=[PSt[bk]], signal=(k == KC - 1))
                    evac_copy(tt, STG[b][:, 0:512], PSa[bk][:, :], [PSt[bk]], [STGt[b]])
                    for hh in range(2):
                        h = 2 * cg + hh
                        P.op("sp", lambda e, b=b, tt=tt, h=h, hh=hh: e.dma_start(out=cciv[h][tt * 128:(tt + 1) * 128, :], in_=STG[b][:, hh * 256:(hh + 1) * 256]),
                             reads=[STGt[b]], writes=[Vd[tt][h]], dma=f"st{b}")
                w_consume_done(ti)
                wi[0] += 1
                gather_v(cg)
            proj_fm(lambda oc: qT_d[oc * 128:(oc + 1) * 128, :], Qd)

            ACC = [0, 1, 2, 3]
            SCB = [4, 5, 6, 7]
            scn = [0]
            ptn = [0]
            qn = [0]
            onn = [0]
            LOOK = 2
            pending_pe = []
            pending_now = []
            OC2 = LA["OC2"]
            OC2t = [Tok(f"oc2_{i}") for i in range(4)]
            ONB = [LA[f"ON{i}"] for i in range(4)]
            ONBt = [Tok(f"on{i}") for i in range(4)]

            for h in range(NH):
                nsub = 512 // QBH[h]
                for blk in range(4):
                    vsrc = cciv[h] if blk == 3 else ccov[h][blk * 1024:(blk + 1) * 1024, :]
                    vsrc = vsrc.rearrange("(k p) e -> p k e", p=128)
                    rd = [CCVt[h]] if blk < 3 else [Vd[tt][h] for tt in range(8)]
                    P.op("sp", lambda e, blk=blk, vsrc=vsrc: e.dma_start(out=VB[blk].rearrange("p (k e) -> p k e", e=257)[:, :, 0:256], in_=vsrc),
                         reads=rd, writes=[VBt[blk]], dma=f"v{blk}")
                for qb in range(2):
                    for c in range(2):
                        ch = 2 * h + c
                        qi = qn[0] % 2
                        qn[0] += 1
                        P.op("sp", lambda e, qi=qi, ch=ch, qb=qb: e.dma_start(out=QB[qi][:, :], in_=qT_d[ch * 128:(ch + 1) * 128, qb * 512:(qb + 1) * 512]),
                             reads=[Qd[ch]], writes=[QBt[qi]], dma=f"q{qi}")
                        for blk in range(4):
                            rd = [CCKt[h]] if blk < 3 else [Kd[ch]]
                            ksrc = ccik[h][c * 128:(c + 1) * 128, :] if blk == 3 else ccok[h][blk * 256 + c * 128:blk * 256 + (c + 1) * 128, :]
                            P.op("sp", lambda e, blk=blk, ksrc=ksrc: e.dma_start(out=KB[blk][:, :], in_=ksrc),
                                 reads=rd, writes=[KBt[blk]], dma=f"k{blk}")
                        work = []
                        for blk in range(3):
                            for kt in range(8):
                                work.append((blk, kt, 0, False))
                        for kt in range(8):
                            if kt < 4 * qb:
                                work.append((3, kt, 0, False))
                            elif kt <= 4 * qb + 3:
                                work.append((3, kt, kt - 4 * qb, True))
                        nW = len(work)
                        pis = [None] * nW

                        def emit_score(i):
                            blk, kt, m, diag = work[i]
                            c0 = 128 * m
                            sb = SCB[scn[0] % 4]
                            scn[0] += 1
                            pi = ptn[0] % 4
                            ptn[0] += 1
                            pis[i] = pi
                            P.op("pe", lambda e, sb=sb, blk=blk, kt=kt, qi=qi, c0=c0: e.matmul(
                                PSa[sb][:, c0:512], lhsT=KB[blk][:, kt * 128:(kt + 1) * 128], rhs=QB[qi][:, c0:512], start=True, stop=True),
                                 reads=[KBt[blk], QBt[qi]], writes=[PSt[sb]], signal=True)
                            for sub in range(nsub):
                                a0 = max(c0, sub * QBH[h])
                                a1 = (sub + 1) * QBH[h]
                                if a0 >= a1:
                                    continue
                                bc = BIDX[(h, blk, qb, kt, sub)]
                                P.op("act", lambda e, pi=pi, sb=sb, a0=a0, a1=a1, bc=bc: e.activation(
                                    out=PT[pi][:, a0:a1], in_=PSa[sb][:, a0:a1], func=AF.Exp, bias=BIAS[:, bc:bc + 1], scale=float(SCALE)),
                                     reads=[PSt[sb], CONSTt], writes=[PTt[pi]])
                            if diag:
                                P.op("dve", lambda e, pi=pi, c0=c0: e.tensor_tensor(out=PT[pi][:, c0:c0 + 128], in0=PT[pi][:, c0:c0 + 128], in1=TRI[:, :], op=ALU.mult),
                                     reads=[PTt[pi], CONSTt], writes=[PTt[pi]])

                        def emit_av(i):
                            blk, kt, m, diag = work[i]
                            pi = pis[i]
                            for qs in range(m, 4):
                                first = (i == 0)
                                lastk = (blk == 3 and kt == 4 * qb + qs)
                                P.op("pe", lambda e, qs=qs, pi=pi, blk=blk, kt=kt, first=first, lastk=lastk: e.matmul(
                                    PSa[ACC[qs]][:, 0:257], lhsT=PT[pi][:, qs * 128:(qs + 1) * 128], rhs=VB[blk][:, kt * 257:(kt + 1) * 257],
                                    start=first, stop=lastk),
                                     reads=[PTt[pi], VBt[blk]], writes=[PSt[ACC[qs]]], signal=(lastk or qs == 3))

                        for idx in range(nW + LOOK):
                            if idx < nW:
                                emit_score(idx)
                            if idx == 8 and pending_pe:
                                for fn_ in pending_pe:
                                    fn_()
                                pending_pe.clear()
                            if idx >= LOOK:
                                emit_av(idx - LOOK)
                        dstb = OC if c == 0 else OC2
                        dstt = OCt if c == 0 else OC2t
                        for qs in range(4):
                            P.op("dve", lambda e, qs=qs, dstb=dstb: e.tensor_copy(out=dstb[:, qs * 257:(qs + 1) * 257], in_=PSa[ACC[qs]][:, 0:257]),
                                 reads=[PSt[ACC[qs]]], writes=[dstt[qs]])
                        if c == 1:
                            OC3 = OC.rearrange("p (q e) -> p q e", e=257)
                            OC23 = OC2.rearrange("p (q e) -> p q e", e=257)
                            P.op("dve", lambda e: e.reciprocal(out=SS[:, 0:4], in_=OC3[:, :, 256]), reads=OCt, writes=[SSt])
                            P.op("dve", lambda e: e.reciprocal(out=SS[:, 4:8], in_=OC23[:, :, 256]), reads=OC2t + [SSt], writes=[SSt])
                            P.op("dve", lambda e: e.tensor_scalar(out=SS[:, 4:8], in0=SS[:, 4:8], scalar1=LAM, scalar2=None, op0=ALU.mult),
                                 reads=[SSt, SMt], writes=[SSt])
                            for qs in range(4):
                                ocq = OC[:, qs * 257:qs * 257 + 256]
                                oc2q = OC2[:, qs * 257:qs * 257 + 256]
                                P.op("dve", lambda e, oc2q=oc2q, qs=qs: e.tensor_scalar(out=OT[:, :], in0=oc2q, scalar1=SS[:, 4 + qs:5 + qs], scalar2=None, op0=ALU.mult),
                                     reads=[OC2t[qs], SSt], writes=[OTt])
                                P.op("dve", lambda e, ocq=ocq, qs=qs: e.scalar_tensor_tensor(out=ocq, in0=ocq, scalar=SS[:, qs:qs + 1], in1=OT[:, :],
                                                                                            op0=ALU.mult, op1=ALU.subtract),
                                     reads=[OCt[qs], OTt, SSt], writes=[OCt[qs]])
                                P.op("dve", lambda e, ocq=ocq, qs=qs: e.scalar_tensor_tensor(out=JUNK[:, :], in0=ocq, scalar=1.0, in1=ocq, op0=ALU.mult, op1=ALU.mult,
                                                                                            accum_out=SS[:, 8 + qs:9 + qs]),
                                     reads=[OCt[qs], SSt], writes=[JUNKt, SSt])
                            P.op("act", lambda e: e.activation(out=SS[:, 12:16], in_=SS[:, 8:12], func=AF.Ln, bias=EPSC[:, 0:1], scale=1.0 / 256),
                                 reads=[SSt, ONESt], writes=[SSt])
                            P.op("act", lambda e: e.activation(out=SS[:, 12:16], in_=SS[:, 12:16], func=AF.Exp, scale=-0.5),
                                 reads=[SSt], writes=[SSt])
                            for qs in range(4):
                                tq = (qb * 4 + qs) * 128
                                ocq = OC[:, qs * 257:qs * 257 + 256]
                                oi = onn[0] % 4
                                onn[0] += 1
                                P.op("dve", lambda e, ocq=ocq, qs=qs, oi=oi: e.scalar_tensor_tensor(out=ONB[oi][:, :], in0=ocq, scalar=SS[:, 12 + qs:13 + qs], in1=SUBG[:, :],
                                                                                                   op0=ALU.mult, op1=ALU.mult),
                                     reads=[OCt[qs], SSt, CONSTt], writes=[ONBt[oi]])

                                def tr_fn(oi=oi, tq=tq, h=h):
                                    for j in range(2):
                                        sb = SCB[scn[0] % 4]
                                        scn[0] += 1
                                        pv = PSa[sb].bitcast(BF16)
                                        P.op("pe", lambda e, pv=pv, j=j, oi=oi: e.transpose(out=pv[:, 0:128], in_=ONB[oi][:, j * 128:(j + 1) * 128], identity=IDENT[:, :]),
                                             reads=[ONBt[oi], CONSTt], writes=[PSt[sb]], signal=True)
                                        kk = 2 * h + j
                                        P.op("dve", lambda e, pv=pv, kk=kk, tq=tq: e.tensor_copy(out=Ak(kk, tq, tq + 128), in_=pv[:, 0:128]),
                                             reads=[PSt[sb]], writes=[At[kk][tq // 512]])
                                if qs % 2 == 1:
                                    pending_now.append(tr_fn)
                                else:
                                    pending_now.append(tr_fn)
                            pending_pe.extend(pending_now)
                            pending_now.clear()
            for fn_ in pending_pe:
                fn_()
            pending_pe.clear()

            for t in range(4):
                ti = wi[0]
                for ocl in range(4):
                    oc = t * 4 + ocl
                    for hf in range(2):
                        t0, t1 = hslice(hf)
                        bk = nb()
                        for k in range(KC):
                            off = (ocl * 16 + k) * 128
                            P.op("pe", lambda e, bk=bk, off=off, k=k, t0=t0, t1=t1, ti=ti: e.matmul(
                                PSa[bk][:, :], lhsT=wslot(ti, off), rhs=Ak(k, t0, t1), start=(k == 0), stop=(k == KC - 1)),
                                 reads=[Wt[ti % NSLOT], At[k][hf]], writes=[PSt[bk]], signal=(k == KC - 1))
                        P.op("dve", lambda e, bk=bk, oc=oc, t0=t0, t1=t1: e.tensor_tensor(out=Hk(oc, t0, t1), in0=Hk(oc, t0, t1), in1=PSa[bk][:, :], op=ALU.add),
                             reads=[PSt[bk], Ht[oc][hf]], writes=[Ht[oc][hf]])
                w_consume_done(ti)
                wi[0] += 1

        pool_mixer()
        dump(0)
        P.barrier()
        if stop >= 2:
            ffn(C_NFFN0)
            dump(1)
            P.barrier()
        if stop >= 3:
            attention()
            dump(2)
            P.barrier()
        if stop >= 4:
            ffn(C_NFFN1)
            dump(3)
            P.barrier()
        norm_stats(False)
        OUT = [LO["OUT0"], LO["OUT1"]]
        OUTt = [Tok("out0"), Tok("out1")]
        for k in range(KC):
            b = k % 2
            P.op("dve", lambda e, k=k, b=b: e.scalar_tensor_tensor(out=OUT[b][:, :], in0=Hk(k), scalar=col(C_FINAL + k), in1=RSTD[:, HALO:TE],
                                                                  op0=ALU.mult, op1=ALU.mult),
                 reads=[Ht[k][0], Ht[k][1], RSTDt, CONSTt], writes=[OUTt[b]])
            P.op("sp", lambda e, k=k, b=b: e.dma_start(out=out_d[:, k * T:(k + 1) * T], in_=OUT[b][:, :]), reads=[OUTt[b]], dma=f"o{b}")
        assert wi[0] == NT, (wi[0], NT)
        P.final_wait("sp", ["o0", "o1", "dbg"])

        sems = {}
        for key in P.semkeys:
            sems[key] = es.enter_context(nc.semaphore(f"s_{key}"))
        block = es.enter_context(nc.Block())

        def run(e, name):
            for item in P.q[name]:
                kind = item[0]
                if kind == "wait":
                    e.wait_ge(sems[item[1]], item[2])
                elif kind == "dma":
                    item[1](e).then_inc(sems[item[2]], item[3])
                elif kind == "sig":
                    item[1](e).then_inc(sems[item[2]], 1)
                else:
                    item[1](e)

        @block.tensor
        def _(e):
            run(e, "pe")

        @block.scalar
        def _(e):
            run(e, "act")

        @block.vector
        def _(e):
            run(e, "dve")

        @block.gpsimd
        def _(e):
            run(e, "pool")

        @block.sync
        def _(e):
            run(e, "sp")
    return nc


def _fm(a, n):
    return np.ascontiguousarray(a.T.reshape(KC, 128, n).transpose(1, 0, 2).reshape(128, KC * n))


def _colvec(v):
    return v.reshape(KC, 128).T


DEBUG = False
STOP = 4
_LAST = {}


def kernel(x, norm_mix, norm_ffn, pool_w, pool_scale, w_qkv, lambda_q1, lambda_k1, lambda_q2, lambda_k2,
           subln_g, w_o, w_gate, w_up, w_down, final_norm):
    inp = dict(x=x, norm_mix=norm_mix, norm_ffn=norm_ffn, pool_w=pool_w, pool_scale=pool_scale, w_qkv=w_qkv,
               w_o=w_o, w_gate=w_gate, w_up=w_up, w_down=w_down)
    inp = {k: np.asarray(v, dtype=np.float32) for k, v in inp.items()}
    x = inp["x"]
    wpack = pack_weights(inp)
    cols = np.zeros((128, NCOLS), np.float32)
    cols[:, C_NMIX0:C_NMIX0 + 16] = _colvec(inp["norm_mix"][0])
    cols[:, C_NFFN0:C_NFFN0 + 16] = _colvec(inp["norm_ffn"][0])
    cols[:, C_PSCALE:C_PSCALE + 16] = _colvec(inp["pool_scale"][0])
    cols[:, C_NMIX1:C_NMIX1 + 16] = _colvec(inp["norm_mix"][1])
    cols[:, C_NFFN1:C_NFFN1 + 16] = _colvec(inp["norm_ffn"][1])
    cols[:, C_FINAL:C_FINAL + 16] = _colvec(np.asarray(final_norm, np.float32))
    lamw = np.concatenate([np.asarray(v, np.float32).reshape(1, 128) for v in (lambda_q1, lambda_k1, lambda_q2, lambda_k2)], axis=1)
    lamw = np.ascontiguousarray(np.broadcast_to(lamw, (128, 512)))
    subg = np.ascontiguousarray(np.broadcast_to(np.asarray(subln_g, np.float32).reshape(1, 256), (128, 256)))
    tri = (np.arange(128)[None, :] >= np.arange(128)[:, None]).astype(ml_dtypes.bfloat16)
    ident = np.eye(128).astype(ml_dtypes.bfloat16)

    in_maps = []
    for c in range(NCORES):
        b, r = c // 4, c % 4
        xs = x[b, r * T:(r + 1) * T, :]
        if r == 0:
            xh = np.zeros((HALO, D), np.float32)
        else:
            xh = x[b, r * T - HALO:r * T, :]
        inv16 = np.zeros((128, 64), np.float32)
        for g in range(4):
            w = 2 << g
            tpos = r * T + np.arange(16)
            inv16[:, g * 16:(g + 1) * 16] = (1.0 / np.minimum(tpos + 1, w)).astype(np.float32)[None, :]
        in_maps.append({
            "xT": _fm(xs, T), "xh": _fm(xh, HALO), "cols": cols, "bias": make_bias(r), "inv16": inv16,
            "tri": tri, "ident": ident, "lamw": lamw, "subg": subg, "w": wpack,
        })
    nc = build_program(debug=DEBUG, stop=STOP)
    for m in in_maps:
        m["w"] = wpack[:NEED_TILES[STOP]]
    res = run_bass_kernel_spmd(nc, in_maps, core_ids=list(range(NCORES)))
    out = np.zeros((2, 4096, D), np.float32)
    for c in range(NCORES):
        b, r = c // 4, c % 4
        o = np.asarray(res.results[c]["out"]).reshape(128, KC, T).transpose(2, 1, 0).reshape(T, D)
        out[b, r * T:(r + 1) * T, :] = o
    if DEBUG:
        _LAST["dbg"] = [np.asarray(res.results[c]["dbg"]) for c in range(NCORES)]
    return out
```

```python
import math
import numpy as np
import ml_dtypes
import concourse.bass as bass
import concourse.mybir as mybir
from concourse.bass_utils import run_bass_kernel_spmd

F32 = mybir.dt.float32
BF16 = mybir.dt.bfloat16
AF = mybir.ActivationFunctionType
ALU = mybir.AluOpType

NCORES = 8
D = 2048
T = 1024
HALO = 16
TE = T + HALO
KC = 16
FF = 5632
FC = 44
NFB = 4
FCB = FC // NFB
NH = 8
WS = 8192
NSLOT = 3
EPS = 1e-6
LAMBDA_INIT = 0.8 - 0.6 * math.exp(-0.3 * 1)
SCALE = 128 ** -0.5
SLOPES = [2.0 ** (-8.0 * (i + 1) / NH) for i in range(NH)]
QBH = [128, 256, 512, 512, 512, 512, 512, 512]
NEG = -30000.0
GROUPS = [[0, 1, 2, 3], [4, 5, 6, 7]]

C_NMIX0, C_NFFN0, C_PSCALE, C_NMIX1, C_NFFN1, C_FINAL = [i * 16 for i in range(6)]
NCOLS = 96


def weight_tiles():
    tiles = [("pool", (), 8192)]

    def ffn(l):
        out = []
        for blk in range(NFB):
            j0 = blk * FCB
            jl = 0
            while jl < FCB:
                nj = min(2, FCB - jl)
                out.append(("gu", (l, j0 + jl, nj), nj * 4096))
                jl += nj
            for dg in range(4):
                out.append(("down", (l, blk, dg), FCB * 512))
        return out

    tiles += ffn(0)
    for t in range(4):
        tiles.append(("k", (t,), 8192))
    for t in range(4):
        tiles.append(("v", (t,), 8192))
    for t in range(4):
        tiles.append(("q", (t,), 8192))
    for t in range(4):
        tiles.append(("o", (t,), 8192))
    tiles += ffn(1)
    return tiles


def pack_weights(inp):
    tiles = weight_tiles()
    w = np.zeros((len(tiles), 128, WS), np.float32)
    wqkv = inp["w_qkv"][0]
    for i, (kind, a, n) in enumerate(tiles):
        if kind == "pool":
            pw = inp["pool_w"][0].reshape(4, 4, 128, 4, 128).transpose(2, 0, 1, 3, 4)
            w[i, :, :n] = pw.reshape(128, n)
        elif kind == "gu":
            l, j0, nj = a
            for jj in range(nj):
                for gu, W in enumerate((inp["w_gate"][l], inp["w_up"][l])):
                    sub = W[:, (j0 + jj) * 128:(j0 + jj + 1) * 128].reshape(16, 128, 128).transpose(1, 0, 2)
                    o = (jj * 2 + gu) * 2048
                    w[i, :, o:o + 2048] = sub.reshape(128, 2048)
        elif kind == "down":
            l, blk, dg = a
            sub = inp["w_down"][l][blk * FCB * 128:(blk + 1) * FCB * 128, dg * 512:(dg + 1) * 512]
            sub = sub.reshape(FCB, 128, 4, 128).transpose(1, 0, 2, 3)
            w[i, :, :n] = sub.reshape(128, n)
        elif kind in ("k", "q", "o"):
            t = a[0]
            if kind == "q":
                W = wqkv[:, 0:2048]
            elif kind == "k":
                W = wqkv[:, 2048:4096]
            else:
                W = inp["w_o"][0]
            sub = W[:, t * 512:(t + 1) * 512].reshape(16, 128, 4, 128).transpose(1, 2, 0, 3)
            w[i, :, :n] = sub.reshape(128, n)
        elif kind == "v":
            t = a[0]
            sub = wqkv[:, 4096 + t * 512:4096 + (t + 1) * 512].reshape(16, 128, 512).transpose(1, 0, 2)
            w[i, :, :n] = sub.reshape(128, n)
    return w


def bias_index():
    idx = {}
    n = 0
    for h in range(NH):
        ns = 512 // QBH[h]
        for blk in range(4):
            for qb in range(2):
                for kt in range(8):
                    for sub in range(ns):
                        idx[(h, blk, qb, kt, sub)] = n
                        n += 1
    return idx, n


BIDX, NBIAS = bias_index()


def make_bias(r):
    b = np.zeros((128, NBIAS), np.float32)
    j = np.arange(128, dtype=np.float64)
    for (h, blk, qb, kt, sub), col in BIDX.items():
        s = r if blk == 3 else blk
        if blk != 3 and s >= r:
            b[:, col] = NEG
            continue
        kpos = s * 1024 + kt * 128 + j
        qref = r * 1024 + qb * 512 + sub * QBH[h] + QBH[h] // 2
        b[:, col] = (SLOPES[h] * (kpos - qref)).astype(np.float32)
    return b


class Tok:
    __slots__ = ("name", "w", "r")

    def __init__(self, name):
        self.name = name
        self.w = None
        self.r = {}


class Prog:
    ENG = ("pe", "act", "dve", "pool", "sp")
    LIMIT = 240

    def __init__(self):
        self.q = {e: [] for e in self.ENG}
        self.cnt = {}
        self.known = {e: {} for e in self.ENG}
        self.semkeys = []
        self.epoch = {}
        self.keysrc = {}
        self.basekeys = {}

    def _key(self, base, inc, src):
        if base not in self.epoch:
            self.epoch[base] = 0
            self.basekeys[base] = []
            k = f"{base}.0"
            self.cnt[k] = 0
            self.semkeys.append(k)
            self.keysrc[k] = src
            self.basekeys[base].append(k)
        k = f"{base}.{self.epoch[base]}"
        if self.cnt[k] + inc > self.LIMIT:
            self.epoch[base] += 1
            k = f"{base}.{self.epoch[base]}"
            self.cnt[k] = 0
            self.semkeys.append(k)
            self.keysrc[k] = src
            self.basekeys[base].append(k)
        return k

    def total(self, base):
        return sum(self.cnt[k] for k in self.basekeys.get(base, []))

    def op(self, eng, fn, reads=(), writes=(), signal=True, dma=None, inc=16):
        src = "dma" if dma else eng
        need = {}

        def consider(ev, kind):
            key, val, s = ev
            if s == src and s != "dma":
                if s == "pe" or kind != "raw":
                    return
            if self.known[eng].get(key, 0) >= val:
                return
            if need.get(key, 0) < val:
                need[key] = val

        for t in reads:
            if t.w is not None:
                consider(t.w, "raw")
        for t in writes:
            if t.w is not None:
                consider(t.w, "waw")
            for key, (val, s) in t.r.items():
                consider((key, val, s), "war")
        for key, val in need.items():
            self.known[eng][key] = val
            self.q[eng].append(("wait", key, val))
        if dma:
            key = self._key(dma, inc, "dma")
            self.cnt[key] += inc
            ev = (key, self.cnt[key], "dma")
            self.q[eng].append(("dma", fn, key, inc))
        else:
            key = self._key(eng, 1, eng)
            if signal:
                self.cnt[key] += 1
                ev = (key, self.cnt[key], eng)
                self.q[eng].append(("sig", fn, key))
            else:
                ev = (key, self.cnt[key] + 1, eng)
                self.q[eng].append(("nosig", fn))
        for t in reads:
            old = t.r.get(ev[0])
            if old is None or old[0] < ev[1]:
                t.r[ev[0]] = (ev[1], ev[2])
        for t in writes:
            t.w = ev
            t.r = {}
        return ev

    def last_event(self, base):
        k = self.basekeys[base][-1]
        return (k, self.cnt[k], self.keysrc[k])

    def barrier(self):
        for e in self.ENG:
            for base, keys in self.basekeys.items():
                src = self.keysrc[keys[0]]
                if src == "dma":
                    wk = keys
                else:
                    if src == "pe" and e == "pe":
                        continue
                    wk = keys[-1:]
                    for k in keys[:-1]:
                        self.known[e][k] = self.cnt[k]
                for key in wk:
                    val = self.cnt[key]
                    if val and self.known[e].get(key, 0) < val:
                        self.known[e][key] = val
                        self.q[e].append(("wait", key, val))

    def final_wait(self, eng, bases):
        for base in bases:
            for key in self.basekeys.get(base, []):
                val = self.cnt[key]
                if val and self.known[eng].get(key, 0) < val:
                    self.known[eng][key] = val
                    self.q[eng].append(("wait", key, val))


NEED_TILES = {1: 1, 2: 41, 3: 57, 4: 97}


def build_program(debug=False, stop=4):
    nc = bass.Bass("TRN2", target_bir_lowering=False)
    P = Prog()
    tiles = weight_tiles()
    NT = min(len(tiles), NEED_TILES[stop])

    xT_d = nc.dram_tensor("xT", [128, KC * T], F32, kind="ExternalInput").ap()
    xh_d = nc.dram_tensor("xh", [128, KC * HALO], F32, kind="ExternalInput").ap()
    cols_d = nc.dram_tensor("cols", [128, NCOLS], F32, kind="ExternalInput").ap()
    bias_d = nc.dram_tensor("bias", [128, NBIAS], F32, kind="ExternalInput").ap()
    inv16_d = nc.dram_tensor("inv16", [128, 64], F32, kind="ExternalInput").ap()
    tri_d = nc.dram_tensor("tri", [128, 128], BF16, kind="ExternalInput").ap()
    ident_d = nc.dram_tensor("ident", [128, 128], BF16, kind="ExternalInput").ap()
    lamw_d = nc.dram_tensor("lamw", [128, 512], F32, kind="ExternalInput").ap()
    subg_d = nc.dram_tensor("subg", [128, 256], F32, kind="ExternalInput").ap()
    w_d = nc.dram_tensor("w", [NT, 128, WS], F32, kind="ExternalInput").ap()
    out_d = nc.dram_tensor("out", [128, KC * T], F32, kind="ExternalOutput").ap()
    ccik = [nc.dram_tensor(f"ccik{h}", [256, 1024], BF16, kind="Internal", addr_space="Local").ap() for h in range(NH)]
    ccok = [nc.dram_tensor(f"ccok{h}", [4 * 256, 1024], BF16, kind="Internal", addr_space="Local").ap() for h in range(NH)]
    cciv = [nc.dram_tensor(f"cciv{h}", [1024, 256], BF16, kind="Internal", addr_space="Local").ap() for h in range(NH)]
    ccov = [nc.dram_tensor(f"ccov{h}", [4 * 1024, 256], BF16, kind="Internal", addr_space="Local").ap() for h in range(NH)]
    qT_d = nc.dram_tensor("qT_s", [2048, 1024], BF16, kind="Internal", addr_space="Local").ap()
    dbg_d = None
    if debug:
        dbg_d = nc.dram_tensor("dbg", [4, 128, KC * T], F32, kind="ExternalOutput").ap()

    cur = [16512]

    def alloc(name, n, dt, at=None):
        sz = n * (4 if dt == F32 else 2)
        sz = (sz + 31) // 32 * 32
        if at is None:
            off = cur[0]
            cur[0] += sz
        else:
            off = at
        assert off + sz <= 229344, (name, off, sz)
        return nc.alloc_sbuf_tensor_at(name, [128, n], dt, offset=off).ap(), off + sz

    H, _ = alloc("H", KC * T, F32)
    HH, _ = alloc("HH", KC * HALO, F32)
    A, _ = alloc("A", KC * TE, BF16)
    WSL, _ = alloc("WSL", NSLOT * WS, BF16)
    COLS, _ = alloc("COLS", NCOLS, F32)
    BIAS, _ = alloc("BIAS", NBIAS, F32)
    INV16, _ = alloc("INV16", 64, F32)
    TRI, _ = alloc("TRI", 128, BF16)
    IDENT, _ = alloc("IDENT", 128, BF16)
    ONES, _ = alloc("ONES", 128, F32)
    LAMW, _ = alloc("LAMW", 512, F32)
    SUBG, _ = alloc("SUBG", 256, F32)
    SM, _ = alloc("SM", 32, F32)
    EPSC, _ = alloc("EPSC", 8, F32)
    RSTD, _ = alloc("RSTD", TE, F32)
    ONESB, _ = alloc("ONESB", 128, BF16)
    SQ = []
    for b in range(2):
        t_, _ = alloc(f"SQ{b}", 528, BF16)
        SQ.append(t_)
    S0 = cur[0]

    OFF = {}

    def layout(base, specs):
        res = {}
        off = base
        for name, n, dt in specs:
            OFF[name] = off
            res[name], off = alloc(name, n, dt, at=off)
        return res

    LP = layout(S0, [("SQH", 256, BF16), ("T16a", 16, F32), ("T16b", 16, F32)] +
                [(f"{n}{ab}", TE, F32) for ab in "ab" for n in ("E", "S2", "S4", "S8", "S16")])
    LF = layout(S0, [("ACTB", FCB * T, BF16), ("SG0", 512, F32), ("SG1", 512, F32)])
    LA = layout(S0, [("STG0", 1024, BF16), ("STG1", 1024, BF16)] +
                [(f"VB{i}", 8 * 257, BF16) for i in range(4)] +
                [(f"KB{i}", 1024, BF16) for i in range(4)] +
                [("QB0", 512, BF16), ("QB1", 512, BF16)] +
                [(f"P{i}", 512, BF16) for i in range(4)] +
                [("OC", 4 * 257, F32), ("OC2", 4 * 257, F32), ("OT", 256, F32)] +
                [(f"ON{i}", 256, BF16) for i in range(4)] +
                [("JUNK", 256, F32), ("SS", 16, F32)])
    LA.update(layout(OFF["VB0"], [(f"STG{i}", 1024, BF16) for i in range(2, 6)]))
    assert OFF["STG5"] + 2048 <= OFF["VB2"]
    LO = layout(S0, [("OUT0", T, F32), ("OUT1", T, F32)])

    Ht = [[Tok(f"H{k}_{hf}") for hf in range(2)] for k in range(KC)]
    HHt = Tok("HH")
    At = [[Tok(f"A{k}_{hf}") for hf in range(2)] for k in range(KC)]
    Aht = [Tok(f"Ah{k}") for k in range(KC)]
    Wt = [Tok(f"W{s}") for s in range(NSLOT)]
    CONSTt = Tok("const")
    ONESt = Tok("ones")
    RSTDt = Tok("rstd")
    SQt = [Tok("sq0"), Tok("sq1")]
    PSt = [Tok(f"ps{i}") for i in range(8)]
    SMt = Tok("sm")

    def Hk(k, t0=0, t1=T):
        return H[:, k * T + t0:k * T + t1]

    def Ak(k, t0=0, t1=T):
        return A[:, k * TE + HALO + t0:k * TE + HALO + t1]

    def col(c):
        return COLS[:, c:c + 1]

    def hslice(hf):
        return (hf * 512, (hf + 1) * 512)

    import contextlib
    es = contextlib.ExitStack()
    with es:
        PS = [es.enter_context(nc.psum_tensor(f"psb{i}", [128, 512], F32)) for i in range(8)]
        PSa = [p[:] for p in PS]
        bank_rr = [0]

        def nb():
            b = bank_rr[0]
            bank_rr[0] = (b + 1) % 8
            return b

        wstate = {"next_load": 0}

        def w_load(i):
            if i >= NT:
                return
            s = i % NSLOT
            n = tiles[i][2]
            P.op("pool", lambda e, i=i, s=s, n=n: e.dma_start(out=WSL[:, s * WS:s * WS + n], in_=w_d[i, :, 0:n]),
                 writes=[Wt[s]], dma=f"w{s}")

        def w_consume_done(i):
            w_load(i + NSLOT)

        def wslot(i, off, n=128):
            s = i % NSLOT
            return WSL[:, s * WS + off:s * WS + off + n]

        for g in range(4):
            P.op("sp", lambda e, g=g: e.dma_start(out=H[:, g * 4 * T:(g + 1) * 4 * T], in_=xT_d[:, g * 4 * T:(g + 1) * 4 * T]),
                 writes=[Ht[k][hf] for k in range(g * 4, g * 4 + 4) for hf in range(2)], dma=f"x{g}")
        P.op("sp", lambda e: e.dma_start(out=HH[:, :], in_=xh_d[:, :]), writes=[HHt], dma="xh")
        for dst, srcd in ((COLS, cols_d), (BIAS, bias_d), (INV16, inv16_d), (TRI, tri_d), (IDENT, ident_d),
                          (LAMW, lamw_d), (SUBG, subg_d)):
            P.op("sp", lambda e, dst=dst, srcd=srcd: e.dma_start(out=dst[:, :], in_=srcd[:, :]), writes=[CONSTt], dma="cst")
        CONSTt.w = P.last_event("cst")
        P.op("dve", lambda e: e.memset(ONES[:, :], 1.0), writes=[ONESt])
        P.op("dve", lambda e: e.memset(EPSC[:, :], EPS), writes=[ONESt])
        P.op("dve", lambda e: e.memset(ONESB[:, :], 1.0), writes=[ONESt])
        for i in range(NSLOT):
            w_load(i)
        wi = [0]

        def norm_stats(with_halo):
            banks = [nb(), nb()]
            n = 0
            for k in range(KC):
                for hf in range(2):
                    b = n % 2
                    n += 1
                    t0, t1 = hslice(hf)
                    P.op("act", lambda e, b=b, k=k, t0=t0, t1=t1: e.activation(out=SQ[b][:, 0:512], in_=Hk(k, t0, t1), func=AF.Square),
                         reads=[Ht[k][hf]], writes=[SQt[b]])
                    P.op("pe", lambda e, b=b, bk=banks[hf], k=k: e.matmul(PSa[bk][:, :], lhsT=ONESB[:, :], rhs=SQ[b][:, 0:512],
                                                                           start=(k == 0), stop=(k == KC - 1)),
                         reads=[SQt[b], ONESt], writes=[PSt[banks[hf]]], signal=True)
            hb = None
            if with_halo:
                hb = nb()
                SQH = LP["SQH"]
                sqht = Tok("sqh")
                P.op("act", lambda e: e.activation(out=SQH[:, :], in_=HH[:, :], func=AF.Square), reads=[HHt], writes=[sqht])
                for k in range(KC):
                    P.op("pe", lambda e, k=k: e.matmul(PSa[hb][:, 0:HALO], lhsT=ONESB[:, :], rhs=SQH[:, k * HALO:(k + 1) * HALO],
                                                        start=(k == 0), stop=(k == KC - 1)),
                         reads=[sqht, ONESt], writes=[PSt[hb]], signal=(k == KC - 1))
            for hf in range(2):
                t0, t1 = hslice(hf)
                P.op("act", lambda e, bk=banks[hf], t0=t0, t1=t1: e.activation(out=RSTD[:, HALO + t0:HALO + t1], in_=PSa[bk][:, :],
                                                                              func=AF.Sqrt, bias=EPSC[:, 0:1], scale=1.0 / D),
                     reads=[PSt[banks[hf]], ONESt], writes=[RSTDt])
            if with_halo:
                P.op("act", lambda e: e.activation(out=RSTD[:, 0:HALO], in_=PSa[hb][:, 0:HALO], func=AF.Sqrt, bias=EPSC[:, 0:1], scale=1.0 / D),
                     reads=[PSt[hb], ONESt], writes=[RSTDt])
            lo = 0 if with_halo else HALO
            P.op("dve", lambda e, lo=lo: e.reciprocal(out=RSTD[:, lo:TE], in_=RSTD[:, lo:TE]),
                 reads=[RSTDt], writes=[RSTDt])

        def norm_apply(cbase):
            for k in range(KC):
                P.op("dve", lambda e, k=k: e.scalar_tensor_tensor(out=Ak(k), in0=Hk(k), scalar=col(cbase + k), in1=RSTD[:, HALO:TE],
                                                                  op0=ALU.mult, op1=ALU.mult),
                     reads=[Ht[k][0], Ht[k][1], RSTDt, CONSTt], writes=[At[k][0], At[k][1]])

        def dump(i):
            if debug:
                P.barrier()
                P.op("sp", lambda e, i=i: e.dma_start(out=dbg_d[i], in_=H[:, :]),
                     reads=[Ht[k][hf] for k in range(KC) for hf in range(2)], dma="dbg")

        def pool_mixer():
            norm_stats(True)
            sets = []
            for ab in "ab":
                sets.append(dict(E=LP["E" + ab], S=[LP["S2" + ab], LP["S4" + ab], LP["S8" + ab], LP["S16" + ab]], T16=LP["T16" + ab],
                                 Et=Tok("E" + ab), St=[Tok(f"S{i}{ab}") for i in range(4)], T16t=Tok("T16" + ab)))

            def chunk_ops(k, B):
                g = k // 4
                w = 2 << g
                E, Et = B["E"], B["Et"]
                yield lambda: P.op("dve", lambda e: e.scalar_tensor_tensor(out=E[:, 0:HALO], in0=HH[:, k * HALO:(k + 1) * HALO], scalar=col(C_NMIX0 + k),
                                                                         in1=RSTD[:, 0:HALO], op0=ALU.mult, op1=ALU.mult),
                                   reads=[HHt, RSTDt, CONSTt], writes=[Et])
                yield lambda: P.op("dve", lambda e: e.scalar_tensor_tensor(out=E[:, HALO:TE], in0=Hk(k), scalar=col(C_NMIX0 + k),
                                                                         in1=RSTD[:, HALO:TE], op0=ALU.mult, op1=ALU.mult),
                                   reads=[Ht[k][0], Ht[k][1], RSTDt, CONSTt], writes=[Et])
                prev, prevt = E, Et
                lo = 0
                for step in range(g + 1):
                    sh = 1 << step
                    lo2 = lo + sh
                    dst, dstt = B["S"][step], B["St"][step]
                    yield lambda dst=dst, dstt=dstt, prev=prev, prevt=prevt, lo2=lo2, sh=sh: P.op(
                        "dve", lambda e: e.tensor_tensor(out=dst[:, lo2:TE], in0=prev[:, lo2:TE], in1=prev[:, lo2 - sh:TE - sh], op=ALU.add),
                        reads=[prevt], writes=[dstt])
                    prev, prevt, lo = dst, dstt, lo2
                S, Stok = prev, prevt
                yield lambda: P.op("dve", lambda e: e.scalar_tensor_tensor(out=Ak(k), in0=S[:, HALO:TE], scalar=1.0 / w, in1=E[:, HALO:TE],
                                                                         op0=ALU.mult, op1=ALU.subtract),
                                   reads=[Stok, Et], writes=[At[k][0], At[k][1]])
                yield lambda: P.op("dve", lambda e: e.tensor_tensor(out=B["T16"][:, :], in0=S[:, HALO:2 * HALO], in1=INV16[:, g * 16:(g + 1) * 16], op=ALU.mult),
                                   reads=[Stok, CONSTt], writes=[B["T16t"]])
                yield lambda: P.op("dve", lambda e: e.tensor_tensor(out=Ak(k, 0, HALO), in0=B["T16"][:, :], in1=E[:, HALO:2 * HALO], op=ALU.subtract),
                                   reads=[B["T16t"], Et, At[k][0]], writes=[At[k][0]])

            for k in range(0, KC, 2):
                ga, gb = chunk_ops(k, sets[0]), chunk_ops(k + 1, sets[1])
                while True:
                    fa, fb = next(ga, None), next(gb, None)
                    if fa is None and fb is None:
                        break
                    if fa is not None:
                        fa()
                    if fb is not None:
                        fb()
            ti = wi[0]
            for g in range(4):
                for oc in range(4):
                    c = g * 4 + oc
                    for hf in range(2):
                        t0, t1 = hslice(hf)
                        bk = nb()
                        for kk in range(4):
                            off = ((g * 4 + kk) * 4 + oc) * 128
                            last = (g == 3 and oc == 3 and hf == 1 and kk == 3)
                            P.op("pe", lambda e, bk=bk, off=off, kk=kk, g=g, t0=t0, t1=t1: e.matmul(
                                PSa[bk][:, :], lhsT=wslot(ti, off), rhs=Ak(g * 4 + kk, t0, t1), start=(kk == 0), stop=(kk == 3)),
                                 reads=[Wt[ti % NSLOT], At[g * 4 + kk][hf]], writes=[PSt[bk]], signal=(kk == 3))
                        P.op("dve", lambda e, bk=bk, c=c, t0=t0, t1=t1: e.scalar_tensor_tensor(
                            out=Hk(c, t0, t1), in0=PSa[bk][:, :], scalar=col(C_PSCALE + c), in1=Hk(c, t0, t1), op0=ALU.mult, op1=ALU.add),
                             reads=[PSt[bk], Ht[c][hf], CONSTt], writes=[Ht[c][hf]])
            w_consume_done(ti)
            wi[0] += 1

        def ffn(cnorm):
            norm_stats(False)
            norm_apply(cnorm)
            ACTB = LF["ACTB"]
            SG = [LF["SG0"], LF["SG1"]]
            SGt = [Tok("sg0"), Tok("sg1")]
            ACTt = [[Tok(f"act{j}_{hf}") for hf in range(2)] for j in range(FCB)]
            sgn = [0]
            for blk in range(NFB):
                jl = 0
                while jl < FCB:
                    nj = min(2, FCB - jl)
                    ti = wi[0]
                    for jj in range(nj):
                        j = jl + jj
                        banks = [nb() for _ in range(4)]
                        for k in range(KC):
                            for gu in range(2):
                                off = ((jj * 2 + gu) * 16 + k) * 128
                                for hf in range(2):
                                    t0, t1 = hslice(hf)
                                    bk = banks[gu * 2 + hf]
                                    P.op("pe", lambda e, bk=bk, off=off, k=k, t0=t0, t1=t1, ti=ti: e.matmul(
                                        PSa[bk][:, :], lhsT=wslot(ti, off), rhs=Ak(k, t0, t1), start=(k == 0), stop=(k == KC - 1)),
                                         reads=[Wt[ti % NSLOT], At[k][hf]], writes=[PSt[bk]], signal=(k == KC - 1))
                        for hf in range(2):
                            t0, t1 = hslice(hf)
                            b = sgn[0] % 2
                            sgn[0] += 1
                            P.op("act", lambda e, b=b, bk=banks[hf]: e.activation(out=SG[b][:, :], in_=PSa[bk][:, :], func=AF.Silu),
                                 reads=[PSt[banks[hf]]], writes=[SGt[b]])
                            P.op("dve", lambda e, b=b, bk=banks[2 + hf], j=j, t0=t0, t1=t1: e.tensor_tensor(
                                out=ACTB[:, j * T + t0:j * T + t1], in0=SG[b][:, :], in1=PSa[bk][:, :], op=ALU.mult),
                                 reads=[SGt[b], PSt[banks[2 + hf]]], writes=[ACTt[j][hf]])
                    w_consume_done(ti)
                    wi[0] += 1
                    jl += nj
                for dg in range(4):
                    ti = wi[0]
                    for dd in range(4):
                        dc = dg * 4 + dd
                        for hf in range(2):
                            t0, t1 = hslice(hf)
                            bk = nb()
                            for fc in range(FCB):
                                off = (fc * 4 + dd) * 128
                                P.op("pe", lambda e, bk=bk, off=off, fc=fc, t0=t0, t1=t1, ti=ti: e.matmul(
                                    PSa[bk][:, :], lhsT=wslot(ti, off), rhs=ACTB[:, fc * T + t0:fc * T + t1], start=(fc == 0), stop=(fc == FCB - 1)),
                                     reads=[Wt[ti % NSLOT], ACTt[fc][hf]], writes=[PSt[bk]], signal=(fc == FCB - 1))
                            P.op("dve", lambda e, bk=bk, dc=dc, t0=t0, t1=t1: e.tensor_tensor(
                                out=Hk(dc, t0, t1), in0=Hk(dc, t0, t1), in1=PSa[bk][:, :], op=ALU.add),
                                 reads=[PSt[bk], Ht[dc][hf]], writes=[Ht[dc][hf]])
                    w_consume_done(ti)
                    wi[0] += 1

        def attention():
            norm_stats(False)
            norm_apply(C_NMIX1)
            NSTG = 6
            STG = [LA[f"STG{i}"] for i in range(NSTG)]
            STGt = [Tok(f"stg{i}") for i in range(NSTG)]
            VB = [LA[f"VB{i}"] for i in range(4)]
            VBt = [Tok(f"vb{i}") for i in range(4)]
            KB = [LA[f"KB{i}"] for i in range(4)]
            KBt = [Tok(f"kb{i}") for i in range(4)]
            QB = [LA["QB0"], LA["QB1"]]
            QBt = [Tok("qb0"), Tok("qb1")]
            PT = [LA[f"P{i}"] for i in range(4)]
            PTt = [Tok(f"p{i}") for i in range(4)]
            OC, OT, JUNK, SS = LA["OC"], LA["OT"], LA["JUNK"], LA["SS"]
            OCt = [Tok(f"oc{i}") for i in range(4)]
            OTt, JUNKt, SSt = Tok("ot"), Tok("junk"), Tok("ss")
            Kd = [Tok(f"kd{c}") for c in range(KC)]
            Vd = [[Tok(f"vd{tt}_{h}") for h in range(NH)] for tt in range(8)]
            Qd = [Tok(f"qd{c}") for c in range(KC)]
            CCKt = [Tok(f"cck{h}") for h in range(NH)]
            CCVt = [Tok(f"ccv{h}") for h in range(NH)]
            stn = [0]

            LAM = SM[:, 0:1]
            P.op("dve", lambda e: e.scalar_tensor_tensor(out=JUNK[:, 0:128], in0=LAMW[:, 0:128], scalar=1.0, in1=LAMW[:, 128:256],
                                                         op0=ALU.mult, op1=ALU.mult, accum_out=SM[:, 1:2]),
                 reads=[CONSTt], writes=[JUNKt, SMt])
            P.op("dve", lambda e: e.scalar_tensor_tensor(out=JUNK[:, 128:256], in0=LAMW[:, 256:384], scalar=1.0, in1=LAMW[:, 384:512],
                                                         op0=ALU.mult, op1=ALU.mult, accum_out=SM[:, 2:3]),
                 reads=[CONSTt, SMt], writes=[JUNKt, SMt])
            P.op("act", lambda e: e.activation(out=SM[:, 3:5], in_=SM[:, 1:3], func=AF.Exp), reads=[SMt], writes=[SMt])
            P.op("dve", lambda e: e.tensor_tensor(out=SM[:, 5:6], in0=SM[:, 3:4], in1=SM[:, 4:5], op=ALU.subtract), reads=[SMt], writes=[SMt])
            P.op("dve", lambda e: e.tensor_scalar(out=LAM, in0=SM[:, 5:6], scalar1=float(LAMBDA_INIT), scalar2=None, op0=ALU.add),
                 reads=[SMt], writes=[SMt])
            P.op("dve", lambda e: e.tensor_scalar(out=SUBG[:, :], in0=SUBG[:, :], scalar1=float(1.0 - LAMBDA_INIT), scalar2=None, op0=ALU.mult),
                 reads=[CONSTt], writes=[CONSTt])

            def evac_copy(n, out_ap, in_ap, reads, writes):
                if n % 2 == 0:
                    P.op("act", lambda e: e.activation(out=out_ap, in_=in_ap, func=AF.Copy), reads=reads, writes=writes)
                else:
                    P.op("dve", lambda e: e.tensor_copy(out=out_ap, in_=in_ap), reads=reads, writes=writes)

            def proj_fm(dst_rows, dtoks, after_tile=None):
                for t in range(4):
                    ti = wi[0]
                    for ocl in range(4):
                        oc = t * 4 + ocl
                        b = stn[0] % NSTG
                        stn[0] += 1
                        for hf in range(2):
                            t0, t1 = hslice(hf)
                            bk = nb()
                            for k in range(KC):
                                off = (ocl * 16 + k) * 128
                                P.op("pe", lambda e, bk=bk, off=off, k=k, t0=t0, t1=t1, ti=ti: e.matmul(
                                    PSa[bk][:, :], lhsT=wslot(ti, off), rhs=Ak(k, t0, t1), start=(k == 0), stop=(k == KC - 1)),
                                     reads=[Wt[ti % NSLOT], At[k][hf]], writes=[PSt[bk]], signal=(k == KC - 1))
                            evac_copy(hf, STG[b][:, t0:t1], PSa[bk][:, :], [PSt[bk]], [STGt[b]])
                        P.op("sp", lambda e, b=b, oc=oc: e.dma_start(out=dst_rows(oc), in_=STG[b][:, :]),
                             reads=[STGt[b]], writes=[dtoks[oc]], dma=f"st{b}")
                    w_consume_done(ti)
                    wi[0] += 1
                    if after_tile is not None:
                        after_tile(t)

            def gather_k(t):
                for h in (2 * t, 2 * t + 1):
                    P.op("pool", lambda e, h=h: e.collective_compute("AllGather", ALU.bypass, replica_groups=GROUPS, ins=[ccik[h]], outs=[ccok[h]]),
                         reads=[Kd[2 * h], Kd[2 * h + 1]], writes=[CCKt[h]], dma=f"cck{h}", inc=1)

            def gather_v(cg):
                for h in (2 * cg, 2 * cg + 1):
                    P.op("pool", lambda e, h=h: e.collective_compute("AllGather", ALU.bypass, replica_groups=GROUPS, ins=[cciv[h]], outs=[ccov[h]]),
                         reads=[Vd[tt][h] for tt in range(8)], writes=[CCVt[h]], dma=f"ccv{h}", inc=1)

            proj_fm(lambda oc: ccik[oc // 2][(oc % 2) * 128:(oc % 2 + 1) * 128, :], Kd, after_tile=gather_k)
            for cg in range(4):
                ti = wi[0]
                for tt in range(8):
                    b = stn[0] % NSTG
                    stn[0] += 1
                    bk = nb()
                    for k in range(KC):
                        P.op("pe", lambda e, bk=bk, k=k, tt=tt, ti=ti: e.matmul(
                            PSa[bk][:, :], lhsT=Ak(k, tt * 128, (tt + 1) * 128), rhs=wslot(ti, k * 512, 512), start=(k == 0), stop=(k == KC - 1)),
                             reads=[Wt[ti % NSLOT], At[k][tt // 4]], writes=[PSt[bk]], signal=(k == KC - 1))
                    evac_copy(tt, STG[b][:, 0:512], PSa[bk][:, :], [PSt[bk]], [STGt[b]])
                    for hh in range(2):
                        h = 2 * cg + hh
                        P.op("sp", lambda e, b=b, tt=tt, h=h, hh=hh: e.dma_start(out=cciv[h][tt * 128:(tt + 1) * 128, :], in_=STG[b][:, hh * 256:(hh + 1) * 256]),
                             reads=[STGt[b]], writes=[Vd[tt][h]], dma=f"st{b}")
                w_consume_done(ti)
                wi[0] += 1
                gather_v(cg)
            proj_fm(lambda oc: qT_d[oc * 128:(oc + 1) * 128, :], Qd)

            P.barrier()
            for i in range(4):
                P.op("dve", lambda e, i=i: e.memset(VB[i].rearrange("p (k e) -> p k e", e=257)[:, :, 256:257], 1.0), writes=[VBt[i]])
            ACC = [0, 1, 2, 3]
            SCB = [4, 5, 6, 7]
            scn = [0]
            ptn = [0]
            qn = [0]
            onn = [0]
            LOOK = 2
            pending_pe = []
            pending_now = []
            OC2 = LA["OC2"]
            OC2t = [Tok(f"oc2_{i}") for i in range(4)]
            ONB = [LA[f"ON{i}"] for i in range(4)]
            ONBt = [Tok(f"on{i}") for i in range(4)]

            for h in range(NH):
                nsub = 512 // QBH[h]
                for blk in range(4):
                    vsrc = cciv[h] if blk == 3 else ccov[h][blk * 1024:(blk + 1) * 1024, :]
                    vsrc = vsrc.rearrange("(k p) e -> p k e", p=128)
                    rd = [CCVt[h]] if blk < 3 else [Vd[tt][h] for tt in range(8)]
                    P.op("sp", lambda e, blk=blk, vsrc=vsrc: e.dma_start(out=VB[blk].rearrange("p (k e) -> p k e", e=257)[:, :, 0:256], in_=vsrc),
                         reads=rd, writes=[VBt[blk]], dma=f"v{blk}")
                for qb in range(2):
                    for c in range(2):
                        ch = 2 * h + c
                        qi = qn[0] % 2
                        qn[0] += 1
                        P.op("sp", lambda e, qi=qi, ch=ch, qb=qb: e.dma_start(out=QB[qi][:, :], in_=qT_d[ch * 128:(ch + 1) * 128, qb * 512:(qb + 1) * 512]),
                             reads=[Qd[ch]], writes=[QBt[qi]], dma=f"q{qi}")
                        for blk in range(4):
                            rd = [CCKt[h]] if blk < 3 else [Kd[ch]]
                            ksrc = ccik[h][c * 128:(c + 1) * 128, :] if blk == 3 else ccok[h][blk * 256 + c * 128:blk * 256 + (c + 1) * 128, :]
                            P.op("sp", lambda e, blk=blk, ksrc=ksrc: e.dma_start(out=KB[blk][:, :], in_=ksrc),
                                 reads=rd, writes=[KBt[blk]], dma=f"k{blk}")
                        work = []
                        for blk in range(3):
                            for kt in range(8):
                                work.append((blk, kt, 0, False))
                        for kt in range(8):
                            if kt < 4 * qb:
                                work.append((3, kt, 0, False))
                            elif kt <= 4 * qb + 3:
                                work.append((3, kt, kt - 4 * qb, True))
                        nW = len(work)
                        pis = [None] * nW

                        def emit_score(i):
                            blk, kt, m, diag = work[i]
                            c0 = 128 * m
                            sb = SCB[scn[0] % 4]
                            scn[0] += 1
                            pi = ptn[0] % 4
                            ptn[0] += 1
                            pis[i] = pi
                            P.op("pe", lambda e, sb=sb, blk=blk, kt=kt, qi=qi, c0=c0: e.matmul(
                                PSa[sb][:, c0:512], lhsT=KB[blk][:, kt * 128:(kt + 1) * 128], rhs=QB[qi][:, c0:512], start=True, stop=True),
                                 reads=[KBt[blk], QBt[qi]], writes=[PSt[sb]], signal=True)
                            for sub in range(nsub):
                                a0 = max(c0, sub * QBH[h])
                                a1 = (sub + 1) * QBH[h]
                                if a0 >= a1:
                                    continue
                                bc = BIDX[(h, blk, qb, kt, sub)]
                                P.op("act", lambda e, pi=pi, sb=sb, a0=a0, a1=a1, bc=bc: e.activation(
                                    out=PT[pi][:, a0:a1], in_=PSa[sb][:, a0:a1], func=AF.Exp, bias=BIAS[:, bc:bc + 1], scale=float(SCALE)),
                                     reads=[PSt[sb], CONSTt], writes=[PTt[pi]])
                            if diag:
                                P.op("dve", lambda e, pi=pi, c0=c0: e.tensor_tensor(out=PT[pi][:, c0:c0 + 128], in0=PT[pi][:, c0:c0 + 128], in1=TRI[:, :], op=ALU.mult),
                                     reads=[PTt[pi], CONSTt], writes=[PTt[pi]])

                        def emit_av(i):
                            blk, kt, m, diag = work[i]
                            pi = pis[i]
                            for qs in range(m, 4):
                                first = (i == 0)
                                lastk = (blk == 3 and kt == 4 * qb + qs)
                                P.op("pe", lambda e, qs=qs, pi=pi, blk=blk, kt=kt, first=first, lastk=lastk: e.matmul(
                                    PSa[ACC[qs]][:, 0:257], lhsT=PT[pi][:, qs * 128:(qs + 1) * 128], rhs=VB[blk][:, kt * 257:(kt + 1) * 257],
                                    start=first, stop=lastk),
                                     reads=[PTt[pi], VBt[blk]], writes=[PSt[ACC[qs]]], signal=(lastk or qs == 3))

                        for idx in range(nW + LOOK):
                            if idx < nW:
                                emit_score(idx)
                            if idx == 8 and pending_pe:
                                for fn_ in pending_pe:
                                    fn_()
                                pending_pe.clear()
                            if idx >= LOOK:
                                emit_av(idx - LOOK)
                        dstb = OC if c == 0 else OC2
                        dstt = OCt if c == 0 else OC2t
                        for qs in range(4):
                            P.op("dve", lambda e, qs=qs, dstb=dstb: e.tensor_copy(out=dstb[:, qs * 257:(qs + 1) * 257], in_=PSa[ACC[qs]][:, 0:257]),
                                 reads=[PSt[ACC[qs]]], writes=[dstt[qs]])
                        if c == 1:
                            OC3 = OC.rearrange("p (q e) -> p q e", e=257)
                            OC23 = OC2.rearrange("p (q e) -> p q e", e=257)
                            P.op("dve", lambda e: e.reciprocal(out=SS[:, 0:4], in_=OC3[:, :, 256]), reads=OCt, writes=[SSt])
                            P.op("dve", lambda e: e.reciprocal(out=SS[:, 4:8], in_=OC23[:, :, 256]), reads=OC2t + [SSt], writes=[SSt])
                            P.op("dve", lambda e: e.tensor_scalar(out=SS[:, 4:8], in0=SS[:, 4:8], scalar1=LAM, scalar2=None, op0=ALU.mult),
                                 reads=[SSt, SMt], writes=[SSt])
                            for qs in range(4):
                                ocq = OC[:, qs * 257:qs * 257 + 256]
                                oc2q = OC2[:, qs * 257:qs * 257 + 256]
                                P.op("dve", lambda e, oc2q=oc2q, qs=qs: e.tensor_scalar(out=OT[:, :], in0=oc2q, scalar1=SS[:, 4 + qs:5 + qs], scalar2=None, op0=ALU.mult),
                                     reads=[OC2t[qs], SSt], writes=[OTt])
                                P.op("dve", lambda e, ocq=ocq, qs=qs: e.scalar_tensor_tensor(out=ocq, in0=ocq, scalar=SS[:, qs:qs + 1], in1=OT[:, :],
                                                                                            op0=ALU.mult, op1=ALU.subtract),
                                     reads=[OCt[qs], OTt, SSt], writes=[OCt[qs]])
                                P.op("dve", lambda e, ocq=ocq, qs=qs: e.scalar_tensor_tensor(out=JUNK[:, :], in0=ocq, scalar=1.0, in1=ocq, op0=ALU.mult, op1=ALU.mult,
                                                                                            accum_out=SS[:, 8 + qs:9 + qs]),
                                     reads=[OCt[qs], SSt], writes=[JUNKt, SSt])
                            P.op("act", lambda e: e.activation(out=SS[:, 12:16], in_=SS[:, 8:12], func=AF.Ln, bias=EPSC[:, 0:1], scale=1.0 / 256),
                                 reads=[SSt, ONESt], writes=[SSt])
                            P.op("act", lambda e: e.activation(out=SS[:, 12:16], in_=SS[:, 12:16], func=AF.Exp, scale=-0.5),
                                 reads=[SSt], writes=[SSt])
                            for qs in range(4):
                                tq = (qb * 4 + qs) * 128
                                ocq = OC[:, qs * 257:qs * 257 + 256]
                                oi = onn[0] % 4
                                onn[0] += 1
                                P.op("dve", lambda e, ocq=ocq, qs=qs, oi=oi: e.scalar_tensor_tensor(out=ONB[oi][:, :], in0=ocq, scalar=SS[:, 12 + qs:13 + qs], in1=SUBG[:, :],
                                                                                                   op0=ALU.mult, op1=ALU.mult),
                                     reads=[OCt[qs], SSt, CONSTt], writes=[ONBt[oi]])

                                def tr_fn(oi=oi, tq=tq, h=h):
                                    for j in range(2):
                                        sb = SCB[scn[0] % 4]
                                        scn[0] += 1
                                        pv = PSa[sb].bitcast(BF16)
                                        P.op("pe", lambda e, pv=pv, j=j, oi=oi: e.transpose(out=pv[:, 0:128], in_=ONB[oi][:, j * 128:(j + 1) * 128], identity=IDENT[:, :]),
                                             reads=[ONBt[oi], CONSTt], writes=[PSt[sb]], signal=True)
                                        kk = 2 * h + j
                                        P.op("dve", lambda e, pv=pv, kk=kk, tq=tq: e.tensor_copy(out=Ak(kk, tq, tq + 128), in_=pv[:, 0:128]),
                                             reads=[PSt[sb]], writes=[At[kk][tq // 512]])
                                if qs % 2 == 1:
                                    pending_now.append(tr_fn)
                                else:
                                    pending_now.append(tr_fn)
                            pending_pe.extend(pending_now)
                            pending_now.clear()
            for fn_ in pending_pe:
                fn_()
            pending_pe.clear()

            for t in range(4):
                ti = wi[0]
                for ocl in range(4):
                    oc = t * 4 + ocl
                    for hf in range(2):
                        t0, t1 = hslice(hf)
                        bk = nb()
                        for k in range(KC):
                            off = (ocl * 16 + k) * 128
                            P.op("pe", lambda e, bk=bk, off=off, k=k, t0=t0, t1=t1, ti=ti: e.matmul(
                                PSa[bk][:, :], lhsT=wslot(ti, off), rhs=Ak(k, t0, t1), start=(k == 0), stop=(k == KC - 1)),
                                 reads=[Wt[ti % NSLOT], At[k][hf]], writes=[PSt[bk]], signal=(k == KC - 1))
                        P.op("dve", lambda e, bk=bk, oc=oc, t0=t0, t1=t1: e.tensor_tensor(out=Hk(oc, t0, t1), in0=Hk(oc, t0, t1), in1=PSa[bk][:, :], op=ALU.add),
                             reads=[PSt[bk], Ht[oc][hf]], writes=[Ht[oc][hf]])
                w_consume_done(ti)
                wi[0] += 1

        pool_mixer()
        dump(0)
        P.barrier()
        if stop >= 2:
            ffn(C_NFFN0)
            dump(1)
            P.barrier()
        if stop >= 3:
            attention()
            dump(2)
            P.barrier()
        if stop >= 4:
            ffn(C_NFFN1)
            dump(3)
            P.barrier()
        norm_stats(False)
        OUT = [LO["OUT0"], LO["OUT1"]]
        OUTt = [Tok("out0"), Tok("out1")]
        for k in range(KC):
            b = k % 2
            P.op("dve", lambda e, k=k, b=b: e.scalar_tensor_tensor(out=OUT[b][:, :], in0=Hk(k), scalar=col(C_FINAL + k), in1=RSTD[:, HALO:TE],
                                                                  op0=ALU.mult, op1=ALU.mult),
                 reads=[Ht[k][0], Ht[k][1], RSTDt, CONSTt], writes=[OUTt[b]])
            P.op("sp", lambda e, k=k, b=b: e.dma_start(out=out_d[:, k * T:(k + 1) * T], in_=OUT[b][:, :]), reads=[OUTt[b]], dma=f"o{b}")
        assert wi[0] == NT, (wi[0], NT)
        P.final_wait("sp", ["o0", "o1", "dbg"])

        sems = {}
        for key in P.semkeys:
            sems[key] = es.enter_context(nc.semaphore(f"s_{key}"))
        block = es.enter_context(nc.Block())

        def run(e, name):
            for item in P.q[name]:
                kind = item[0]
                if kind == "wait":
                    e.wait_ge(sems[item[1]], item[2])
                elif kind == "dma":
                    item[1](e).then_inc(sems[item[2]], item[3])
                elif kind == "sig":
                    item[1](e).then_inc(sems[item[2]], 1)
                else:
                    item[1](e)

        @block.tensor
        def _(e):
            run(e, "pe")

        @block.scalar
        def _(e):
            run(e, "act")

        @block.vector
        def _(e):
            run(e, "dve")

        @block.gpsimd
        def _(e):
            run(e, "pool")

        @block.sync
        def _(e):
            run(e, "sp")
    return nc


def _fm(a, n):
    return np.ascontiguousarray(a.T.reshape(KC, 128, n).transpose(1, 0, 2).reshape(128, KC * n))


def _colvec(v):
    return v.reshape(KC, 128).T


DEBUG = False
STOP = 4
_LAST = {}


def kernel(x, norm_mix, norm_ffn, pool_w, pool_scale, w_qkv, lambda_q1, lambda_k1, lambda_q2, lambda_k2,
           subln_g, w_o, w_gate, w_up, w_down, final_norm):
    inp = dict(x=x, norm_mix=norm_mix, norm_ffn=norm_ffn, pool_w=pool_w, pool_scale=pool_scale, w_qkv=w_qkv,
               w_o=w_o, w_gate=w_gate, w_up=w_up, w_down=w_down)
    inp = {k: np.asarray(v, dtype=np.float32) for k, v in inp.items()}
    x = inp["x"]
    wpack = pack_weights(inp)
    cols = np.zeros((128, NCOLS), np.float32)
    cols[:, C_NMIX0:C_NMIX0 + 16] = _colvec(inp["norm_mix"][0])
    cols[:, C_NFFN0:C_NFFN0 + 16] = _colvec(inp["norm_ffn"][0])
    cols[:, C_PSCALE:C_PSCALE + 16] = _colvec(inp["pool_scale"][0])
    cols[:, C_NMIX1:C_NMIX1 + 16] = _colvec(inp["norm_mix"][1])
    cols[:, C_NFFN1:C_NFFN1 + 16] = _colvec(inp["norm_ffn"][1])
    cols[:, C_FINAL:C_FINAL + 16] = _colvec(np.asarray(final_norm, np.float32))
    lamw = np.concatenate([np.asarray(v, np.float32).reshape(1, 128) for v in (lambda_q1, lambda_k1, lambda_q2, lambda_k2)], axis=1)
    lamw = np.ascontiguousarray(np.broadcast_to(lamw, (128, 512)))
    subg = np.ascontiguousarray(np.broadcast_to(np.asarray(subln_g, np.float32).reshape(1, 256), (128, 256)))
    tri = (np.arange(128)[None, :] >= np.arange(128)[:, None]).astype(ml_dtypes.bfloat16)
    ident = np.eye(128).astype(ml_dtypes.bfloat16)

    in_maps = []
    for c in range(NCORES):
        b, r = c // 4, c % 4
        xs = x[b, r * T:(r + 1) * T, :]
        if r == 0:
            xh = np.zeros((HALO, D), np.float32)
        else:
            xh = x[b, r * T - HALO:r * T, :]
        inv16 = np.zeros((128, 64), np.float32)
        for g in range(4):
            w = 2 << g
            tpos = r * T + np.arange(16)
            inv16[:, g * 16:(g + 1) * 16] = (1.0 / np.minimum(tpos + 1, w)).astype(np.float32)[None, :]
        in_maps.append({
            "xT": _fm(xs, T), "xh": _fm(xh, HALO), "cols": cols, "bias": make_bias(r), "inv16": inv16,
            "tri": tri, "ident": ident, "lamw": lamw, "subg": subg, "w": wpack,
        })
    nc = build_program(debug=DEBUG, stop=STOP)
    for m in in_maps:
        m["w"] = wpack[:NEED_TILES[STOP]]
    res = run_bass_kernel_spmd(nc, in_maps, core_ids=list(range(NCORES)))
    out = np.zeros((2, 4096, D), np.float32)
    for c in range(NCORES):
        b, r = c // 4, c % 4
        o = np.asarray(res.results[c]["out"]).reshape(128, KC, T).transpose(2, 1, 0).reshape(T, D)
        out[b, r * T:(r + 1) * T, :] = o
    if DEBUG:
        _LAST["dbg"] = [np.asarray(res.results[c]["dbg"]) for c in range(NCORES)]
    return out
```

```python
import math
import numpy as np
import ml_dtypes
import concourse.bass as bass
import concourse.mybir as mybir
from concourse.bass_utils import run_bass_kernel_spmd

F32 = mybir.dt.float32
BF16 = mybir.dt.bfloat16
AF = mybir.ActivationFunctionType
ALU = mybir.AluOpType

NCORES = 8
D = 2048
T = 1024
HALO = 16
TE = T + HALO
KC = 16
FF = 5632
FC = 44
NFB = 4
FCB = FC // NFB
NH = 8
WS = 8192
NSLOT = 3
EPS = 1e-6
LAMBDA_INIT = 0.8 - 0.6 * math.exp(-0.3 * 1)
SCALE = 128 ** -0.5
SLOPES = [2.0 ** (-8.0 * (i + 1) / NH) for i in range(NH)]
QBH = [128, 256, 512, 512, 512, 512, 512, 512]
NEG = -30000.0
GROUPS = [[0, 1, 2, 3], [4, 5, 6, 7]]

C_NMIX0, C_NFFN0, C_PSCALE, C_NMIX1, C_NFFN1, C_FINAL = [i * 16 for i in range(6)]
NCOLS = 96


def weight_tiles():
    tiles = [("pool", (), 8192)]

    def ffn(l):
        out = []
        for blk in range(NFB):
            j0 = blk * FCB
            jl = 0
            while jl < FCB:
                nj = min(2, FCB - jl)
                out.append(("gu", (l, j0 + jl, nj), nj * 4096))
                jl += nj
            for dg in range(4):
                out.append(("down", (l, blk, dg), FCB * 512))
        return out

    tiles += ffn(0)
    for t in range(4):
        tiles.append(("k", (t,), 8192))
    for t in range(4):
        tiles.append(("v", (t,), 8192))
    for t in range(4):
        tiles.append(("q", (t,), 8192))
    for t in range(4):
        tiles.append(("o", (t,), 8192))
    tiles += ffn(1)
    return tiles


def pack_weights(inp):
    tiles = weight_tiles()
    w = np.zeros((len(tiles), 128, WS), np.float32)
    wqkv = inp["w_qkv"][0]
    for i, (kind, a, n) in enumerate(tiles):
        if kind == "pool":
            pw = inp["pool_w"][0].reshape(4, 4, 128, 4, 128).transpose(2, 0, 1, 3, 4)
            w[i, :, :n] = pw.reshape(128, n)
        elif kind == "gu":
            l, j0, nj = a
            for jj in range(nj):
                for gu, W in enumerate((inp["w_gate"][l], inp["w_up"][l])):
                    sub = W[:, (j0 + jj) * 128:(j0 + jj + 1) * 128].reshape(16, 128, 128).transpose(1, 0, 2)
                    o = (jj * 2 + gu) * 2048
                    w[i, :, o:o + 2048] = sub.reshape(128, 2048)
        elif kind == "down":
            l, blk, dg = a
            sub = inp["w_down"][l][blk * FCB * 128:(blk + 1) * FCB * 128, dg * 512:(dg + 1) * 512]
            sub = sub.reshape(FCB, 128, 4, 128).transpose(1, 0, 2, 3)
            w[i, :, :n] = sub.reshape(128, n)
        elif kind in ("k", "q", "o"):
            t = a[0]
            if kind == "q":
                W = wqkv[:, 0:2048]
            elif kind == "k":
                W = wqkv[:, 2048:4096]
            else:
                W = inp["w_o"][0]
            sub = W[:, t * 512:(t + 1) * 512].reshape(16, 128, 4, 128).transpose(1, 2, 0, 3)
            w[i, :, :n] = sub.reshape(128, n)
        elif kind == "v":
            t = a[0]
            sub = wqkv[:, 4096 + t * 512:4096 + (t + 1) * 512].reshape(16, 128, 512).transpose(1, 0, 2)
            w[i, :, :n] = sub.reshape(128, n)
    return w


def bias_index():
    idx = {}
    n = 0
    for h in range(NH):
        ns = 512 // QBH[h]
        for blk in range(4):
            for qb in range(2):
                for kt in range(8):
                    for sub in range(ns):
                        idx[(h, blk, qb, kt, sub)] = n
                        n += 1
    return idx, n


BIDX, NBIAS = bias_index()


def make_bias(r):
    b = np.zeros((128, NBIAS), np.float32)
    j = np.arange(128, dtype=np.float64)
    for (h, blk, qb, kt, sub), col in BIDX.items():
        s = r if blk == 3 else blk
        if blk != 3 and s >= r:
            b[:, col] = NEG
            continue
        kpos = s * 1024 + kt * 128 + j
        qref = r * 1024 + qb * 512 + sub * QBH[h] + QBH[h] // 2
        b[:, col] = (SLOPES[h] * (kpos - qref)).astype(np.float32)
    return b


class Tok:
    __slots__ = ("name", "w", "r")

    def __init__(self, name):
        self.name = name
        self.w = None
        self.r = {}


class Prog:
    ENG = ("pe", "act", "dve", "pool", "sp")
    LIMIT = 240

    def __init__(self):
        self.q = {e: [] for e in self.ENG}
        self.cnt = {}
        self.known = {e: {} for e in self.ENG}
        self.semkeys = []
        self.epoch = {}
        self.keysrc = {}
        self.basekeys = {}

    def _key(self, base, inc, src):
        if base not in self.epoch:
            self.epoch[base] = 0
            self.basekeys[base] = []
            k = f"{base}.0"
            self.cnt[k] = 0
            self.semkeys.append(k)
            self.keysrc[k] = src
            self.basekeys[base].append(k)
        k = f"{base}.{self.epoch[base]}"
        if self.cnt[k] + inc > self.LIMIT:
            self.epoch[base] += 1
            k = f"{base}.{self.epoch[base]}"
            self.cnt[k] = 0
            self.semkeys.append(k)
            self.keysrc[k] = src
            self.basekeys[base].append(k)
        return k

    def total(self, base):
        return sum(self.cnt[k] for k in self.basekeys.get(base, []))

    def op(self, eng, fn, reads=(), writes=(), signal=True, dma=None, inc=16):
        src = "dma" if dma else eng
        need = {}

        def consider(ev, kind):
            key, val, s = ev
            if s == src and s != "dma":
                if s == "pe" or kind != "raw":
                    return
            if self.known[eng].get(key, 0) >= val:
                return
            if need.get(key, 0) < val:
                need[key] = val

        for t in reads:
            if t.w is not None:
                consider(t.w, "raw")
        for t in writes:
            if t.w is not None:
                consider(t.w, "waw")
            for key, (val, s) in t.r.items():
                consider((key, val, s), "war")
        for key, val in need.items():
            self.known[eng][key] = val
            self.q[eng].append(("wait", key, val))
        if dma:
            key = self._key(dma, inc, "dma")
            self.cnt[key] += inc
            ev = (key, self.cnt[key], "dma")
            self.q[eng].append(("dma", fn, key, inc))
        else:
            key = self._key(eng, 1, eng)
            if signal:
                self.cnt[key] += 1
                ev = (key, self.cnt[key], eng)
                self.q[eng].append(("sig", fn, key))
            else:
                ev = (key, self.cnt[key] + 1, eng)
                self.q[eng].append(("nosig", fn))
        for t in reads:
            old = t.r.get(ev[0])
            if old is None or old[0] < ev[1]:
                t.r[ev[0]] = (ev[1], ev[2])
        for t in writes:
            t.w = ev
            t.r = {}
        return ev

    def last_event(self, base):
        k = self.basekeys[base][-1]
        return (k, self.cnt[k], self.keysrc[k])

    def barrier(self):
        for e in self.ENG:
            for base, keys in self.basekeys.items():
                src = self.keysrc[keys[0]]
                if src == "dma":
                    wk = keys
                else:
                    if src == "pe" and e == "pe":
                        continue
                    wk = keys[-1:]
                    for k in keys[:-1]:
                        self.known[e][k] = self.cnt[k]
                for key in wk:
                    val = self.cnt[key]
                    if val and self.known[e].get(key, 0) < val:
                        self.known[e][key] = val
                        self.q[e].append(("wait", key, val))

    def final_wait(self, eng, bases):
        for base in bases:
            for key in self.basekeys.get(base, []):
                val = self.cnt[key]
                if val and self.known[eng].get(key, 0) < val:
                    self.known[eng][key] = val
                    self.q[eng].append(("wait", key, val))


NEED_TILES = {1: 1, 2: 41, 3: 57, 4: 97}


def build_program(debug=False, stop=4):
    nc = bass.Bass("TRN2", target_bir_lowering=False)
    P = Prog()
    tiles = weight_tiles()
    NT = min(len(tiles), NEED_TILES[stop])

    xT_d = nc.dram_tensor("xT", [128, KC * T], F32, kind="ExternalInput").ap()
    xh_d = nc.dram_tensor("xh", [128, KC * HALO], F32, kind="ExternalInput").ap()
    cols_d = nc.dram_tensor("cols", [128, NCOLS], F32, kind="ExternalInput").ap()
    bias_d = nc.dram_tensor("bias", [128, NBIAS], F32, kind="ExternalInput").ap()
    inv16_d = nc.dram_tensor("inv16", [128, 64], F32, kind="ExternalInput").ap()
    tri_d = nc.dram_tensor("tri", [128, 128], BF16, kind="ExternalInput").ap()
    ident_d = nc.dram_tensor("ident", [128, 128], BF16, kind="ExternalInput").ap()
    lamw_d = nc.dram_tensor("lamw", [128, 512], F32, kind="ExternalInput").ap()
    subg_d = nc.dram_tensor("subg", [128, 256], F32, kind="ExternalInput").ap()
    w_d = nc.dram_tensor("w", [NT, 128, WS], F32, kind="ExternalInput").ap()
    out_d = nc.dram_tensor("out", [128, KC * T], F32, kind="ExternalOutput").ap()
    ccik = [nc.dram_tensor(f"ccik{h}", [256, 1024], BF16, kind="Internal", addr_space="Local").ap() for h in range(NH)]
    ccok = [nc.dram_tensor(f"ccok{h}", [4 * 256, 1024], BF16, kind="Internal", addr_space="Local").ap() for h in range(NH)]
    cciv = [nc.dram_tensor(f"cciv{h}", [1024, 256], BF16, kind="Internal", addr_space="Local").ap() for h in range(NH)]
    ccov = [nc.dram_tensor(f"ccov{h}", [4 * 1024, 256], BF16, kind="Internal", addr_space="Local").ap() for h in range(NH)]
    qT_d = nc.dram_tensor("qT_s", [2048, 1024], BF16, kind="Internal", addr_space="Local").ap()
    dbg_d = None
    if debug:
        dbg_d = nc.dram_tensor("dbg", [4, 128, KC * T], F32, kind="ExternalOutput").ap()

    cur = [16512]

    def alloc(name, n, dt, at=None):
        sz = n * (4 if dt == F32 else 2)
        sz = (sz + 31) // 32 * 32
        if at is None:
            off = cur[0]
            cur[0] += sz
        else:
            off = at
        assert off + sz <= 229344, (name, off, sz)
        return nc.alloc_sbuf_tensor_at(name, [128, n], dt, offset=off).ap(), off + sz

    H, _ = alloc("H", KC * T, F32)
    HH, _ = alloc("HH", KC * HALO, F32)
    A, _ = alloc("A", KC * TE, BF16)
    WSL, _ = alloc("WSL", NSLOT * WS, BF16)
    COLS, _ = alloc("COLS", NCOLS, F32)
    BIAS, _ = alloc("BIAS", NBIAS, F32)
    INV16, _ = alloc("INV16", 64, F32)
    TRI, _ = alloc("TRI", 128, BF16)
    IDENT, _ = alloc("IDENT", 128, BF16)
    ONES, _ = alloc("ONES", 128, F32)
    LAMW, _ = alloc("LAMW", 512, F32)
    SUBG, _ = alloc("SUBG", 256, F32)
    SM, _ = alloc("SM", 32, F32)
    EPSC, _ = alloc("EPSC", 8, F32)
    RSTD, _ = alloc("RSTD", TE, F32)
    ONESB, _ = alloc("ONESB", 128, BF16)
    SQ = []
    for b in range(4):
        t_, _ = alloc(f"SQ{b}", 528, BF16)
        SQ.append(t_)
    S0 = cur[0]

    OFF = {}

    def layout(base, specs):
        res = {}
        off = base
        for name, n, dt in specs:
            OFF[name] = off
            res[name], off = alloc(name, n, dt, at=off)
        return res

    LP = layout(S0, [("SQH", 256, BF16), ("T16a", 16, F32), ("T16b", 16, F32)] +
                [(f"{n}{ab}", TE, F32) for ab in "ab" for n in ("E", "S2", "S4", "S8", "S16")])
    LF = layout(S0, [("ACTB", FCB * T, BF16), ("SG0", 512, F32), ("SG1", 512, F32)])
    LA = layout(S0, [("STG0", 1024, BF16), ("STG1", 1024, BF16)] +
                [(f"VB{i}", 8 * 257, BF16) for i in range(4)] +
                [(f"KB{i}", 1024, BF16) for i in range(4)] +
                [("QB0", 512, BF16), ("QB1", 512, BF16)] +
                [(f"P{i}", 512, BF16) for i in range(4)] +
                [("OC", 4 * 257, F32), ("OC2", 4 * 257, F32), ("OT", 256, F32)] +
                [(f"ON{i}", 256, BF16) for i in range(4)] +
                [("SS", 16, F32)])
    LA.update(layout(OFF["OC"], [("STG2", 1024, BF16), ("STG3", 1024, BF16)]))
    LA.update(layout(OFF["OC2"], [("STG4", 1024, BF16), ("STG5", 1024, BF16)]))
    assert OFF["STG3"] + 2048 <= OFF["OC2"] and OFF["STG5"] + 2048 <= OFF["OT"]
    LO = layout(S0, [("OUT0", T, F32), ("OUT1", T, F32)])

    Ht = [[Tok(f"H{k}_{hf}") for hf in range(2)] for k in range(KC)]
    HHt = Tok("HH")
    At = [[Tok(f"A{k}_{hf}") for hf in range(2)] for k in range(KC)]
    Aht = [Tok(f"Ah{k}") for k in range(KC)]
    Wt = [Tok(f"W{s}") for s in range(NSLOT)]
    CONSTt = Tok("const")
    ONESt = Tok("ones")
    RSTDt = Tok("rstd")
    SQt = [Tok(f"sq{i}") for i in range(4)]
    PSt = [Tok(f"ps{i}") for i in range(8)]
    SMt = Tok("sm")

    def Hk(k, t0=0, t1=T):
        return H[:, k * T + t0:k * T + t1]

    def Ak(k, t0=0, t1=T):
        return A[:, k * TE + HALO + t0:k * TE + HALO + t1]

    def col(c):
        return COLS[:, c:c + 1]

    def hslice(hf):
        return (hf * 512, (hf + 1) * 512)

    import contextlib
    es = contextlib.ExitStack()
    with es:
        PS = [es.enter_context(nc.psum_tensor(f"psb{i}", [128, 512], F32)) for i in range(8)]
        PSa = [p[:] for p in PS]
        bank_rr = [0]

        def nb():
            b = bank_rr[0]
            bank_rr[0] = (b + 1) % 8
            return b

        wstate = {"next_load": 0}

        def w_load(i, after=()):
            if i >= NT:
                return
            s = i % NSLOT
            n = tiles[i][2]
            P.op("pool", lambda e, i=i, s=s, n=n: e.dma_start(out=WSL[:, s * WS:s * WS + n], in_=w_d[i, :, 0:n]),
                 reads=list(after), writes=[Wt[s]], dma=f"w{s}")

        def w_consume_done(i):
            w_load(i + NSLOT)

        def wslot(i, off, n=128):
            s = i % NSLOT
            return WSL[:, s * WS + off:s * WS + off + n]

        for g in range(4):
            P.op("sp", lambda e, g=g: e.dma_start(out=H[:, g * 4 * T:(g + 1) * 4 * T], in_=xT_d[:, g * 4 * T:(g + 1) * 4 * T]),
                 writes=[Ht[k][hf] for k in range(g * 4, g * 4 + 4) for hf in range(2)], dma=f"x{g}")
        P.op("sp", lambda e: e.dma_start(out=HH[:, :], in_=xh_d[:, :]), writes=[HHt], dma="xh")
        for dst, srcd in ((COLS, cols_d), (BIAS, bias_d), (INV16, inv16_d), (TRI, tri_d), (IDENT, ident_d),
                          (LAMW, lamw_d), (SUBG, subg_d)):
            P.op("sp", lambda e, dst=dst, srcd=srcd: e.dma_start(out=dst[:, :], in_=srcd[:, :]), writes=[CONSTt], dma="cst")
        CONSTt.w = P.last_event("cst")
        P.op("dve", lambda e: e.memset(ONES[:, :], 1.0), writes=[ONESt])
        P.op("dve", lambda e: e.memset(EPSC[:, :], EPS), writes=[ONESt])
        P.op("dve", lambda e: e.memset(ONESB[:, :], 1.0), writes=[ONESt])
        for i in range(NSLOT):
            w_load(i, after=[Ht[11][1]] if i == 0 else [Ht[15][1]])
        wi = [0]

        def norm_stats(with_halo):
            banks = [nb(), nb()]
            n = 0
            for k in range(KC):
                for hf in range(2):
                    b = n % 4
                    n += 1
                    t0, t1 = hslice(hf)
                    P.op("act", lambda e, b=b, k=k, t0=t0, t1=t1: e.activation(out=SQ[b][:, 0:512], in_=Hk(k, t0, t1), func=AF.Square),
                         reads=[Ht[k][hf]], writes=[SQt[b]])
                    P.op("pe", lambda e, b=b, bk=banks[hf], k=k: e.matmul(PSa[bk][:, :], lhsT=ONESB[:, :], rhs=SQ[b][:, 0:512],
                                                                           start=(k == 0), stop=(k == KC - 1)),
                         reads=[SQt[b], ONESt], writes=[PSt[banks[hf]]], signal=True)
            hb = None
            if with_halo:
                hb = nb()
                SQH = LP["SQH"]
                sqht = Tok("sqh")
                P.op("act", lambda e: e.activation(out=SQH[:, :], in_=HH[:, :], func=AF.Square), reads=[HHt], writes=[sqht])
                for k in range(KC):
                    P.op("pe", lambda e, k=k: e.matmul(PSa[hb][:, 0:HALO], lhsT=ONESB[:, :], rhs=SQH[:, k * HALO:(k + 1) * HALO],
                                                        start=(k == 0), stop=(k == KC - 1)),
                         reads=[sqht, ONESt], writes=[PSt[hb]], signal=(k == KC - 1))
            for hf in range(2):
                t0, t1 = hslice(hf)
                P.op("act", lambda e, bk=banks[hf], t0=t0, t1=t1: e.activation(out=RSTD[:, HALO + t0:HALO + t1], in_=PSa[bk][:, :],
                                                                              func=AF.Sqrt, bias=EPSC[:, 0:1], scale=1.0 / D),
                     reads=[PSt[banks[hf]], ONESt], writes=[RSTDt])
            if with_halo:
                P.op("act", lambda e: e.activation(out=RSTD[:, 0:HALO], in_=PSa[hb][:, 0:HALO], func=AF.Sqrt, bias=EPSC[:, 0:1], scale=1.0 / D),
                     reads=[PSt[hb], ONESt], writes=[RSTDt])
            lo = 0 if with_halo else HALO
            P.op("dve", lambda e, lo=lo: e.reciprocal(out=RSTD[:, lo:TE], in_=RSTD[:, lo:TE]),
                 reads=[RSTDt], writes=[RSTDt])

        def norm_apply(cbase):
            for k in range(KC):
                P.op("dve", lambda e, k=k: e.scalar_tensor_tensor(out=Ak(k), in0=Hk(k), scalar=col(cbase + k), in1=RSTD[:, HALO:TE],
                                                                  op0=ALU.mult, op1=ALU.mult),
                     reads=[Ht[k][0], Ht[k][1], RSTDt, CONSTt], writes=[At[k][0], At[k][1]])

        def dump(i):
            if debug:
                P.barrier()
                P.op("sp", lambda e, i=i: e.dma_start(out=dbg_d[i], in_=H[:, :]),
                     reads=[Ht[k][hf] for k in range(KC) for hf in range(2)], dma="dbg")

        def pool_mixer():
            norm_stats(True)
            sets = []
            for ab in "ab":
                sets.append(dict(E=LP["E" + ab], S=[LP["S2" + ab], LP["S4" + ab], LP["S8" + ab], LP["S16" + ab]], T16=LP["T16" + ab],
                                 Et=Tok("E" + ab), St=[Tok(f"S{i}{ab}") for i in range(4)], T16t=Tok("T16" + ab)))

            def chunk_ops(k, B):
                g = k // 4
                w = 2 << g
                E, Et = B["E"], B["Et"]
                yield lambda: P.op("dve", lambda e: e.scalar_tensor_tensor(out=E[:, 0:HALO], in0=HH[:, k * HALO:(k + 1) * HALO], scalar=col(C_NMIX0 + k),
                                                                         in1=RSTD[:, 0:HALO], op0=ALU.mult, op1=ALU.mult),
                                   reads=[HHt, RSTDt, CONSTt], writes=[Et])
                yield lambda: P.op("dve", lambda e: e.scalar_tensor_tensor(out=E[:, HALO:TE], in0=Hk(k), scalar=col(C_NMIX0 + k),
                                                                         in1=RSTD[:, HALO:TE], op0=ALU.mult, op1=ALU.mult),
                                   reads=[Ht[k][0], Ht[k][1], RSTDt, CONSTt], writes=[Et])
                prev, prevt = E, Et
                lo = 0
                for step in range(g + 1):
                    sh = 1 << step
                    lo2 = lo + sh
                    dst, dstt = B["S"][step], B["St"][step]
                    yield lambda dst=dst, dstt=dstt, prev=prev, prevt=prevt, lo2=lo2, sh=sh: P.op(
                        "dve", lambda e: e.tensor_tensor(out=dst[:, lo2:TE], in0=prev[:, lo2:TE], in1=prev[:, lo2 - sh:TE - sh], op=ALU.add),
                        reads=[prevt], writes=[dstt])
                    prev, prevt, lo = dst, dstt, lo2
                S, Stok = prev, prevt
                yield lambda: P.op("dve", lambda e: e.scalar_tensor_tensor(out=Ak(k), in0=S[:, HALO:TE], scalar=1.0 / w, in1=E[:, HALO:TE],
                                                                         op0=ALU.mult, op1=ALU.subtract),
                                   reads=[Stok, Et], writes=[At[k][0], At[k][1]])
                yield lambda: P.op("dve", lambda e: e.tensor_tensor(out=B["T16"][:, :], in0=S[:, HALO:2 * HALO], in1=INV16[:, g * 16:(g + 1) * 16], op=ALU.mult),
                                   reads=[Stok, CONSTt], writes=[B["T16t"]])
                yield lambda: P.op("dve", lambda e: e.tensor_tensor(out=Ak(k, 0, HALO), in0=B["T16"][:, :], in1=E[:, HALO:2 * HALO], op=ALU.subtract),
                                   reads=[B["T16t"], Et, At[k][0]], writes=[At[k][0]])

            for k in range(0, KC, 2):
                ga, gb = chunk_ops(k, sets[0]), chunk_ops(k + 1, sets[1])
                while True:
                    fa, fb = next(ga, None), next(gb, None)
                    if fa is None and fb is None:
                        break
                    if fa is not None:
                        fa()
                    if fb is not None:
                        fb()
            ti = wi[0]
            for g in range(4):
                for oc in range(4):
                    c = g * 4 + oc
                    for hf in range(2):
                        t0, t1 = hslice(hf)
                        bk = nb()
                        for kk in range(4):
                            off = ((g * 4 + kk) * 4 + oc) * 128
                            last = (g == 3 and oc == 3 and hf == 1 and kk == 3)
                            P.op("pe", lambda e, bk=bk, off=off, kk=kk, g=g, t0=t0, t1=t1: e.matmul(
                                PSa[bk][:, :], lhsT=wslot(ti, off), rhs=Ak(g * 4 + kk, t0, t1), start=(kk == 0), stop=(kk == 3)),
                                 reads=[Wt[ti % NSLOT], At[g * 4 + kk][hf]], writes=[PSt[bk]], signal=(kk == 3))
                        P.op("dve", lambda e, bk=bk, c=c, t0=t0, t1=t1: e.scalar_tensor_tensor(
                            out=Hk(c, t0, t1), in0=PSa[bk][:, :], scalar=col(C_PSCALE + c), in1=Hk(c, t0, t1), op0=ALU.mult, op1=ALU.add),
                             reads=[PSt[bk], Ht[c][hf], CONSTt], writes=[Ht[c][hf]])
            w_consume_done(ti)
            wi[0] += 1

        def ffn(cnorm):
            norm_stats(False)
            norm_apply(cnorm)
            ACTB = LF["ACTB"]
            SG = [LF["SG0"], LF["SG1"]]
            SGt = [Tok("sg0"), Tok("sg1")]
            ACTt = [[Tok(f"act{j}_{hf}") for hf in range(2)] for j in range(FCB)]
            sgn = [0]
            for blk in range(NFB):
                jl = 0
                while jl < FCB:
                    nj = min(2, FCB - jl)
                    ti = wi[0]
                    for jj in range(nj):
                        j = jl + jj
                        banks = [nb() for _ in range(4)]
                        for k in range(KC):
                            for gu in range(2):
                                off = ((jj * 2 + gu) * 16 + k) * 128
                                for hf in range(2):
                                    t0, t1 = hslice(hf)
                                    bk = banks[gu * 2 + hf]
                                    P.op("pe", lambda e, bk=bk, off=off, k=k, t0=t0, t1=t1, ti=ti: e.matmul(
                                        PSa[bk][:, :], lhsT=wslot(ti, off), rhs=Ak(k, t0, t1), start=(k == 0), stop=(k == KC - 1)),
                                         reads=[Wt[ti % NSLOT], At[k][hf]], writes=[PSt[bk]], signal=(k == KC - 1))
                        for hf in range(2):
                            t0, t1 = hslice(hf)
                            b = sgn[0] % 2
                            sgn[0] += 1
                            P.op("act", lambda e, b=b, bk=banks[hf]: e.activation(out=SG[b][:, :], in_=PSa[bk][:, :], func=AF.Silu),
                                 reads=[PSt[banks[hf]]], writes=[SGt[b]])
                            P.op("dve", lambda e, b=b, bk=banks[2 + hf], j=j, t0=t0, t1=t1: e.tensor_tensor(
                                out=ACTB[:, j * T + t0:j * T + t1], in0=SG[b][:, :], in1=PSa[bk][:, :], op=ALU.mult),
                                 reads=[SGt[b], PSt[banks[2 + hf]]], writes=[ACTt[j][hf]])
                    w_consume_done(ti)
                    wi[0] += 1
                    jl += nj
                for dg in range(4):
                    ti = wi[0]
                    for dd in range(4):
                        dc = dg * 4 + dd
                        for hf in range(2):
                            t0, t1 = hslice(hf)
                            bk = nb()
                            for fc in range(FCB):
                                off = (fc * 4 + dd) * 128
                                P.op("pe", lambda e, bk=bk, off=off, fc=fc, t0=t0, t1=t1, ti=ti: e.matmul(
                                    PSa[bk][:, :], lhsT=wslot(ti, off), rhs=ACTB[:, fc * T + t0:fc * T + t1], start=(fc == 0), stop=(fc == FCB - 1)),
                                     reads=[Wt[ti % NSLOT], ACTt[fc][hf]], writes=[PSt[bk]], signal=(fc == FCB - 1))
                            P.op("dve", lambda e, bk=bk, dc=dc, t0=t0, t1=t1: e.tensor_tensor(
                                out=Hk(dc, t0, t1), in0=Hk(dc, t0, t1), in1=PSa[bk][:, :], op=ALU.add),
                                 reads=[PSt[bk], Ht[dc][hf]], writes=[Ht[dc][hf]])
                    w_consume_done(ti)
                    wi[0] += 1

        def attention():
            norm_stats(False)
            norm_apply(C_NMIX1)
            NSTG = 6
            STG = [LA[f"STG{i}"] for i in range(NSTG)]
            STGt = [Tok(f"stg{i}") for i in range(NSTG)]
            VB = [LA[f"VB{i}"] for i in range(4)]
            VBt = [Tok(f"vb{i}") for i in range(4)]
            KB = [LA[f"KB{i}"] for i in range(4)]
            KBt = [Tok(f"kb{i}") for i in range(4)]
            QB = [LA["QB0"], LA["QB1"]]
            QBt = [Tok("qb0"), Tok("qb1")]
            PT = [LA[f"P{i}"] for i in range(4)]
            PTt = [Tok(f"p{i}") for i in range(4)]
            OC, OT, SS = LA["OC"], LA["OT"], LA["SS"]
            JUNK = OT
            OCt = [Tok(f"oc{i}") for i in range(4)]
            OTt, SSt = Tok("ot"), Tok("ss")
            JUNKt = OTt
            Kd = [Tok(f"kd{c}") for c in range(KC)]
            Vd = [[Tok(f"vd{tt}_{h}") for h in range(NH)] for tt in range(8)]
            Qd = [Tok(f"qd{c}") for c in range(KC)]
            CCKt = [Tok(f"cck{h}") for h in range(NH)]
            CCVt = [Tok(f"ccv{h}") for h in range(NH)]
            stn = [0]

            LAM = SM[:, 0:1]
            P.op("dve", lambda e: e.scalar_tensor_tensor(out=JUNK[:, 0:128], in0=LAMW[:, 0:128], scalar=1.0, in1=LAMW[:, 128:256],
                                                         op0=ALU.mult, op1=ALU.mult, accum_out=SM[:, 1:2]),
                 reads=[CONSTt], writes=[JUNKt, SMt])
            P.op("dve", lambda e: e.scalar_tensor_tensor(out=JUNK[:, 128:256], in0=LAMW[:, 256:384], scalar=1.0, in1=LAMW[:, 384:512],
                                                         op0=ALU.mult, op1=ALU.mult, accum_out=SM[:, 2:3]),
                 reads=[CONSTt, SMt], writes=[JUNKt, SMt])
            P.op("act", lambda e: e.activation(out=SM[:, 3:5], in_=SM[:, 1:3], func=AF.Exp), reads=[SMt], writes=[SMt])
            P.op("dve", lambda e: e.tensor_tensor(out=SM[:, 5:6], in0=SM[:, 3:4], in1=SM[:, 4:5], op=ALU.subtract), reads=[SMt], writes=[SMt])
            P.op("dve", lambda e: e.tensor_scalar(out=LAM, in0=SM[:, 5:6], scalar1=float(LAMBDA_INIT), scalar2=None, op0=ALU.add),
                 reads=[SMt], writes=[SMt])
            P.op("dve", lambda e: e.tensor_scalar(out=SUBG[:, :], in0=SUBG[:, :], scalar1=float(1.0 - LAMBDA_INIT), scalar2=None, op0=ALU.mult),
                 reads=[CONSTt], writes=[CONSTt])

            for i in range(4):
                P.op("dve", lambda e, i=i: e.memset(VB[i].rearrange("p (k e) -> p k e", e=257)[:, :, 256:257], 1.0), writes=[VBt[i]])

            def evac_copy(n, out_ap, in_ap, reads, writes):
                if n % 2 == 0:
                    P.op("act", lambda e: e.activation(out=out_ap, in_=in_ap, func=AF.Copy), reads=reads, writes=writes)
                else:
                    P.op("dve", lambda e: e.tensor_copy(out=out_ap, in_=in_ap), reads=reads, writes=writes)

            def proj_fm(dst_rows, dtoks, after_tile=None):
                for t in range(4):
                    ti = wi[0]
                    for ocl in range(4):
                        oc = t * 4 + ocl
                        b = stn[0] % NSTG
                        stn[0] += 1
                        for hf in range(2):
                            t0, t1 = hslice(hf)
                            bk = nb()
                            for k in range(KC):
                                off = (ocl * 16 + k) * 128
                                P.op("pe", lambda e, bk=bk, off=off, k=k, t0=t0, t1=t1, ti=ti: e.matmul(
                                    PSa[bk][:, :], lhsT=wslot(ti, off), rhs=Ak(k, t0, t1), start=(k == 0), stop=(k == KC - 1)),
                                     reads=[Wt[ti % NSLOT], At[k][hf]], writes=[PSt[bk]], signal=(k == KC - 1))
                            evac_copy(hf, STG[b][:, t0:t1], PSa[bk][:, :], [PSt[bk]], [STGt[b]])
                        P.op("sp", lambda e, b=b, oc=oc: e.dma_start(out=dst_rows(oc), in_=STG[b][:, :]),
                             reads=[STGt[b]], writes=[dtoks[oc]], dma=f"st{b}")
                    w_consume_done(ti)
                    wi[0] += 1
                    if after_tile is not None:
                        after_tile(t)

            def gather_k(t):
                for h in (2 * t, 2 * t + 1):
                    P.op("pool", lambda e, h=h: e.collective_compute("AllGather", ALU.bypass, replica_groups=GROUPS, ins=[ccik[h]], outs=[ccok[h]]),
                         reads=[Kd[2 * h], Kd[2 * h + 1]], writes=[CCKt[h]], dma=f"cck{h}", inc=1)

            def gather_v(cg):
                for h in (2 * cg, 2 * cg + 1):
                    P.op("pool", lambda e, h=h: e.collective_compute("AllGather", ALU.bypass, replica_groups=GROUPS, ins=[cciv[h]], outs=[ccov[h]]),
                         reads=[Vd[tt][h] for tt in range(8)], writes=[CCVt[h]], dma=f"ccv{h}", inc=1)

            proj_fm(lambda oc: ccik[oc // 2][(oc % 2) * 128:(oc % 2 + 1) * 128, :], Kd, after_tile=gather_k)
            for cg in range(4):
                ti = wi[0]
                for tt in range(8):
                    b = stn[0] % NSTG
                    stn[0] += 1
                    bk = nb()
                    for k in range(KC):
                        P.op("pe", lambda e, bk=bk, k=k, tt=tt, ti=ti: e.matmul(
                            PSa[bk][:, :], lhsT=Ak(k, tt * 128, (tt + 1) * 128), rhs=wslot(ti, k * 512, 512), start=(k == 0), stop=(k == KC - 1)),
                             reads=[Wt[ti % NSLOT], At[k][tt // 4]], writes=[PSt[bk]], signal=(k == KC - 1))
                    evac_copy(tt, STG[b][:, 0:512], PSa[bk][:, :], [PSt[bk]], [STGt[b]])
                    for hh in range(2):
                        h = 2 * cg + hh
                        P.op("sp", lambda e, b=b, tt=tt, h=h, hh=hh: e.dma_start(out=cciv[h][tt * 128:(tt + 1) * 128, :], in_=STG[b][:, hh * 256:(hh + 1) * 256]),
                             reads=[STGt[b]], writes=[Vd[tt][h]], dma=f"st{b}")
                w_consume_done(ti)
                wi[0] += 1
                gather_v(cg)
            proj_fm(lambda oc: qT_d[oc * 128:(oc + 1) * 128, :], Qd)

            ACC = [0, 1, 2, 3]
            SCB = [4, 5, 6, 7]
            scn = [0]
            ptn = [0]
            qn = [0]
            onn = [0]
            LOOK = 2
            pending_pe = []
            pending_now = []
            OC2 = LA["OC2"]
            OC2t = [Tok(f"oc2_{i}") for i in range(4)]
            ONB = [LA[f"ON{i}"] for i in range(4)]
            ONBt = [Tok(f"on{i}") for i in range(4)]

            for h in range(NH):
                nsub = 512 // QBH[h]
                for blk in range(4):
                    vsrc = cciv[h] if blk == 3 else ccov[h][blk * 1024:(blk + 1) * 1024, :]
                    vsrc = vsrc.rearrange("(k p) e -> p k e", p=128)
                    rd = [CCVt[h]] if blk < 3 else [Vd[tt][h] for tt in range(8)]
                    P.op("sp", lambda e, blk=blk, vsrc=vsrc: e.dma_start(out=VB[blk].rearrange("p (k e) -> p k e", e=257)[:, :, 0:256], in_=vsrc),
                         reads=rd, writes=[VBt[blk]], dma=f"v{blk}")
                for qb in range(2):
                    for c in range(2):
                        ch = 2 * h + c
                        qi = qn[0] % 2
                        qn[0] += 1
                        P.op("sp", lambda e, qi=qi, ch=ch, qb=qb: e.dma_start(out=QB[qi][:, :], in_=qT_d[ch * 128:(ch + 1) * 128, qb * 512:(qb + 1) * 512]),
                             reads=[Qd[ch]], writes=[QBt[qi]], dma=f"q{qi}")
                        for blk in range(4):
                            rd = [CCKt[h]] if blk < 3 else [Kd[ch]]
                            ksrc = ccik[h][c * 128:(c + 1) * 128, :] if blk == 3 else ccok[h][blk * 256 + c * 128:blk * 256 + (c + 1) * 128, :]
                            P.op("sp", lambda e, blk=blk, ksrc=ksrc: e.dma_start(out=KB[blk][:, :], in_=ksrc),
                                 reads=rd, writes=[KBt[blk]], dma=f"k{blk}")
                        work = []
                        for blk in range(3):
                            for kt in range(8):
                                work.append((blk, kt, 0, False))
                        for kt in range(8):
                            if kt < 4 * qb:
                                work.append((3, kt, 0, False))
                            elif kt <= 4 * qb + 3:
                                work.append((3, kt, kt - 4 * qb, True))
                        nW = len(work)
                        pis = [None] * nW

                        def emit_score(i):
                            blk, kt, m, diag = work[i]
                            c0 = 128 * m
                            sb = SCB[scn[0] % 4]
                            scn[0] += 1
                            pi = ptn[0] % 4
                            ptn[0] += 1
                            pis[i] = pi
                            P.op("pe", lambda e, sb=sb, blk=blk, kt=kt, qi=qi, c0=c0: e.matmul(
                                PSa[sb][:, c0:512], lhsT=KB[blk][:, kt * 128:(kt + 1) * 128], rhs=QB[qi][:, c0:512], start=True, stop=True),
                                 reads=[KBt[blk], QBt[qi]], writes=[PSt[sb]], signal=True)
                            for sub in range(nsub):
                                a0 = max(c0, sub * QBH[h])
                                a1 = (sub + 1) * QBH[h]
                                if a0 >= a1:
                                    continue
                                bc = BIDX[(h, blk, qb, kt, sub)]
                                P.op("act", lambda e, pi=pi, sb=sb, a0=a0, a1=a1, bc=bc: e.activation(
                                    out=PT[pi][:, a0:a1], in_=PSa[sb][:, a0:a1], func=AF.Exp, bias=BIAS[:, bc:bc + 1], scale=float(SCALE)),
                                     reads=[PSt[sb], CONSTt], writes=[PTt[pi]])
                            if diag:
                                P.op("dve", lambda e, pi=pi, c0=c0: e.tensor_tensor(out=PT[pi][:, c0:c0 + 128], in0=PT[pi][:, c0:c0 + 128], in1=TRI[:, :], op=ALU.mult),
                                     reads=[PTt[pi], CONSTt], writes=[PTt[pi]])

                        def emit_av(i):
                            blk, kt, m, diag = work[i]
                            pi = pis[i]
                            for qs in range(m, 4):
                                first = (i == 0)
                                lastk = (blk == 3 and kt == 4 * qb + qs)
                                P.op("pe", lambda e, qs=qs, pi=pi, blk=blk, kt=kt, first=first, lastk=lastk: e.matmul(
                                    PSa[ACC[qs]][:, 0:257], lhsT=PT[pi][:, qs * 128:(qs + 1) * 128], rhs=VB[blk][:, kt * 257:(kt + 1) * 257],
                                    start=first, stop=lastk),
                                     reads=[PTt[pi], VBt[blk]], writes=[PSt[ACC[qs]]], signal=(lastk or qs == 3))

                        for idx in range(nW + LOOK):
                            if idx < nW:
                                emit_score(idx)
                            if idx == 8 and pending_pe:
                                for fn_ in pending_pe:
                                    fn_()
                                pending_pe.clear()
                            if idx >= LOOK:
                                emit_av(idx - LOOK)
                        dstb = OC if c == 0 else OC2
                        dstt = OCt if c == 0 else OC2t
                        alias = [STGt[2], STGt[3]] if c == 0 else [STGt[4], STGt[5]]
                        for qs in range(4):
                            P.op("dve", lambda e, qs=qs, dstb=dstb: e.tensor_copy(out=dstb[:, qs * 257:(qs + 1) * 257], in_=PSa[ACC[qs]][:, 0:257]),
                                 reads=[PSt[ACC[qs]]], writes=[dstt[qs]] + alias)
                        if c == 1:
                            OC3 = OC.rearrange("p (q e) -> p q e", e=257)
                            OC23 = OC2.rearrange("p (q e) -> p q e", e=257)
                            P.op("dve", lambda e: e.reciprocal(out=SS[:, 0:4], in_=OC3[:, :, 256]), reads=OCt, writes=[SSt])
                            P.op("dve", lambda e: e.reciprocal(out=SS[:, 4:8], in_=OC23[:, :, 256]), reads=OC2t + [SSt], writes=[SSt])
                            P.op("dve", lambda e: e.tensor_scalar(out=SS[:, 4:8], in0=SS[:, 4:8], scalar1=LAM, scalar2=None, op0=ALU.mult),
                                 reads=[SSt, SMt], writes=[SSt])
                            for qs in range(4):
                                ocq = OC[:, qs * 257:qs * 257 + 256]
                                oc2q = OC2[:, qs * 257:qs * 257 + 256]
                                P.op("dve", lambda e, oc2q=oc2q, qs=qs: e.tensor_scalar(out=OT[:, :], in0=oc2q, scalar1=SS[:, 4 + qs:5 + qs], scalar2=None, op0=ALU.mult),
                                     reads=[OC2t[qs], SSt], writes=[OTt])
                                P.op("dve", lambda e, ocq=ocq, qs=qs: e.scalar_tensor_tensor(out=ocq, in0=ocq, scalar=SS[:, qs:qs + 1], in1=OT[:, :],
                                                                                            op0=ALU.mult, op1=ALU.subtract),
                                     reads=[OCt[qs], OTt, SSt], writes=[OCt[qs]])
                                P.op("dve", lambda e, ocq=ocq, qs=qs: e.scalar_tensor_tensor(out=JUNK[:, :], in0=ocq, scalar=1.0, in1=ocq, op0=ALU.mult, op1=ALU.mult,
                                                                                            accum_out=SS[:, 8 + qs:9 + qs]),
                                     reads=[OCt[qs], SSt], writes=[JUNKt, SSt])
                            P.op("act", lambda e: e.activation(out=SS[:, 12:16], in_=SS[:, 8:12], func=AF.Ln, bias=EPSC[:, 0:1], scale=1.0 / 256),
                                 reads=[SSt, ONESt], writes=[SSt])
                            P.op("act", lambda e: e.activation(out=SS[:, 12:16], in_=SS[:, 12:16], func=AF.Exp, scale=-0.5),
                                 reads=[SSt], writes=[SSt])
                            for qs in range(4):
                                tq = (qb * 4 + qs) * 128
                                ocq = OC[:, qs * 257:qs * 257 + 256]
                                oi = onn[0] % 4
                                onn[0] += 1
                                P.op("dve", lambda e, ocq=ocq, qs=qs, oi=oi: e.scalar_tensor_tensor(out=ONB[oi][:, :], in0=ocq, scalar=SS[:, 12 + qs:13 + qs], in1=SUBG[:, :],
                                                                                                   op0=ALU.mult, op1=ALU.mult),
                                     reads=[OCt[qs], SSt, CONSTt], writes=[ONBt[oi]])

                                def tr_fn(oi=oi, tq=tq, h=h):
                                    for j in range(2):
                                        sb = SCB[scn[0] % 4]
                                        scn[0] += 1
                                        pv = PSa[sb].bitcast(BF16)
                                        P.op("pe", lambda e, pv=pv, j=j, oi=oi: e.transpose(out=pv[:, 0:128], in_=ONB[oi][:, j * 128:(j + 1) * 128], identity=IDENT[:, :]),
                                             reads=[ONBt[oi], CONSTt], writes=[PSt[sb]], signal=True)
                                        kk = 2 * h + j
                                        P.op("dve", lambda e, pv=pv, kk=kk, tq=tq: e.tensor_copy(out=Ak(kk, tq, tq + 128), in_=pv[:, 0:128]),
                                             reads=[PSt[sb]], writes=[At[kk][tq // 512]])
                                if qs % 2 == 1:
                                    pending_now.append(tr_fn)
                                else:
                                    pending_now.append(tr_fn)
                            pending_pe.extend(pending_now)
                            pending_now.clear()
            for fn_ in pending_pe:
                fn_()
            pending_pe.clear()

            for t in range(4):
                ti = wi[0]
                for ocl in range(4):
                    oc = t * 4 + ocl
                    for hf in range(2):
                        t0, t1 = hslice(hf)
                        bk = nb()
                        for k in range(KC):
                            off = (ocl * 16 + k) * 128
                            P.op("pe", lambda e, bk=bk, off=off, k=k, t0=t0, t1=t1, ti=ti: e.matmul(
                                PSa[bk][:, :], lhsT=wslot(ti, off), rhs=Ak(k, t0, t1), start=(k == 0), stop=(k == KC - 1)),
                                 reads=[Wt[ti % NSLOT], At[k][hf]], writes=[PSt[bk]], signal=(k == KC - 1))
                        P.op("dve", lambda e, bk=bk, oc=oc, t0=t0, t1=t1: e.tensor_tensor(out=Hk(oc, t0, t1), in0=Hk(oc, t0, t1), in1=PSa[bk][:, :], op=ALU.add),
                             reads=[PSt[bk], Ht[oc][hf]], writes=[Ht[oc][hf]])
                w_consume_done(ti)
                wi[0] += 1

        pool_mixer()
        dump(0)
        P.barrier()
        if stop >= 2:
            ffn(C_NFFN0)
            dump(1)
            P.barrier()
        if stop >= 3:
            attention()
            dump(2)
            P.barrier()
        if stop >= 4:
            ffn(C_NFFN1)
            dump(3)
            P.barrier()
        norm_stats(False)
        OUT = [LO["OUT0"], LO["OUT1"]]
        OUTt = [Tok("out0"), Tok("out1")]
        for k in range(KC):
            b = k % 2
            P.op("dve", lambda e, k=k, b=b: e.scalar_tensor_tensor(out=OUT[b][:, :], in0=Hk(k), scalar=col(C_FINAL + k), in1=RSTD[:, HALO:TE],
                                                                  op0=ALU.mult, op1=ALU.mult),
                 reads=[Ht[k][0], Ht[k][1], RSTDt, CONSTt], writes=[OUTt[b]])
            P.op("sp", lambda e, k=k, b=b: e.dma_start(out=out_d[:, k * T:(k + 1) * T], in_=OUT[b][:, :]), reads=[OUTt[b]], dma=f"o{b}")
        assert wi[0] == NT, (wi[0], NT)
        P.final_wait("sp", ["o0", "o1", "dbg"])

        sems = {}
        for key in P.semkeys:
            sems[key] = es.enter_context(nc.semaphore(f"s_{key}"))
        block = es.enter_context(nc.Block())

        def run(e, name):
            for item in P.q[name]:
                kind = item[0]
                if kind == "wait":
                    e.wait_ge(sems[item[1]], item[2])
                elif kind == "dma":
                    item[1](e).then_inc(sems[item[2]], item[3])
                elif kind == "sig":
                    item[1](e).then_inc(sems[item[2]], 1)
                else:
                    item[1](e)

        @block.tensor
        def _(e):
            run(e, "pe")

        @block.scalar
        def _(e):
            run(e, "act")

        @block.vector
        def _(e):
            run(e, "dve")

        @block.gpsimd
        def _(e):
            run(e, "pool")

        @block.sync
        def _(e):
            run(e, "sp")
    return nc


def _fm(a, n):
    return np.ascontiguousarray(a.T.reshape(KC, 128, n).transpose(1, 0, 2).reshape(128, KC * n))


def _colvec(v):
    return v.reshape(KC, 128).T


DEBUG = False
STOP = 4
_LAST = {}


def kernel(x, norm_mix, norm_ffn, pool_w, pool_scale, w_qkv, lambda_q1, lambda_k1, lambda_q2, lambda_k2,
           subln_g, w_o, w_gate, w_up, w_down, final_norm):
    inp = dict(x=x, norm_mix=norm_mix, norm_ffn=norm_ffn, pool_w=pool_w, pool_scale=pool_scale, w_qkv=w_qkv,
               w_o=w_o, w_gate=w_gate, w_up=w_up, w_down=w_down)
    inp = {k: np.asarray(v, dtype=np.float32) for k, v in inp.items()}
    x = inp["x"]
    wpack = pack_weights(inp)
    cols = np.zeros((128, NCOLS), np.float32)
    cols[:, C_NMIX0:C_NMIX0 + 16] = _colvec(inp["norm_mix"][0])
    cols[:, C_NFFN0:C_NFFN0 + 16] = _colvec(inp["norm_ffn"][0])
    cols[:, C_PSCALE:C_PSCALE + 16] = _colvec(inp["pool_scale"][0])
    cols[:, C_NMIX1:C_NMIX1 + 16] = _colvec(inp["norm_mix"][1])
    cols[:, C_NFFN1:C_NFFN1 + 16] = _colvec(inp["norm_ffn"][1])
    cols[:, C_FINAL:C_FINAL + 16] = _colvec(np.asarray(final_norm, np.float32))
    lamw = np.concatenate([np.asarray(v, np.float32).reshape(1, 128) for v in (lambda_q1, lambda_k1, lambda_q2, lambda_k2)], axis=1)
    lamw = np.ascontiguousarray(np.broadcast_to(lamw, (128, 512)))
    subg = np.ascontiguousarray(np.broadcast_to(np.asarray(subln_g, np.float32).reshape(1, 256), (128, 256)))
    tri = (np.arange(128)[None, :] >= np.arange(128)[:, None]).astype(ml_dtypes.bfloat16)
    ident = np.eye(128).astype(ml_dtypes.bfloat16)

    in_maps = []
    for c in range(NCORES):
        b, r = c // 4, c % 4
        xs = x[b, r * T:(r + 1) * T, :]
        if r == 0:
            xh = np.zeros((HALO, D), np.float32)
        else:
            xh = x[b, r * T - HALO:r * T, :]
        inv16 = np.zeros((128, 64), np.float32)
        for g in range(4):
            w = 2 << g
            tpos = r * T + np.arange(16)
            inv16[:, g * 16:(g + 1) * 16] = (1.0 / np.minimum(tpos + 1, w)).astype(np.float32)[None, :]
        in_maps.append({
            "xT": _fm(xs, T), "xh": _fm(xh, HALO), "cols": cols, "bias": make_bias(r), "inv16": inv16,
            "tri": tri, "ident": ident, "lamw": lamw, "subg": subg, "w": wpack,
        })
    nc = build_program(debug=DEBUG, stop=STOP)
    for m in in_maps:
        m["w"] = wpack[:NEED_TILES[STOP]]
    res = run_bass_kernel_spmd(nc, in_maps, core_ids=list(range(NCORES)))
    out = np.zeros((2, 4096, D), np.float32)
    for c in range(NCORES):
        b, r = c // 4, c % 4
        o = np.asarray(res.results[c]["out"]).reshape(128, KC, T).transpose(2, 1, 0).reshape(T, D)
        out[b, r * T:(r + 1) * T, :] = o
    if DEBUG:
        _LAST["dbg"] = [np.asarray(res.results[c]["dbg"]) for c in range(NCORES)]
    return out
```

```python
import math
import numpy as np
import ml_dtypes
import concourse.bass as bass
import concourse.mybir as mybir
from concourse.bass_utils import run_bass_kernel_spmd

F32 = mybir.dt.float32
BF16 = mybir.dt.bfloat16
AF = mybir.ActivationFunctionType
ALU = mybir.AluOpType

NCORES = 8
D = 2048
T = 1024
HALO = 16
TE = T + HALO
KC = 16
FF = 5632
FC = 44
NFB = 4
FCB = FC // NFB
NH = 8
WS = 8192
NSLOT = 3
EPS = 1e-6
LAMBDA_INIT = 0.8 - 0.6 * math.exp(-0.3 * 1)
SCALE = 128 ** -0.5
SLOPES = [2.0 ** (-8.0 * (i + 1) / NH) for i in range(NH)]
QBH = [256, 512, 512, 512, 512, 512, 512, 512]
NEG = -30000.0
GROUPS = [[0, 1, 2, 3], [4, 5, 6, 7]]

C_NMIX0, C_NFFN0, C_PSCALE, C_NMIX1, C_NFFN1, C_FINAL = [i * 16 for i in range(6)]
NCOLS = 96


def weight_tiles():
    tiles = [("pool", (), 8192)]

    def ffn(l):
        out = []
        for blk in range(NFB):
            j0 = blk * FCB
            jl = 0
            while jl < FCB:
                nj = min(2, FCB - jl)
                out.append(("gu", (l, j0 + jl, nj), nj * 4096))
                jl += nj
            for dg in range(4):
                out.append(("down", (l, blk, dg), FCB * 512))
        return out

    tiles += ffn(0)
    for t in range(4):
        tiles.append(("k", (t,), 8192))
    for t in range(4):
        tiles.append(("v", (t,), 8192))
    for t in range(4):
        tiles.append(("q", (t,), 8192))
    for t in range(4):
        tiles.append(("o", (t,), 8192))
    tiles += ffn(1)
    return tiles


def pack_weights(inp):
    tiles = weight_tiles()
    w = np.zeros((len(tiles), 128, WS), np.float32)
    wqkv = inp["w_qkv"][0]
    for i, (kind, a, n) in enumerate(tiles):
        if kind == "pool":
            pw = inp["pool_w"][0].reshape(4, 4, 128, 4, 128).transpose(2, 0, 1, 3, 4)
            w[i, :, :n] = pw.reshape(128, n)
        elif kind == "gu":
            l, j0, nj = a
            for jj in range(nj):
                for gu, W in enumerate((inp["w_gate"][l], inp["w_up"][l])):
                    sub = W[:, (j0 + jj) * 128:(j0 + jj + 1) * 128].reshape(16, 128, 128).transpose(1, 0, 2)
                    o = (jj * 2 + gu) * 2048
                    w[i, :, o:o + 2048] = sub.reshape(128, 2048)
        elif kind == "down":
            l, blk, dg = a
            sub = inp["w_down"][l][blk * FCB * 128:(blk + 1) * FCB * 128, dg * 512:(dg + 1) * 512]
            sub = sub.reshape(FCB, 128, 4, 128).transpose(1, 0, 2, 3)
            w[i, :, :n] = sub.reshape(128, n)
        elif kind in ("k", "q", "o"):
            t = a[0]
            if kind == "q":
                W = wqkv[:, 0:2048]
            elif kind == "k":
                W = wqkv[:, 2048:4096]
            else:
                W = inp["w_o"][0]
            sub = W[:, t * 512:(t + 1) * 512].reshape(16, 128, 4, 128).transpose(1, 2, 0, 3)
            w[i, :, :n] = sub.reshape(128, n)
        elif kind == "v":
            t = a[0]
            sub = wqkv[:, 4096 + t * 512:4096 + (t + 1) * 512].reshape(16, 128, 512).transpose(1, 0, 2)
            w[i, :, :n] = sub.reshape(128, n)
    return w


def bias_index():
    idx = {}
    n = 0
    for h in range(NH):
        ns = 512 // QBH[h]
        for blk in range(4):
            for qb in range(2):
                for kt in range(8):
                    for sub in range(ns):
                        idx[(h, blk, qb, kt, sub)] = n
                        n += 1
    return idx, n


BIDX, NBIAS = bias_index()


def make_bias(r):
    b = np.zeros((128, NBIAS), np.float32)
    j = np.arange(128, dtype=np.float64)
    for (h, blk, qb, kt, sub), col in BIDX.items():
        s = r if blk == 3 else blk
        if blk != 3 and s >= r:
            b[:, col] = NEG
            continue
        kpos = s * 1024 + kt * 128 + j
        qref = r * 1024 + qb * 512 + sub * QBH[h] + QBH[h] // 2
        b[:, col] = (SLOPES[h] * (kpos - qref)).astype(np.float32)
    return b


class Tok:
    __slots__ = ("name", "w", "r")

    def __init__(self, name):
        self.name = name
        self.w = None
        self.r = {}


class Prog:
    ENG = ("pe", "act", "dve", "pool", "sp")
    LIMIT = 240

    def __init__(self):
        self.q = {e: [] for e in self.ENG}
        self.cnt = {}
        self.known = {e: {} for e in self.ENG}
        self.semkeys = []
        self.epoch = {}
        self.keysrc = {}
        self.basekeys = {}

    def _key(self, base, inc, src):
        if base not in self.epoch:
            self.epoch[base] = 0
            self.basekeys[base] = []
            k = f"{base}.0"
            self.cnt[k] = 0
            self.semkeys.append(k)
            self.keysrc[k] = src
            self.basekeys[base].append(k)
        k = f"{base}.{self.epoch[base]}"
        if self.cnt[k] + inc > self.LIMIT:
            self.epoch[base] += 1
            k = f"{base}.{self.epoch[base]}"
            self.cnt[k] = 0
            self.semkeys.append(k)
            self.keysrc[k] = src
            self.basekeys[base].append(k)
        return k

    def total(self, base):
        return sum(self.cnt[k] for k in self.basekeys.get(base, []))

    def op(self, eng, fn, reads=(), writes=(), signal=True, dma=None, inc=16):
        src = "dma" if dma else eng
        need = {}

        def consider(ev, kind):
            key, val, s = ev
            if s == src and s != "dma":
                if s == "pe" or kind != "raw":
                    return
            if self.known[eng].get(key, 0) >= val:
                return
            if need.get(key, 0) < val:
                need[key] = val

        for t in reads:
            if t.w is not None:
                consider(t.w, "raw")
        for t in writes:
            if t.w is not None:
                consider(t.w, "waw")
            for key, (val, s) in t.r.items():
                consider((key, val, s), "war")
        for key, val in need.items():
            self.known[eng][key] = val
            self.q[eng].append(("wait", key, val))
        if dma:
            key = self._key(dma, inc, "dma")
            self.cnt[key] += inc
            ev = (key, self.cnt[key], "dma")
            self.q[eng].append(("dma", fn, key, inc))
        else:
            key = self._key(eng, 1, eng)
            if signal:
                self.cnt[key] += 1
                ev = (key, self.cnt[key], eng)
                self.q[eng].append(("sig", fn, key))
            else:
                ev = (key, self.cnt[key] + 1, eng)
                self.q[eng].append(("nosig", fn))
        for t in reads:
            old = t.r.get(ev[0])
            if old is None or old[0] < ev[1]:
                t.r[ev[0]] = (ev[1], ev[2])
        for t in writes:
            t.w = ev
            t.r = {}
        return ev

    def last_event(self, base):
        k = self.basekeys[base][-1]
        return (k, self.cnt[k], self.keysrc[k])

    def barrier(self):
        for e in self.ENG:
            for base, keys in self.basekeys.items():
                src = self.keysrc[keys[0]]
                if src == "dma":
                    wk = keys
                else:
                    if src == "pe" and e == "pe":
                        continue
                    wk = keys[-1:]
                    for k in keys[:-1]:
                        self.known[e][k] = self.cnt[k]
                for key in wk:
                    val = self.cnt[key]
                    if val and self.known[e].get(key, 0) < val:
                        self.known[e][key] = val
                        self.q[e].append(("wait", key, val))

    def final_wait(self, eng, bases):
        for base in bases:
            for key in self.basekeys.get(base, []):
                val = self.cnt[key]
                if val and self.known[eng].get(key, 0) < val:
                    self.known[eng][key] = val
                    self.q[eng].append(("wait", key, val))


NEED_TILES = {1: 1, 2: 41, 3: 57, 4: 97}


def build_program(debug=False, stop=4):
    nc = bass.Bass("TRN2", target_bir_lowering=False)
    P = Prog()
    tiles = weight_tiles()
    NT = min(len(tiles), NEED_TILES[stop])

    xT_d = nc.dram_tensor("xT", [128, KC * T], F32, kind="ExternalInput").ap()
    xh_d = nc.dram_tensor("xh", [128, KC * HALO], F32, kind="ExternalInput").ap()
    cols_d = nc.dram_tensor("cols", [128, NCOLS], F32, kind="ExternalInput").ap()
    bias_d = nc.dram_tensor("bias", [128, NBIAS], F32, kind="ExternalInput").ap()
    inv16_d = nc.dram_tensor("inv16", [128, 64], F32, kind="ExternalInput").ap()
    tri_d = nc.dram_tensor("tri", [128, 128], BF16, kind="ExternalInput").ap()
    ident_d = nc.dram_tensor("ident", [128, 128], BF16, kind="ExternalInput").ap()
    lamw_d = nc.dram_tensor("lamw", [128, 512], F32, kind="ExternalInput").ap()
    subg_d = nc.dram_tensor("subg", [128, 256], F32, kind="ExternalInput").ap()
    w_d = nc.dram_tensor("w", [NT, 128, WS], F32, kind="ExternalInput").ap()
    out_d = nc.dram_tensor("out", [128, KC * T], F32, kind="ExternalOutput").ap()
    ccik = [nc.dram_tensor(f"ccik{h}", [256, 1024], BF16, kind="Internal", addr_space="Local").ap() for h in range(NH)]
    ccok = [nc.dram_tensor(f"ccok{h}", [4 * 256, 1024], BF16, kind="Internal", addr_space="Local").ap() for h in range(NH)]
    cciv = [nc.dram_tensor(f"cciv{h}", [1024, 256], BF16, kind="Internal", addr_space="Local").ap() for h in range(NH)]
    ccov = [nc.dram_tensor(f"ccov{h}", [4 * 1024, 256], BF16, kind="Internal", addr_space="Local").ap() for h in range(NH)]
    qT_d = nc.dram_tensor("qT_s", [2048, 1024], BF16, kind="Internal", addr_space="Local").ap()
    dbg_d = None
    if debug:
        dbg_d = nc.dram_tensor("dbg", [4, 128, KC * T], F32, kind="ExternalOutput").ap()

    cur = [16512]

    def alloc(name, n, dt, at=None):
        sz = n * (4 if dt == F32 else 2)
        sz = (sz + 31) // 32 * 32
        if at is None:
            off = cur[0]
            cur[0] += sz
        else:
            off = at
        assert off + sz <= 229344, (name, off, sz)
        return nc.alloc_sbuf_tensor_at(name, [128, n], dt, offset=off).ap(), off + sz

    H, _ = alloc("H", KC * T, F32)
    HH, _ = alloc("HH", KC * HALO, F32)
    A, _ = alloc("A", KC * TE, BF16)
    WSL, _ = alloc("WSL", NSLOT * WS, BF16)
    COLS, _ = alloc("COLS", NCOLS, F32)
    BIAS, _ = alloc("BIAS", NBIAS, F32)
    INV16, _ = alloc("INV16", 64, F32)
    TRI, _ = alloc("TRI", 128, BF16)
    IDENT, _ = alloc("IDENT", 128, BF16)
    ONES, _ = alloc("ONES", 128, F32)
    LAMW, _ = alloc("LAMW", 512, F32)
    SUBG, _ = alloc("SUBG", 256, F32)
    SM, _ = alloc("SM", 32, F32)
    EPSC, _ = alloc("EPSC", 8, F32)
    RSTD, _ = alloc("RSTD", TE, F32)
    ONESB, _ = alloc("ONESB", 128, BF16)
    SQ = []
    for b in range(4):
        t_, _ = alloc(f"SQ{b}", 528, BF16)
        SQ.append(t_)
    S0 = cur[0]

    OFF = {}

    def layout(base, specs):
        res = {}
        off = base
        for name, n, dt in specs:
            OFF[name] = off
            res[name], off = alloc(name, n, dt, at=off)
        return res

    LP = layout(S0, [("SQH", 256, BF16), ("T16a", 16, F32), ("T16b", 16, F32)] +
                [(f"{n}{ab}", TE, F32) for ab in "ab" for n in ("E", "S2", "S4", "S8", "S16")])
    LF = layout(S0, [("ACTB", FCB * T, BF16), ("SG0", 512, F32), ("SG1", 512, F32)])
    LA = layout(S0, [("STG0", 1024, BF16), ("STG1", 1024, BF16)] +
                [(f"VB{i}", 8 * 257, BF16) for i in range(4)] +
                [(f"KB{i}", 1024, BF16) for i in range(4)] +
                [("QB0", 512, BF16), ("QB1", 512, BF16)] +
                [(f"P{i}", 512, BF16) for i in range(4)] +
                [("OC", 4 * 257, F32), ("OC2", 4 * 257, F32), ("OT", 256, F32)] +
                [(f"ON{i}", 256, BF16) for i in range(4)] +
                [("SS", 16, F32)])
    LA.update(layout(OFF["OC"], [("STG2", 1024, BF16), ("STG3", 1024, BF16)]))
    LA.update(layout(OFF["OC2"], [("STG4", 1024, BF16), ("STG5", 1024, BF16)]))
    assert OFF["STG3"] + 2048 <= OFF["OC2"] and OFF["STG5"] + 2048 <= OFF["OT"]
    LO = layout(S0, [("OUT0", T, F32), ("OUT1", T, F32)])

    Ht = [[Tok(f"H{k}_{hf}") for hf in range(2)] for k in range(KC)]
    HHt = Tok("HH")
    At = [[Tok(f"A{k}_{hf}") for hf in range(2)] for k in range(KC)]
    Aht = [Tok(f"Ah{k}") for k in range(KC)]
    Wt = [Tok(f"W{s}") for s in range(NSLOT)]
    CONSTt = Tok("const")
    ONESt = Tok("ones")
    RSTDt = Tok("rstd")
    SQt = [Tok(f"sq{i}") for i in range(4)]
    PSt = [Tok(f"ps{i}") for i in range(8)]
    SMt = Tok("sm")

    def Hk(k, t0=0, t1=T):
        return H[:, k * T + t0:k * T + t1]

    def Ak(k, t0=0, t1=T):
        return A[:, k * TE + HALO + t0:k * TE + HALO + t1]

    def col(c):
        return COLS[:, c:c + 1]

    def hslice(hf):
        return (hf * 512, (hf + 1) * 512)

    import contextlib
    es = contextlib.ExitStack()
    with es:
        PS = [es.enter_context(nc.psum_tensor(f"psb{i}", [128, 512], F32)) for i in range(8)]
        PSa = [p[:] for p in PS]
        bank_rr = [0]

        def nb():
            b = bank_rr[0]
            bank_rr[0] = (b + 1) % 8
            return b

        wstate = {"next_load": 0}

        def w_load(i, after=()):
            if i >= NT:
                return
            s = i % NSLOT
            n = tiles[i][2]
            P.op("pool", lambda e, i=i, s=s, n=n: e.dma_start(out=WSL[:, s * WS:s * WS + n], in_=w_d[i, :, 0:n]),
                 reads=list(after), writes=[Wt[s]], dma=f"w{s}")

        def w_consume_done(i):
            w_load(i + NSLOT)

        def wslot(i, off, n=128):
            s = i % NSLOT
            return WSL[:, s * WS + off:s * WS + off + n]

        for dst, srcd in ((COLS, cols_d), (BIAS, bias_d), (INV16, inv16_d), (TRI, tri_d), (IDENT, ident_d),
                          (LAMW, lamw_d), (SUBG, subg_d)):
            P.op("sp", lambda e, dst=dst, srcd=srcd: e.dma_start(out=dst[:, :], in_=srcd[:, :]), dma="cst")
        P.op("sp", lambda e: e.dma_start(out=HH[:, :], in_=xh_d[:, :]), writes=[HHt], dma="xh")
        for g in range(4):
            P.op("sp", lambda e, g=g: e.dma_start(out=H[:, g * 4 * T:(g + 1) * 4 * T], in_=xT_d[:, g * 4 * T:(g + 1) * 4 * T]),
                 writes=[Ht[k][hf] for k in range(g * 4, g * 4 + 4) for hf in range(2)], dma=f"x{g}")
        CONSTt.w = P.last_event("cst")
        P.op("dve", lambda e: e.memset(ONES[:, :], 1.0), writes=[ONESt])
        P.op("dve", lambda e: e.memset(EPSC[:, :], EPS), writes=[ONESt])
        P.op("dve", lambda e: e.memset(ONESB[:, :], 1.0), writes=[ONESt])
        for i in range(NSLOT):
            w_load(i, after=[Ht[11][1]] if i == 0 else [Ht[15][1]])
        wi = [0]

        def norm_stats(with_halo):
            banks = [nb(), nb()]
            n = 0
            for k in range(KC):
                for hf in range(2):
                    b = n % 4
                    n += 1
                    t0, t1 = hslice(hf)
                    P.op("act", lambda e, b=b, k=k, t0=t0, t1=t1: e.activation(out=SQ[b][:, 0:512], in_=Hk(k, t0, t1), func=AF.Square),
                         reads=[Ht[k][hf]], writes=[SQt[b]])
                    P.op("pe", lambda e, b=b, bk=banks[hf], k=k: e.matmul(PSa[bk][:, :], lhsT=ONESB[:, :], rhs=SQ[b][:, 0:512],
                                                                           start=(k == 0), stop=(k == KC - 1)),
                         reads=[SQt[b], ONESt], writes=[PSt[banks[hf]]], signal=True)
            hb = None
            if with_halo:
                hb = nb()
                SQH = LP["SQH"]
                sqht = Tok("sqh")
                P.op("act", lambda e: e.activation(out=SQH[:, :], in_=HH[:, :], func=AF.Square), reads=[HHt], writes=[sqht])
                for k in range(KC):
                    P.op("pe", lambda e, k=k: e.matmul(PSa[hb][:, 0:HALO], lhsT=ONESB[:, :], rhs=SQH[:, k * HALO:(k + 1) * HALO],
                                                        start=(k == 0), stop=(k == KC - 1)),
                         reads=[sqht, ONESt], writes=[PSt[hb]], signal=(k == KC - 1))
            for hf in range(2):
                t0, t1 = hslice(hf)
                P.op("act", lambda e, bk=banks[hf], t0=t0, t1=t1: e.activation(out=RSTD[:, HALO + t0:HALO + t1], in_=PSa[bk][:, :],
                                                                              func=AF.Sqrt, bias=EPSC[:, 0:1], scale=1.0 / D),
                     reads=[PSt[banks[hf]], ONESt], writes=[RSTDt])
            if with_halo:
                P.op("act", lambda e: e.activation(out=RSTD[:, 0:HALO], in_=PSa[hb][:, 0:HALO], func=AF.Sqrt, bias=EPSC[:, 0:1], scale=1.0 / D),
                     reads=[PSt[hb], ONESt], writes=[RSTDt])
            lo = 0 if with_halo else HALO
            P.op("dve", lambda e, lo=lo: e.reciprocal(out=RSTD[:, lo:TE], in_=RSTD[:, lo:TE]),
                 reads=[RSTDt], writes=[RSTDt])

        def norm_apply(cbase):
            for k in range(KC):
                P.op("dve", lambda e, k=k: e.scalar_tensor_tensor(out=Ak(k), in0=Hk(k), scalar=col(cbase + k), in1=RSTD[:, HALO:TE],
                                                                  op0=ALU.mult, op1=ALU.mult),
                     reads=[Ht[k][0], Ht[k][1], RSTDt, CONSTt], writes=[At[k][0], At[k][1]])

        def dump(i):
            if debug:
                P.barrier()
                P.op("sp", lambda e, i=i: e.dma_start(out=dbg_d[i], in_=H[:, :]),
                     reads=[Ht[k][hf] for k in range(KC) for hf in range(2)], dma="dbg")

        def pool_mixer():
            norm_stats(True)
            sets = []
            for ab in "ab":
                sets.append(dict(E=LP["E" + ab], S=[LP["S2" + ab], LP["S4" + ab], LP["S8" + ab], LP["S16" + ab]], T16=LP["T16" + ab],
                                 Et=Tok("E" + ab), St=[Tok(f"S{i}{ab}") for i in range(4)], T16t=Tok("T16" + ab)))

            def chunk_ops(k, B):
                g = k // 4
                w = 2 << g
                E, Et = B["E"], B["Et"]
                yield lambda: P.op("dve", lambda e: e.scalar_tensor_tensor(out=E[:, 0:HALO], in0=HH[:, k * HALO:(k + 1) * HALO], scalar=col(C_NMIX0 + k),
                                                                         in1=RSTD[:, 0:HALO], op0=ALU.mult, op1=ALU.mult),
                                   reads=[HHt, RSTDt, CONSTt], writes=[Et])
                yield lambda: P.op("dve", lambda e: e.scalar_tensor_tensor(out=E[:, HALO:TE], in0=Hk(k), scalar=col(C_NMIX0 + k),
                                                                         in1=RSTD[:, HALO:TE], op0=ALU.mult, op1=ALU.mult),
                                   reads=[Ht[k][0], Ht[k][1], RSTDt, CONSTt], writes=[Et])
                prev, prevt = E, Et
                lo = 0
                for step in range(g + 1):
                    sh = 1 << step
                    lo2 = lo + sh
                    dst, dstt = B["S"][step], B["St"][step]
                    yield lambda dst=dst, dstt=dstt, prev=prev, prevt=prevt, lo2=lo2, sh=sh: P.op(
                        "dve", lambda e: e.tensor_tensor(out=dst[:, lo2:TE], in0=prev[:, lo2:TE], in1=prev[:, lo2 - sh:TE - sh], op=ALU.add),
                        reads=[prevt], writes=[dstt])
                    prev, prevt, lo = dst, dstt, lo2
                S, Stok = prev, prevt
                yield lambda: P.op("dve", lambda e: e.scalar_tensor_tensor(out=Ak(k), in0=S[:, HALO:TE], scalar=1.0 / w, in1=E[:, HALO:TE],
                                                                         op0=ALU.mult, op1=ALU.subtract),
                                   reads=[Stok, Et], writes=[At[k][0], At[k][1]])
                yield lambda: P.op("dve", lambda e: e.tensor_tensor(out=B["T16"][:, :], in0=S[:, HALO:2 * HALO], in1=INV16[:, g * 16:(g + 1) * 16], op=ALU.mult),
                                   reads=[Stok, CONSTt], writes=[B["T16t"]])
                yield lambda: P.op("dve", lambda e: e.tensor_tensor(out=Ak(k, 0, HALO), in0=B["T16"][:, :], in1=E[:, HALO:2 * HALO], op=ALU.subtract),
                                   reads=[B["T16t"], Et, At[k][0]], writes=[At[k][0]])

            for k in range(0, KC, 2):
                ga, gb = chunk_ops(k, sets[0]), chunk_ops(k + 1, sets[1])
                while True:
                    fa, fb = next(ga, None), next(gb, None)
                    if fa is None and fb is None:
                        break
                    if fa is not None:
                        fa()
                    if fb is not None:
                        fb()
            ti = wi[0]
            for g in range(4):
                for oc in range(4):
                    c = g * 4 + oc
                    for hf in range(2):
                        t0, t1 = hslice(hf)
                        bk = nb()
                        for kk in range(4):
                            off = ((g * 4 + kk) * 4 + oc) * 128
                            last = (g == 3 and oc == 3 and hf == 1 and kk == 3)
                            P.op("pe", lambda e, bk=bk, off=off, kk=kk, g=g, t0=t0, t1=t1: e.matmul(
                                PSa[bk][:, :], lhsT=wslot(ti, off), rhs=Ak(g * 4 + kk, t0, t1), start=(kk == 0), stop=(kk == 3)),
                                 reads=[Wt[ti % NSLOT], At[g * 4 + kk][hf]], writes=[PSt[bk]], signal=(kk == 3))
                        P.op("dve", lambda e, bk=bk, c=c, t0=t0, t1=t1: e.scalar_tensor_tensor(
                            out=Hk(c, t0, t1), in0=PSa[bk][:, :], scalar=col(C_PSCALE + c), in1=Hk(c, t0, t1), op0=ALU.mult, op1=ALU.add),
                             reads=[PSt[bk], Ht[c][hf], CONSTt], writes=[Ht[c][hf]])
            w_consume_done(ti)
            wi[0] += 1

        def ffn(cnorm):
            norm_stats(False)
            norm_apply(cnorm)
            ACTB = LF["ACTB"]
            SG = [LF["SG0"], LF["SG1"]]
            SGt = [Tok("sg0"), Tok("sg1")]
            ACTt = [[Tok(f"act{j}_{hf}") for hf in range(2)] for j in range(FCB)]
            sgn = [0]
            for blk in range(NFB):
                jl = 0
                while jl < FCB:
                    nj = min(2, FCB - jl)
                    ti = wi[0]
                    for jj in range(nj):
                        j = jl + jj
                        banks = [nb() for _ in range(4)]
                        for k in range(KC):
                            for gu in range(2):
                                off = ((jj * 2 + gu) * 16 + k) * 128
                                for hf in range(2):
                                    t0, t1 = hslice(hf)
                                    bk = banks[gu * 2 + hf]
                                    P.op("pe", lambda e, bk=bk, off=off, k=k, t0=t0, t1=t1, ti=ti: e.matmul(
                                        PSa[bk][:, :], lhsT=wslot(ti, off), rhs=Ak(k, t0, t1), start=(k == 0), stop=(k == KC - 1)),
                                         reads=[Wt[ti % NSLOT], At[k][hf]], writes=[PSt[bk]], signal=(k == KC - 1))
                        for hf in range(2):
                            t0, t1 = hslice(hf)
                            b = sgn[0] % 2
                            sgn[0] += 1
                            P.op("act", lambda e, b=b, bk=banks[hf]: e.activation(out=SG[b][:, :], in_=PSa[bk][:, :], func=AF.Silu),
                                 reads=[PSt[banks[hf]]], writes=[SGt[b]])
                            P.op("dve", lambda e, b=b, bk=banks[2 + hf], j=j, t0=t0, t1=t1: e.tensor_tensor(
                                out=ACTB[:, j * T + t0:j * T + t1], in0=SG[b][:, :], in1=PSa[bk][:, :], op=ALU.mult),
                                 reads=[SGt[b], PSt[banks[2 + hf]]], writes=[ACTt[j][hf]])
                    w_consume_done(ti)
                    wi[0] += 1
                    jl += nj
                for dg in range(4):
                    ti = wi[0]
                    for dd in range(4):
                        dc = dg * 4 + dd
                        for hf in range(2):
                            t0, t1 = hslice(hf)
                            bk = nb()
                            for fc in range(FCB):
                                off = (fc * 4 + dd) * 128
                                P.op("pe", lambda e, bk=bk, off=off, fc=fc, t0=t0, t1=t1, ti=ti: e.matmul(
                                    PSa[bk][:, :], lhsT=wslot(ti, off), rhs=ACTB[:, fc * T + t0:fc * T + t1], start=(fc == 0), stop=(fc == FCB - 1)),
                                     reads=[Wt[ti % NSLOT], ACTt[fc][hf]], writes=[PSt[bk]], signal=(fc == FCB - 1))
                            P.op("dve", lambda e, bk=bk, dc=dc, t0=t0, t1=t1: e.tensor_tensor(
                                out=Hk(dc, t0, t1), in0=Hk(dc, t0, t1), in1=PSa[bk][:, :], op=ALU.add),
                                 reads=[PSt[bk], Ht[dc][hf]], writes=[Ht[dc][hf]])
                    w_consume_done(ti)
                    wi[0] += 1

        def attention():
            norm_stats(False)
            norm_apply(C_NMIX1)
            NSTG = 6
            STG = [LA[f"STG{i}"] for i in range(NSTG)]
            STGt = [Tok(f"stg{i}") for i in range(NSTG)]
            VB = [LA[f"VB{i}"] for i in range(4)]
            VBt = [Tok(f"vb{i}") for i in range(4)]
            KB = [LA[f"KB{i}"] for i in range(4)]
            KBt = [Tok(f"kb{i}") for i in range(4)]
            QB = [LA["QB0"], LA["QB1"]]
            QBt = [Tok("qb0"), Tok("qb1")]
            PT = [LA[f"P{i}"] for i in range(4)]
            PTt = [Tok(f"p{i}") for i in range(4)]
            OC, OT, SS = LA["OC"], LA["OT"], LA["SS"]
            JUNK = OT
            OCt = [Tok(f"oc{i}") for i in range(4)]
            OTt, SSt = Tok("ot"), Tok("ss")
            JUNKt = OTt
            Kd = [Tok(f"kd{c}") for c in range(KC)]
            Vd = [[Tok(f"vd{tt}_{h}") for h in range(NH)] for tt in range(8)]
            Qd = [Tok(f"qd{c}") for c in range(KC)]
            CCKt = [Tok(f"cck{h}") for h in range(NH)]
            CCVt = [Tok(f"ccv{h}") for h in range(NH)]
            stn = [0]

            LAM = SM[:, 0:1]
            P.op("dve", lambda e: e.scalar_tensor_tensor(out=JUNK[:, 0:128], in0=LAMW[:, 0:128], scalar=1.0, in1=LAMW[:, 128:256],
                                                         op0=ALU.mult, op1=ALU.mult, accum_out=SM[:, 1:2]),
                 reads=[CONSTt], writes=[JUNKt, SMt])
            P.op("dve", lambda e: e.scalar_tensor_tensor(out=JUNK[:, 128:256], in0=LAMW[:, 256:384], scalar=1.0, in1=LAMW[:, 384:512],
                                                         op0=ALU.mult, op1=ALU.mult, accum_out=SM[:, 2:3]),
                 reads=[CONSTt, SMt], writes=[JUNKt, SMt])
            P.op("act", lambda e: e.activation(out=SM[:, 3:5], in_=SM[:, 1:3], func=AF.Exp), reads=[SMt], writes=[SMt])
            P.op("dve", lambda e: e.tensor_tensor(out=SM[:, 5:6], in0=SM[:, 3:4], in1=SM[:, 4:5], op=ALU.subtract), reads=[SMt], writes=[SMt])
            P.op("dve", lambda e: e.tensor_scalar(out=LAM, in0=SM[:, 5:6], scalar1=float(LAMBDA_INIT), scalar2=None, op0=ALU.add),
                 reads=[SMt], writes=[SMt])
            P.op("dve", lambda e: e.tensor_scalar(out=SUBG[:, :], in0=SUBG[:, :], scalar1=float(1.0 - LAMBDA_INIT), scalar2=None, op0=ALU.mult),
                 reads=[CONSTt], writes=[CONSTt])

            for i in range(4):
                P.op("dve", lambda e, i=i: e.memset(VB[i].rearrange("p (k e) -> p k e", e=257)[:, :, 256:257], 1.0), writes=[VBt[i]])

            def evac_copy(n, out_ap, in_ap, reads, writes):
                if n % 2 == 0:
                    P.op("act", lambda e: e.activation(out=out_ap, in_=in_ap, func=AF.Copy), reads=reads, writes=writes)
                else:
                    P.op("dve", lambda e: e.tensor_copy(out=out_ap, in_=in_ap), reads=reads, writes=writes)

            def proj_fm(dst_rows, dtoks, after_tile=None):
                for t in range(4):
                    ti = wi[0]
                    for ocl in range(4):
                        oc = t * 4 + ocl
                        b = stn[0] % NSTG
                        stn[0] += 1
                        for hf in range(2):
                            t0, t1 = hslice(hf)
                            bk = nb()
                            for k in range(KC):
                                off = (ocl * 16 + k) * 128
                                P.op("pe", lambda e, bk=bk, off=off, k=k, t0=t0, t1=t1, ti=ti: e.matmul(
                                    PSa[bk][:, :], lhsT=wslot(ti, off), rhs=Ak(k, t0, t1), start=(k == 0), stop=(k == KC - 1)),
                                     reads=[Wt[ti % NSLOT], At[k][hf]], writes=[PSt[bk]], signal=(k == KC - 1))
                            evac_copy(hf, STG[b][:, t0:t1], PSa[bk][:, :], [PSt[bk]], [STGt[b]])
                        P.op("sp", lambda e, b=b, oc=oc: e.dma_start(out=dst_rows(oc), in_=STG[b][:, :]),
                             reads=[STGt[b]], writes=[dtoks[oc]], dma=f"st{b}")
                    w_consume_done(ti)
                    wi[0] += 1
                    if after_tile is not None:
                        after_tile(t)

            def gather_k(t):
                for h in (2 * t, 2 * t + 1):
                    P.op("pool", lambda e, h=h: e.collective_compute("AllGather", ALU.bypass, replica_groups=GROUPS, ins=[ccik[h]], outs=[ccok[h]]),
                         reads=[Kd[2 * h], Kd[2 * h + 1]], writes=[CCKt[h]], dma=f"cck{h}", inc=1)

            def gather_v(cg):
                for h in (2 * cg, 2 * cg + 1):
                    P.op("pool", lambda e, h=h: e.collective_compute("AllGather", ALU.bypass, replica_groups=GROUPS, ins=[cciv[h]], outs=[ccov[h]]),
                         reads=[Vd[tt][h] for tt in range(8)], writes=[CCVt[h]], dma=f"ccv{h}", inc=1)

            proj_fm(lambda oc: ccik[oc // 2][(oc % 2) * 128:(oc % 2 + 1) * 128, :], Kd, after_tile=gather_k)
            for cg in range(4):
                ti = wi[0]
                for tt in range(8):
                    b = stn[0] % NSTG
                    stn[0] += 1
                    bk = nb()
                    for k in range(KC):
                        P.op("pe", lambda e, bk=bk, k=k, tt=tt, ti=ti: e.matmul(
                            PSa[bk][:, :], lhsT=Ak(k, tt * 128, (tt + 1) * 128), rhs=wslot(ti, k * 512, 512), start=(k == 0), stop=(k == KC - 1)),
                             reads=[Wt[ti % NSLOT], At[k][tt // 4]], writes=[PSt[bk]], signal=(k == KC - 1))
                    evac_copy(tt, STG[b][:, 0:512], PSa[bk][:, :], [PSt[bk]], [STGt[b]])
                    for hh in range(2):
                        h = 2 * cg + hh
                        P.op("sp", lambda e, b=b, tt=tt, h=h, hh=hh: e.dma_start(out=cciv[h][tt * 128:(tt + 1) * 128, :], in_=STG[b][:, hh * 256:(hh + 1) * 256]),
                             reads=[STGt[b]], writes=[Vd[tt][h]], dma=f"st{b}")
                w_consume_done(ti)
                wi[0] += 1
                gather_v(cg)
            proj_fm(lambda oc: qT_d[oc * 128:(oc + 1) * 128, :], Qd)

            ACC = [0, 1, 2, 3]
            SCB = [4, 5, 6, 7]
            scn = [0]
            ptn = [0]
            qn = [0]
            onn = [0]
            LOOK = 2
            pending_pe = []
            pending_now = []
            OC2 = LA["OC2"]
            OC2t = [Tok(f"oc2_{i}") for i in range(4)]
            ONB = [LA[f"ON{i}"] for i in range(4)]
            ONBt = [Tok(f"on{i}") for i in range(4)]

            for h in range(NH):
                nsub = 512 // QBH[h]
                for blk in range(4):
                    vsrc = cciv[h] if blk == 3 else ccov[h][blk * 1024:(blk + 1) * 1024, :]
                    vsrc = vsrc.rearrange("(k p) e -> p k e", p=128)
                    rd = [CCVt[h]] if blk < 3 else [Vd[tt][h] for tt in range(8)]
                    P.op("sp", lambda e, blk=blk, vsrc=vsrc: e.dma_start(out=VB[blk].rearrange("p (k e) -> p k e", e=257)[:, :, 0:256], in_=vsrc),
                         reads=rd, writes=[VBt[blk]], dma=f"v{blk}")
                for qb in range(2):
                    for c in range(2):
                        ch = 2 * h + c
                        qi = qn[0] % 2
                        qn[0] += 1
                        P.op("sp", lambda e, qi=qi, ch=ch, qb=qb: e.dma_start(out=QB[qi][:, :], in_=qT_d[ch * 128:(ch + 1) * 128, qb * 512:(qb + 1) * 512]),
                             reads=[Qd[ch]], writes=[QBt[qi]], dma=f"q{qi}")
                        for blk in range(4):
                            rd = [CCKt[h]] if blk < 3 else [Kd[ch]]
                            ksrc = ccik[h][c * 128:(c + 1) * 128, :] if blk == 3 else ccok[h][blk * 256 + c * 128:blk * 256 + (c + 1) * 128, :]
                            P.op("sp", lambda e, blk=blk, ksrc=ksrc: e.dma_start(out=KB[blk][:, :], in_=ksrc),
                                 reads=rd, writes=[KBt[blk]], dma=f"k{blk}")
                        work = []
                        for blk in range(3):
                            for kt in range(8):
                                work.append((blk, kt, 0, False))
                        for kt in range(8):
                            if kt < 4 * qb:
                                work.append((3, kt, 0, False))
                            elif kt <= 4 * qb + 3:
                                work.append((3, kt, kt - 4 * qb, True))
                        nW = len(work)
                        pis = [None] * nW

                        def emit_score(i):
                            blk, kt, m, diag = work[i]
                            c0 = 128 * m
                            sb = SCB[scn[0] % 4]
                            scn[0] += 1
                            pi = ptn[0] % 4
                            ptn[0] += 1
                            pis[i] = pi
                            P.op("pe", lambda e, sb=sb, blk=blk, kt=kt, qi=qi, c0=c0: e.matmul(
                                PSa[sb][:, c0:512], lhsT=KB[blk][:, kt * 128:(kt + 1) * 128], rhs=QB[qi][:, c0:512], start=True, stop=True),
                                 reads=[KBt[blk], QBt[qi]], writes=[PSt[sb]], signal=True)
                            for sub in range(nsub):
                                a0 = max(c0, sub * QBH[h])
                                a1 = (sub + 1) * QBH[h]
                                if a0 >= a1:
                                    continue
                                bc = BIDX[(h, blk, qb, kt, sub)]
                                P.op("act", lambda e, pi=pi, sb=sb, a0=a0, a1=a1, bc=bc: e.activation(
                                    out=PT[pi][:, a0:a1], in_=PSa[sb][:, a0:a1], func=AF.Exp, bias=BIAS[:, bc:bc + 1], scale=float(SCALE)),
                                     reads=[PSt[sb], CONSTt], writes=[PTt[pi]])
                            if diag:
                                P.op("dve", lambda e, pi=pi, c0=c0: e.tensor_tensor(out=PT[pi][:, c0:c0 + 128], in0=PT[pi][:, c0:c0 + 128], in1=TRI[:, :], op=ALU.mult),
                                     reads=[PTt[pi], CONSTt], writes=[PTt[pi]])

                        def emit_av(i):
                            blk, kt, m, diag = work[i]
                            pi = pis[i]
                            for qs in range(m, 4):
                                first = (i == 0)
                                lastk = (blk == 3 and kt == 4 * qb + qs)
                                P.op("pe", lambda e, qs=qs, pi=pi, blk=blk, kt=kt, first=first, lastk=lastk: e.matmul(
                                    PSa[ACC[qs]][:, 0:257], lhsT=PT[pi][:, qs * 128:(qs + 1) * 128], rhs=VB[blk][:, kt * 257:(kt + 1) * 257],
                                    start=first, stop=lastk),
                                     reads=[PTt[pi], VBt[blk]], writes=[PSt[ACC[qs]]], signal=(lastk or qs == 3))

                        for idx in range(nW + LOOK):
                            if idx < nW:
                                emit_score(idx)
                            if idx == 8 and pending_pe:
                                for fn_ in pending_pe:
                                    fn_()
                                pending_pe.clear()
                            if idx >= LOOK:
                                emit_av(idx - LOOK)
                        dstb = OC if c == 0 else OC2
                        dstt = OCt if c == 0 else OC2t
                        alias = [STGt[2], STGt[3]] if c == 0 else [STGt[4], STGt[5]]
                        for qs in range(4):
                            P.op("dve", lambda e, qs=qs, dstb=dstb: e.tensor_copy(out=dstb[:, qs * 257:(qs + 1) * 257], in_=PSa[ACC[qs]][:, 0:257]),
                                 reads=[PSt[ACC[qs]]], writes=[dstt[qs]] + alias)
                        if c == 1:
                            OC3 = OC.rearrange("p (q e) -> p q e", e=257)
                            OC23 = OC2.rearrange("p (q e) -> p q e", e=257)
                            P.op("dve", lambda e: e.reciprocal(out=SS[:, 0:4], in_=OC3[:, :, 256]), reads=OCt, writes=[SSt])
                            P.op("dve", lambda e: e.reciprocal(out=SS[:, 4:8], in_=OC23[:, :, 256]), reads=OC2t + [SSt], writes=[SSt])
                            P.op("dve", lambda e: e.tensor_scalar(out=SS[:, 4:8], in0=SS[:, 4:8], scalar1=LAM, scalar2=None, op0=ALU.mult),
                                 reads=[SSt, SMt], writes=[SSt])
                            for qs in range(4):
                                ocq = OC[:, qs * 257:qs * 257 + 256]
                                oc2q = OC2[:, qs * 257:qs * 257 + 256]
                                P.op("dve", lambda e, oc2q=oc2q, qs=qs: e.tensor_scalar(out=OT[:, :], in0=oc2q, scalar1=SS[:, 4 + qs:5 + qs], scalar2=None, op0=ALU.mult),
                                     reads=[OC2t[qs], SSt], writes=[OTt])
                                P.op("dve", lambda e, ocq=ocq, qs=qs: e.scalar_tensor_tensor(out=ocq, in0=ocq, scalar=SS[:, qs:qs + 1], in1=OT[:, :],
                                                                                            op0=ALU.mult, op1=ALU.subtract),
                                     reads=[OCt[qs], OTt, SSt], writes=[OCt[qs]])
                                P.op("dve", lambda e, ocq=ocq, qs=qs: e.scalar_tensor_tensor(out=JUNK[:, :], in0=ocq, scalar=1.0, in1=ocq, op0=ALU.mult, op1=ALU.mult,
                                                                                            accum_out=SS[:, 8 + qs:9 + qs]),
                                     reads=[OCt[qs], SSt], writes=[JUNKt, SSt])
                            P.op("act", lambda e: e.activation(out=SS[:, 12:16], in_=SS[:, 8:12], func=AF.Ln, bias=EPSC[:, 0:1], scale=1.0 / 256),
                                 reads=[SSt, ONESt], writes=[SSt])
                            P.op("act", lambda e: e.activation(out=SS[:, 12:16], in_=SS[:, 12:16], func=AF.Exp, scale=-0.5),
                                 reads=[SSt], writes=[SSt])
                            for qs in range(4):
                                tq = (qb * 4 + qs) * 128
                                ocq = OC[:, qs * 257:qs * 257 + 256]
                                oi = onn[0] % 4
                                onn[0] += 1
                                P.op("dve", lambda e, ocq=ocq, qs=qs, oi=oi: e.scalar_tensor_tensor(out=ONB[oi][:, :], in0=ocq, scalar=SS[:, 12 + qs:13 + qs], in1=SUBG[:, :],
                                                                                                   op0=ALU.mult, op1=ALU.mult),
                                     reads=[OCt[qs], SSt, CONSTt], writes=[ONBt[oi]])

                                def tr_fn(oi=oi, tq=tq, h=h):
                                    for j in range(2):
                                        sb = SCB[scn[0] % 4]
                                        scn[0] += 1
                                        pv = PSa[sb].bitcast(BF16)
                                        P.op("pe", lambda e, pv=pv, j=j, oi=oi: e.transpose(out=pv[:, 0:128], in_=ONB[oi][:, j * 128:(j + 1) * 128], identity=IDENT[:, :]),
                                             reads=[ONBt[oi], CONSTt], writes=[PSt[sb]], signal=True)
                                        kk = 2 * h + j
                                        P.op("dve", lambda e, pv=pv, kk=kk, tq=tq: e.tensor_copy(out=Ak(kk, tq, tq + 128), in_=pv[:, 0:128]),
                                             reads=[PSt[sb]], writes=[At[kk][tq // 512]])
                                if qs % 2 == 1:
                                    pending_now.append(tr_fn)
                                else:
                                    pending_now.append(tr_fn)
                            pending_pe.extend(pending_now)
                            pending_now.clear()
            for fn_ in pending_pe:
                fn_()
            pending_pe.clear()

            for t in range(4):
                ti = wi[0]
                for ocl in range(4):
                    oc = t * 4 + ocl
                    for hf in range(2):
                        t0, t1 = hslice(hf)
                        bk = nb()
                        for k in range(KC):
                            off = (ocl * 16 + k) * 128
                            P.op("pe", lambda e, bk=bk, off=off, k=k, t0=t0, t1=t1, ti=ti: e.matmul(
                                PSa[bk][:, :], lhsT=wslot(ti, off), rhs=Ak(k, t0, t1), start=(k == 0), stop=(k == KC - 1)),
                                 reads=[Wt[ti % NSLOT], At[k][hf]], writes=[PSt[bk]], signal=(k == KC - 1))
                        P.op("dve", lambda e, bk=bk, oc=oc, t0=t0, t1=t1: e.tensor_tensor(out=Hk(oc, t0, t1), in0=Hk(oc, t0, t1), in1=PSa[bk][:, :], op=ALU.add),
                             reads=[PSt[bk], Ht[oc][hf]], writes=[Ht[oc][hf]])
                w_consume_done(ti)
                wi[0] += 1

        pool_mixer()
        dump(0)
        P.barrier()
        if stop >= 2:
            ffn(C_NFFN0)
            dump(1)
            P.barrier()
        if stop >= 3:
            attention()
            dump(2)
            P.barrier()
        if stop >= 4:
            ffn(C_NFFN1)
            dump(3)
            P.barrier()
        norm_stats(False)
        OUT = [LO["OUT0"], LO["OUT1"]]
        OUTt = [Tok("out0"), Tok("out1")]
        for k in range(KC):
            b = k % 2
            P.op("dve", lambda e, k=k, b=b: e.scalar_tensor_tensor(out=OUT[b][:, :], in0=Hk(k), scalar=col(C_FINAL + k), in1=RSTD[:, HALO:TE],
                                                                  op0=ALU.mult, op1=ALU.mult),
                 reads=[Ht[k][0], Ht[k][1], RSTDt, CONSTt], writes=[OUTt[b]])
            P.op("sp", lambda e, k=k, b=b: e.dma_start(out=out_d[:, k * T:(k + 1) * T], in_=OUT[b][:, :]), reads=[OUTt[b]], dma=f"o{b}")
        assert wi[0] == NT, (wi[0], NT)
        P.final_wait("sp", ["o0", "o1", "dbg"])

        sems = {}
        for key in P.semkeys:
            sems[key] = es.enter_context(nc.semaphore(f"s_{key}"))
        block = es.enter_context(nc.Block())

        def run(e, name):
            for item in P.q[name]:
                kind = item[0]
                if kind == "wait":
                    e.wait_ge(sems[item[1]], item[2])
                elif kind == "dma":
                    item[1](e).then_inc(sems[item[2]], item[3])
                elif kind == "sig":
                    item[1](e).then_inc(sems[item[2]], 1)
                else:
                    item[1](e)

        @block.tensor
        def _(e):
            run(e, "pe")

        @block.scalar
        def _(e):
            run(e, "act")

        @block.vector
        def _(e):
            run(e, "dve")

        @block.gpsimd
        def _(e):
            run(e, "pool")

        @block.sync
        def _(e):
            run(e, "sp")
    return nc


def _fm(a, n):
    return np.ascontiguousarray(a.T.reshape(KC, 128, n).transpose(1, 0, 2).reshape(128, KC * n))


def _colvec(v):
    return v.reshape(KC, 128).T


DEBUG = False
STOP = 4
_LAST = {}


def kernel(x, norm_mix, norm_ffn, pool_w, pool_scale, w_qkv, lambda_q1, lambda_k1, lambda_q2, lambda_k2,
           subln_g, w_o, w_gate, w_up, w_down, final_norm):
    inp = dict(x=x, norm_mix=norm_mix, norm_ffn=norm_ffn, pool_w=pool_w, pool_scale=pool_scale, w_qkv=w_qkv,
               w_o=w_o, w_gate=w_gate, w_up=w_up, w_down=w_down)
    inp = {k: np.asarray(v, dtype=np.float32) for k, v in inp.items()}
    x = inp["x"]
    wpack = pack_weights(inp)
    cols = np.zeros((128, NCOLS), np.float32)
    cols[:, C_NMIX0:C_NMIX0 + 16] = _colvec(inp["norm_mix"][0])
    cols[:, C_NFFN0:C_NFFN0 + 16] = _colvec(inp["norm_ffn"][0])
    cols[:, C_PSCALE:C_PSCALE + 16] = _colvec(inp["pool_scale"][0])
    cols[:, C_NMIX1:C_NMIX1 + 16] = _colvec(inp["norm_mix"][1])
    cols[:, C_NFFN1:C_NFFN1 + 16] = _colvec(inp["norm_ffn"][1])
    cols[:, C_FINAL:C_FINAL + 16] = _colvec(np.asarray(final_norm, np.float32))
    lamw = np.concatenate([np.asarray(v, np.float32).reshape(1, 128) for v in (lambda_q1, lambda_k1, lambda_q2, lambda_k2)], axis=1)
    lamw = np.ascontiguousarray(np.broadcast_to(lamw, (128, 512)))
    subg = np.ascontiguousarray(np.broadcast_to(np.asarray(subln_g, np.float32).reshape(1, 256), (128, 256)))
    tri = (np.arange(128)[None, :] >= np.arange(128)[:, None]).astype(ml_dtypes.bfloat16)
    ident = np.eye(128).astype(ml_dtypes.bfloat16)

    in_maps = []
    for c in range(NCORES):
        b, r = c // 4, c % 4
        xs = x[b, r * T:(r + 1) * T, :]
        if r == 0:
            xh = np.zeros((HALO, D), np.float32)
        else:
            xh = x[b, r * T - HALO:r * T, :]
        inv16 = np.zeros((128, 64), np.float32)
        for g in range(4):
            w = 2 << g
            tpos = r * T + np.arange(16)
            inv16[:, g * 16:(g + 1) * 16] = (1.0 / np.minimum(tpos + 1, w)).astype(np.float32)[None, :]
        in_maps.append({
            "xT": _fm(xs, T), "xh": _fm(xh, HALO), "cols": cols, "bias": make_bias(r), "inv16": inv16,
            "tri": tri, "ident": ident, "lamw": lamw, "subg": subg, "w": wpack,
        })
    nc = build_program(debug=DEBUG, stop=STOP)
    for m in in_maps:
        m["w"] = wpack[:NEED_TILES[STOP]]
    res = run_bass_kernel_spmd(nc, in_maps, core_ids=list(range(NCORES)))
    out = np.zeros((2, 4096, D), np.float32)
    for c in range(NCORES):
        b, r = c // 4, c % 4
        o = np.asarray(res.results[c]["out"]).reshape(128, KC, T).transpose(2, 1, 0).reshape(T, D)
        out[b, r * T:(r + 1) * T, :] = o
    if DEBUG:
        _LAST["dbg"] = [np.asarray(res.results[c]["dbg"]) for c in range(NCORES)]
    return out
```

```python
import math
import numpy as np
import ml_dtypes
import concourse.bass as bass
import concourse.mybir as mybir
from concourse.bass_utils import run_bass_kernel_spmd

F32 = mybir.dt.float32
BF16 = mybir.dt.bfloat16
AF = mybir.ActivationFunctionType
ALU = mybir.AluOpType

NCORES = 8
D = 2048
T = 1024
HALO = 16
TE = T + HALO
KC = 16
FF = 5632
FC = 44
NFB = 4
FCB = FC // NFB
NH = 8
WS = 8192
NSLOT = 3
EPS = 1e-6
LAMBDA_INIT = 0.8 - 0.6 * math.exp(-0.3 * 1)
SCALE = 128 ** -0.5
SLOPES = [2.0 ** (-8.0 * (i + 1) / NH) for i in range(NH)]
QBH = [256, 512, 512, 512, 512, 512, 512, 512]
NEG = -30000.0
GROUPS = [[0, 1, 2, 3], [4, 5, 6, 7]]

C_NMIX0, C_NFFN0, C_PSCALE, C_NMIX1, C_NFFN1, C_FINAL = [i * 16 for i in range(6)]
NCOLS = 96


def weight_tiles():
    tiles = [("pool", (), 8192)]

    def ffn(l):
        out = []
        for blk in range(NFB):
            j0 = blk * FCB
            jl = 0
            while jl < FCB:
                nj = min(2, FCB - jl)
                out.append(("gu", (l, j0 + jl, nj), nj * 4096))
                jl += nj
            for dg in range(4):
                out.append(("down", (l, blk, dg), FCB * 512))
        return out

    tiles += ffn(0)
    for t in range(4):
        tiles.append(("k", (t,), 8192))
        tiles.append(("v", (t,), 8192))
    for t in range(4):
        tiles.append(("q", (t,), 8192))
    for t in range(4):
        tiles.append(("o", (t,), 8192))
    tiles += ffn(1)
    return tiles


def pack_weights(inp):
    tiles = weight_tiles()
    w = np.zeros((len(tiles), 128, WS), np.float32)
    wqkv = inp["w_qkv"][0]
    for i, (kind, a, n) in enumerate(tiles):
        if kind == "pool":
            pw = inp["pool_w"][0].reshape(4, 4, 128, 4, 128).transpose(2, 0, 1, 3, 4)
            w[i, :, :n] = pw.reshape(128, n)
        elif kind == "gu":
            l, j0, nj = a
            for jj in range(nj):
                for gu, W in enumerate((inp["w_gate"][l], inp["w_up"][l])):
                    sub = W[:, (j0 + jj) * 128:(j0 + jj + 1) * 128].reshape(16, 128, 128).transpose(1, 0, 2)
                    o = (jj * 2 + gu) * 2048
                    w[i, :, o:o + 2048] = sub.reshape(128, 2048)
        elif kind == "down":
            l, blk, dg = a
            sub = inp["w_down"][l][blk * FCB * 128:(blk + 1) * FCB * 128, dg * 512:(dg + 1) * 512]
            sub = sub.reshape(FCB, 128, 4, 128).transpose(1, 0, 2, 3)
            w[i, :, :n] = sub.reshape(128, n)
        elif kind in ("k", "q", "o"):
            t = a[0]
            if kind == "q":
                W = wqkv[:, 0:2048]
            elif kind == "k":
                W = wqkv[:, 2048:4096]
            else:
                W = inp["w_o"][0]
            sub = W[:, t * 512:(t + 1) * 512].reshape(16, 128, 4, 128).transpose(1, 2, 0, 3)
            w[i, :, :n] = sub.reshape(128, n)
        elif kind == "v":
            t = a[0]
            sub = wqkv[:, 4096 + t * 512:4096 + (t + 1) * 512].reshape(16, 128, 512).transpose(1, 0, 2)
            w[i, :, :n] = sub.reshape(128, n)
    return w


def bias_index():
    idx = {}
    n = 0
    for h in range(NH):
        ns = 512 // QBH[h]
        for blk in range(4):
            for qb in range(2):
                for kt in range(8):
                    for sub in range(ns):
                        idx[(h, blk, qb, kt, sub)] = n
                        n += 1
    return idx, n


BIDX, NBIAS = bias_index()


def make_bias(r):
    b = np.zeros((128, NBIAS), np.float32)
    j = np.arange(128, dtype=np.float64)
    for (h, blk, qb, kt, sub), col in BIDX.items():
        s = r if blk == 3 else blk
        if blk != 3 and s >= r:
            b[:, col] = NEG
            continue
        kpos = s * 1024 + kt * 128 + j
        qref = r * 1024 + qb * 512 + sub * QBH[h] + QBH[h] // 2
        b[:, col] = (SLOPES[h] * (kpos - qref)).astype(np.float32)
    return b


class Tok:
    __slots__ = ("name", "w", "r")

    def __init__(self, name):
        self.name = name
        self.w = None
        self.r = {}


class Prog:
    ENG = ("pe", "act", "dve", "pool", "sp")
    LIMIT = 240

    def __init__(self):
        self.q = {e: [] for e in self.ENG}
        self.cnt = {}
        self.known = {e: {} for e in self.ENG}
        self.semkeys = []
        self.epoch = {}
        self.keysrc = {}
        self.basekeys = {}

    def _key(self, base, inc, src):
        if base not in self.epoch:
            self.epoch[base] = 0
            self.basekeys[base] = []
            k = f"{base}.0"
            self.cnt[k] = 0
            self.semkeys.append(k)
            self.keysrc[k] = src
            self.basekeys[base].append(k)
        k = f"{base}.{self.epoch[base]}"
        if self.cnt[k] + inc > self.LIMIT:
            self.epoch[base] += 1
            k = f"{base}.{self.epoch[base]}"
            self.cnt[k] = 0
            self.semkeys.append(k)
            self.keysrc[k] = src
            self.basekeys[base].append(k)
        return k

    def total(self, base):
        return sum(self.cnt[k] for k in self.basekeys.get(base, []))

    def op(self, eng, fn, reads=(), writes=(), signal=True, dma=None, inc=16):
        src = "dma" if dma else eng
        need = {}

        def consider(ev, kind):
            key, val, s = ev
            if s == src and s == "pe":
                return
            if self.known[eng].get(key, 0) >= val:
                return
            if need.get(key, 0) < val:
                need[key] = val

        for t in reads:
            if t.w is not None:
                consider(t.w, "raw")
        for t in writes:
            if t.w is not None:
                consider(t.w, "waw")
            for key, (val, s) in t.r.items():
                consider((key, val, s), "war")
        for key, val in need.items():
            self.known[eng][key] = val
            self.q[eng].append(("wait", key, val))
        if dma:
            key = self._key(dma, inc, "dma")
            self.cnt[key] += inc
            ev = (key, self.cnt[key], "dma")
            self.q[eng].append(("dma", fn, key, inc))
        else:
            key = self._key(eng, 1, eng)
            if signal:
                self.cnt[key] += 1
                ev = (key, self.cnt[key], eng)
                self.q[eng].append(("sig", fn, key))
            else:
                ev = (key, self.cnt[key] + 1, eng)
                self.q[eng].append(("nosig", fn))
        for t in reads:
            old = t.r.get(ev[0])
            if old is None or old[0] < ev[1]:
                t.r[ev[0]] = (ev[1], ev[2])
        for t in writes:
            t.w = ev
            t.r = {}
        return ev

    def last_event(self, base):
        k = self.basekeys[base][-1]
        return (k, self.cnt[k], self.keysrc[k])

    def barrier(self):
        for e in self.ENG:
            for base, keys in self.basekeys.items():
                src = self.keysrc[keys[0]]
                if src == "dma":
                    wk = keys
                else:
                    if src == "pe" and e == "pe":
                        continue
                    wk = keys[-1:]
                    for k in keys[:-1]:
                        self.known[e][k] = self.cnt[k]
                for key in wk:
                    val = self.cnt[key]
                    if val and self.known[e].get(key, 0) < val:
                        self.known[e][key] = val
                        self.q[e].append(("wait", key, val))

    def final_wait(self, eng, bases):
        for base in bases:
            for key in self.basekeys.get(base, []):
                val = self.cnt[key]
                if val and self.known[eng].get(key, 0) < val:
                    self.known[eng][key] = val
                    self.q[eng].append(("wait", key, val))


NEED_TILES = {1: 1, 2: 41, 3: 57, 4: 97}


def build_program(debug=False, stop=4):
    nc = bass.Bass("TRN2", target_bir_lowering=False)
    P = Prog()
    tiles = weight_tiles()
    NT = min(len(tiles), NEED_TILES[stop])

    xT_d = nc.dram_tensor("xT", [128, KC * T], F32, kind="ExternalInput").ap()
    xh_d = nc.dram_tensor("xh", [128, KC * HALO], F32, kind="ExternalInput").ap()
    cols_d = nc.dram_tensor("cols", [128, NCOLS], F32, kind="ExternalInput").ap()
    bias_d = nc.dram_tensor("bias", [128, NBIAS], F32, kind="ExternalInput").ap()
    inv16_d = nc.dram_tensor("inv16", [128, 64], F32, kind="ExternalInput").ap()
    tri_d = nc.dram_tensor("tri", [128, 128], BF16, kind="ExternalInput").ap()
    ident_d = nc.dram_tensor("ident", [128, 128], BF16, kind="ExternalInput").ap()
    lamw_d = nc.dram_tensor("lamw", [128, 512], F32, kind="ExternalInput").ap()
    subg_d = nc.dram_tensor("subg", [128, 256], F32, kind="ExternalInput").ap()
    w_d = nc.dram_tensor("w", [NT, 128, WS], F32, kind="ExternalInput").ap()
    out_d = nc.dram_tensor("out", [128, KC * T], F32, kind="ExternalOutput").ap()
    ccik = [nc.dram_tensor(f"ccik{h}", [256, 1024], BF16, kind="Internal", addr_space="Local").ap() for h in range(NH)]
    ccok = [nc.dram_tensor(f"ccok{h}", [4 * 256, 1024], BF16, kind="Internal", addr_space="Local").ap() for h in range(NH)]
    cciv = [nc.dram_tensor(f"cciv{h}", [1024, 256], BF16, kind="Internal", addr_space="Local").ap() for h in range(NH)]
    ccov = [nc.dram_tensor(f"ccov{h}", [4 * 1024, 256], BF16, kind="Internal", addr_space="Local").ap() for h in range(NH)]
    qT_d = nc.dram_tensor("qT_s", [2048, 1024], BF16, kind="Internal", addr_space="Local").ap()
    dbg_d = None
    if debug:
        dbg_d = nc.dram_tensor("dbg", [4, 128, KC * T], F32, kind="ExternalOutput").ap()

    cur = [16512]

    def alloc(name, n, dt, at=None):
        sz = n * (4 if dt == F32 else 2)
        sz = (sz + 31) // 32 * 32
        if at is None:
            off = cur[0]
            cur[0] += sz
        else:
            off = at
        assert off + sz <= 229344, (name, off, sz)
        return nc.alloc_sbuf_tensor_at(name, [128, n], dt, offset=off).ap(), off + sz

    H, _ = alloc("H", KC * T, F32)
    HH, _ = alloc("HH", KC * HALO, F32)
    A, _ = alloc("A", KC * TE, BF16)
    WSL, _ = alloc("WSL", NSLOT * WS, BF16)
    COLS, _ = alloc("COLS", NCOLS, F32)
    BIAS, _ = alloc("BIAS", NBIAS, F32)
    INV16, _ = alloc("INV16", 64, F32)
    TRI, _ = alloc("TRI", 128, BF16)
    IDENT, _ = alloc("IDENT", 128, BF16)
    ONES, _ = alloc("ONES", 128, F32)
    LAMW, _ = alloc("LAMW", 512, F32)
    SUBG, _ = alloc("SUBG", 256, F32)
    SM, _ = alloc("SM", 32, F32)
    EPSC, _ = alloc("EPSC", 8, F32)
    RSTD, _ = alloc("RSTD", TE, F32)
    ONESB, _ = alloc("ONESB", 128, BF16)
    SQ = []
    for b in range(4):
        t_, _ = alloc(f"SQ{b}", 528, BF16)
        SQ.append(t_)
    S0 = cur[0]

    OFF = {}

    def layout(base, specs):
        res = {}
        off = base
        for name, n, dt in specs:
            OFF[name] = off
            res[name], off = alloc(name, n, dt, at=off)
        return res

    LP = layout(S0, [("SQH", 256, BF16), ("T16a", 16, F32), ("T16b", 16, F32)] +
                [(f"{n}{ab}", TE, F32) for ab in "ab" for n in ("E", "S2", "S4", "S8", "S16")])
    LF = layout(S0, [("ACTB", FCB * T, BF16), ("SG0", 512, F32), ("SG1", 512, F32)])
    LA = layout(S0, [("STG0", 1024, BF16), ("STG1", 1024, BF16)] +
                [(f"VB{i}", 8 * 257, BF16) for i in range(4)] +
                [(f"KB{i}", 1024, BF16) for i in range(4)] +
                [("QB0", 512, BF16), ("QB1", 512, BF16)] +
                [(f"P{i}", 512, BF16) for i in range(4)] +
                [("OC", 4 * 257, F32), ("OC2", 4 * 257, F32), ("OT", 256, F32)] +
                [(f"ON{i}", 256, BF16) for i in range(4)] +
                [("SS", 16, F32)])
    LA.update(layout(OFF["OC"], [("STG2", 1024, BF16), ("STG3", 1024, BF16)]))
    LA.update(layout(OFF["OC2"], [("STG4", 1024, BF16), ("STG5", 1024, BF16)]))
    assert OFF["STG3"] + 2048 <= OFF["OC2"] and OFF["STG5"] + 2048 <= OFF["OT"]
    LO = layout(S0, [("OUT0", T, F32), ("OUT1", T, F32)])

    Ht = [[Tok(f"H{k}_{hf}") for hf in range(2)] for k in range(KC)]
    HHt = Tok("HH")
    At = [[Tok(f"A{k}_{hf}") for hf in range(2)] for k in range(KC)]
    Aht = [Tok(f"Ah{k}") for k in range(KC)]
    Wt = [Tok(f"W{s}") for s in range(NSLOT)]
    CONSTt = Tok("const")
    ONESt = Tok("ones")
    RSTDt = Tok("rstd")
    SQt = [Tok(f"sq{i}") for i in range(4)]
    PSt = [Tok(f"ps{i}") for i in range(8)]
    SMt = Tok("sm")

    def Hk(k, t0=0, t1=T):
        return H[:, k * T + t0:k * T + t1]

    def Ak(k, t0=0, t1=T):
        return A[:, k * TE + HALO + t0:k * TE + HALO + t1]

    def col(c):
        return COLS[:, c:c + 1]

    def hslice(hf):
        return (hf * 512, (hf + 1) * 512)

    import contextlib
    es = contextlib.ExitStack()
    with es:
        PS = [es.enter_context(nc.psum_tensor(f"psb{i}", [128, 512], F32)) for i in range(8)]
        PSa = [p[:] for p in PS]
        bank_rr = [0]

        def nb():
            b = bank_rr[0]
            bank_rr[0] = (b + 1) % 8
            return b

        wstate = {"next_load": 0}

        def w_load(i, after=()):
            if i >= NT:
                return
            s = i % NSLOT
            n = tiles[i][2]
            P.op("pool", lambda e, i=i, s=s, n=n: e.dma_start(out=WSL[:, s * WS:s * WS + n], in_=w_d[i, :, 0:n]),
                 reads=list(after), writes=[Wt[s]], dma=f"w{s}")

        def w_consume_done(i):
            w_load(i + NSLOT)

        def wslot(i, off, n=128):
            s = i % NSLOT
            return WSL[:, s * WS + off:s * WS + off + n]

        for dst, srcd in ((COLS, cols_d), (BIAS, bias_d), (INV16, inv16_d), (TRI, tri_d), (IDENT, ident_d),
                          (LAMW, lamw_d), (SUBG, subg_d)):
            P.op("sp", lambda e, dst=dst, srcd=srcd: e.dma_start(out=dst[:, :], in_=srcd[:, :]), dma="cst")
        P.op("sp", lambda e: e.dma_start(out=HH[:, :], in_=xh_d[:, :]), writes=[HHt], dma="xh")
        for g in range(4):
            P.op("sp", lambda e, g=g: e.dma_start(out=H[:, g * 4 * T:(g + 1) * 4 * T], in_=xT_d[:, g * 4 * T:(g + 1) * 4 * T]),
                 writes=[Ht[k][hf] for k in range(g * 4, g * 4 + 4) for hf in range(2)], dma=f"x{g}")
        CONSTt.w = P.last_event("cst")
        P.op("dve", lambda e: e.memset(ONES[:, :], 1.0), writes=[ONESt])
        P.op("dve", lambda e: e.memset(EPSC[:, :], EPS), writes=[ONESt])
        P.op("dve", lambda e: e.memset(ONESB[:, :], 1.0), writes=[ONESt])
        for i in range(NSLOT):
            w_load(i, after=[Ht[11][1]] if i == 0 else [Ht[15][1]])
        wi = [0]

        def norm_stats(with_halo):
            banks = [nb(), nb()]
            n = 0
            for k in range(KC):
                for hf in range(2):
                    b = n % 4
                    n += 1
                    t0, t1 = hslice(hf)
                    P.op("act", lambda e, b=b, k=k, t0=t0, t1=t1: e.activation(out=SQ[b][:, 0:512], in_=Hk(k, t0, t1), func=AF.Square),
                         reads=[Ht[k][hf]], writes=[SQt[b]])
                    P.op("pe", lambda e, b=b, bk=banks[hf], k=k: e.matmul(PSa[bk][:, :], lhsT=ONESB[:, :], rhs=SQ[b][:, 0:512],
                                                                           start=(k == 0), stop=(k == KC - 1)),
                         reads=[SQt[b], ONESt], writes=[PSt[banks[hf]]], signal=True)
            hb = None
            if with_halo:
                hb = nb()
                SQH = LP["SQH"]
                sqht = Tok("sqh")
                P.op("act", lambda e: e.activation(out=SQH[:, :], in_=HH[:, :], func=AF.Square), reads=[HHt], writes=[sqht])
                for k in range(KC):
                    P.op("pe", lambda e, k=k: e.matmul(PSa[hb][:, 0:HALO], lhsT=ONESB[:, :], rhs=SQH[:, k * HALO:(k + 1) * HALO],
                                                        start=(k == 0), stop=(k == KC - 1)),
                         reads=[sqht, ONESt], writes=[PSt[hb]], signal=(k == KC - 1))
            for hf in range(2):
                t0, t1 = hslice(hf)
                P.op("act", lambda e, bk=banks[hf], t0=t0, t1=t1: e.activation(out=RSTD[:, HALO + t0:HALO + t1], in_=PSa[bk][:, :],
                                                                              func=AF.Sqrt, bias=EPSC[:, 0:1], scale=1.0 / D),
                     reads=[PSt[banks[hf]], ONESt], writes=[RSTDt])
            if with_halo:
                P.op("act", lambda e: e.activation(out=RSTD[:, 0:HALO], in_=PSa[hb][:, 0:HALO], func=AF.Sqrt, bias=EPSC[:, 0:1], scale=1.0 / D),
                     reads=[PSt[hb], ONESt], writes=[RSTDt])
            lo = 0 if with_halo else HALO
            P.op("dve", lambda e, lo=lo: e.reciprocal(out=RSTD[:, lo:TE], in_=RSTD[:, lo:TE]),
                 reads=[RSTDt], writes=[RSTDt])

        def norm_apply(cbase):
            for k in range(KC):
                P.op("dve", lambda e, k=k: e.scalar_tensor_tensor(out=Ak(k), in0=Hk(k), scalar=col(cbase + k), in1=RSTD[:, HALO:TE],
                                                                  op0=ALU.mult, op1=ALU.mult),
                     reads=[Ht[k][0], Ht[k][1], RSTDt, CONSTt], writes=[At[k][0], At[k][1]])

        def dump(i):
            if debug:
                P.barrier()
                P.op("sp", lambda e, i=i: e.dma_start(out=dbg_d[i], in_=H[:, :]),
                     reads=[Ht[k][hf] for k in range(KC) for hf in range(2)], dma="dbg")

        def pool_mixer():
            norm_stats(True)
            sets = []
            for ab in "ab":
                sets.append(dict(E=LP["E" + ab], S=[LP["S2" + ab], LP["S4" + ab], LP["S8" + ab], LP["S16" + ab]], T16=LP["T16" + ab],
                                 Et=Tok("E" + ab), St=[Tok(f"S{i}{ab}") for i in range(4)], T16t=Tok("T16" + ab)))

            def chunk_ops(k, B):
                g = k // 4
                w = 2 << g
                E, Et = B["E"], B["Et"]
                yield lambda: P.op("dve", lambda e: e.scalar_tensor_tensor(out=E[:, 0:HALO], in0=HH[:, k * HALO:(k + 1) * HALO], scalar=col(C_NMIX0 + k),
                                                                         in1=RSTD[:, 0:HALO], op0=ALU.mult, op1=ALU.mult),
                                   reads=[HHt, RSTDt, CONSTt], writes=[Et])
                yield lambda: P.op("dve", lambda e: e.scalar_tensor_tensor(out=E[:, HALO:TE], in0=Hk(k), scalar=col(C_NMIX0 + k),
                                                                         in1=RSTD[:, HALO:TE], op0=ALU.mult, op1=ALU.mult),
                                   reads=[Ht[k][0], Ht[k][1], RSTDt, CONSTt], writes=[Et])
                prev, prevt = E, Et
                lo = 0
                for step in range(g + 1):
                    sh = 1 << step
                    lo2 = lo + sh
                    dst, dstt = B["S"][step], B["St"][step]
                    yield lambda dst=dst, dstt=dstt, prev=prev, prevt=prevt, lo2=lo2, sh=sh: P.op(
                        "dve", lambda e: e.tensor_tensor(out=dst[:, lo2:TE], in0=prev[:, lo2:TE], in1=prev[:, lo2 - sh:TE - sh], op=ALU.add),
                        reads=[prevt], writes=[dstt])
                    prev, prevt, lo = dst, dstt, lo2
                S, Stok = prev, prevt
                yield lambda: P.op("dve", lambda e: e.scalar_tensor_tensor(out=Ak(k), in0=S[:, HALO:TE], scalar=1.0 / w, in1=E[:, HALO:TE],
                                                                         op0=ALU.mult, op1=ALU.subtract),
                                   reads=[Stok, Et], writes=[At[k][0], At[k][1]])
                yield lambda: P.op("dve", lambda e: e.tensor_tensor(out=B["T16"][:, :], in0=S[:, HALO:2 * HALO], in1=INV16[:, g * 16:(g + 1) * 16], op=ALU.mult),
                                   reads=[Stok, CONSTt], writes=[B["T16t"]])
                yield lambda: P.op("dve", lambda e: e.tensor_tensor(out=Ak(k, 0, HALO), in0=B["T16"][:, :], in1=E[:, HALO:2 * HALO], op=ALU.subtract),
                                   reads=[B["T16t"], Et, At[k][0]], writes=[At[k][0]])

            for k in range(0, KC, 2):
                ga, gb = chunk_ops(k, sets[0]), chunk_ops(k + 1, sets[1])
                while True:
                    fa, fb = next(ga, None), next(gb, None)
                    if fa is None and fb is None:
                        break
                    if fa is not None:
                        fa()
                    if fb is not None:
                        fb()
            ti = wi[0]
            for g in range(4):
                for oc in range(4):
                    c = g * 4 + oc
                    for hf in range(2):
                        t0, t1 = hslice(hf)
                        bk = nb()
                        for kk in range(4):
                            off = ((g * 4 + kk) * 4 + oc) * 128
                            last = (g == 3 and oc == 3 and hf == 1 and kk == 3)
                            P.op("pe", lambda e, bk=bk, off=off, kk=kk, g=g, t0=t0, t1=t1: e.matmul(
                                PSa[bk][:, :], lhsT=wslot(ti, off), rhs=Ak(g * 4 + kk, t0, t1), start=(kk == 0), stop=(kk == 3)),
                                 reads=[Wt[ti % NSLOT], At[g * 4 + kk][hf]], writes=[PSt[bk]], signal=(kk == 3))
                        P.op("dve", lambda e, bk=bk, c=c, t0=t0, t1=t1: e.scalar_tensor_tensor(
                            out=Hk(c, t0, t1), in0=PSa[bk][:, :], scalar=col(C_PSCALE + c), in1=Hk(c, t0, t1), op0=ALU.mult, op1=ALU.add),
                             reads=[PSt[bk], Ht[c][hf], CONSTt], writes=[Ht[c][hf]])
            w_consume_done(ti)
            wi[0] += 1

        def ffn(cnorm):
            norm_stats(False)
            norm_apply(cnorm)
            ACTB = LF["ACTB"]
            SG = [LF["SG0"], LF["SG1"]]
            SGt = [Tok("sg0"), Tok("sg1")]
            ACTt = [[Tok(f"act{j}_{hf}") for hf in range(2)] for j in range(FCB)]
            sgn = [0]
            for blk in range(NFB):
                jl = 0
                while jl < FCB:
                    nj = min(2, FCB - jl)
                    ti = wi[0]
                    for jj in range(nj):
                        j = jl + jj
                        banks = [nb() for _ in range(4)]
                        for k in range(KC):
                            for gu in range(2):
                                off = ((jj * 2 + gu) * 16 + k) * 128
                                for hf in range(2):
                                    t0, t1 = hslice(hf)
                                    bk = banks[gu * 2 + hf]
                                    P.op("pe", lambda e, bk=bk, off=off, k=k, t0=t0, t1=t1, ti=ti: e.matmul(
                                        PSa[bk][:, :], lhsT=wslot(ti, off), rhs=Ak(k, t0, t1), start=(k == 0), stop=(k == KC - 1)),
                                         reads=[Wt[ti % NSLOT], At[k][hf]], writes=[PSt[bk]], signal=(k == KC - 1))
                        for hf in range(2):
                            t0, t1 = hslice(hf)
                            b = sgn[0] % 2
                            sgn[0] += 1
                            P.op("act", lambda e, b=b, bk=banks[hf]: e.activation(out=SG[b][:, :], in_=PSa[bk][:, :], func=AF.Silu),
                                 reads=[PSt[banks[hf]]], writes=[SGt[b]])
                            P.op("dve", lambda e, b=b, bk=banks[2 + hf], j=j, t0=t0, t1=t1: e.tensor_tensor(
                                out=ACTB[:, j * T + t0:j * T + t1], in0=SG[b][:, :], in1=PSa[bk][:, :], op=ALU.mult),
                                 reads=[SGt[b], PSt[banks[2 + hf]]], writes=[ACTt[j][hf]])
                    w_consume_done(ti)
                    wi[0] += 1
                    jl += nj
                for dg in range(4):
                    ti = wi[0]
                    for dd in range(4):
                        dc = dg * 4 + dd
                        for hf in range(2):
                            t0, t1 = hslice(hf)
                            bk = nb()
                            for fc in range(FCB):
                                off = (fc * 4 + dd) * 128
                                P.op("pe", lambda e, bk=bk, off=off, fc=fc, t0=t0, t1=t1, ti=ti: e.matmul(
                                    PSa[bk][:, :], lhsT=wslot(ti, off), rhs=ACTB[:, fc * T + t0:fc * T + t1], start=(fc == 0), stop=(fc == FCB - 1)),
                                     reads=[Wt[ti % NSLOT], ACTt[fc][hf]], writes=[PSt[bk]], signal=(fc == FCB - 1))
                            P.op("dve", lambda e, bk=bk, dc=dc, t0=t0, t1=t1: e.tensor_tensor(
                                out=Hk(dc, t0, t1), in0=Hk(dc, t0, t1), in1=PSa[bk][:, :], op=ALU.add),
                                 reads=[PSt[bk], Ht[dc][hf]], writes=[Ht[dc][hf]])
                    w_consume_done(ti)
                    wi[0] += 1

        def attention():
            norm_stats(False)
            norm_apply(C_NMIX1)
            NSTG = 6
            STG = [LA[f"STG{i}"] for i in range(NSTG)]
            STGt = [Tok(f"stg{i}") for i in range(NSTG)]
            VB = [LA[f"VB{i}"] for i in range(4)]
            VBt = [Tok(f"vb{i}") for i in range(4)]
            KB = [LA[f"KB{i}"] for i in range(4)]
            KBt = [Tok(f"kb{i}") for i in range(4)]
            QB = [LA["QB0"], LA["QB1"]]
            QBt = [Tok("qb0"), Tok("qb1")]
            PT = [LA[f"P{i}"] for i in range(4)]
            PTt = [Tok(f"p{i}") for i in range(4)]
            OC, OT, SS = LA["OC"], LA["OT"], LA["SS"]
            JUNK = OT
            OCt = [Tok(f"oc{i}") for i in range(4)]
            OTt, SSt = Tok("ot"), Tok("ss")
            JUNKt = OTt
            Kd = [Tok(f"kd{c}") for c in range(KC)]
            Vd = [[Tok(f"vd{tt}_{h}") for h in range(NH)] for tt in range(8)]
            Qd = [Tok(f"qd{c}") for c in range(KC)]
            CCKt = [Tok(f"cck{h}") for h in range(NH)]
            CCVt = [Tok(f"ccv{h}") for h in range(NH)]
            stn = [0]

            LAM = SM[:, 0:1]
            P.op("dve", lambda e: e.scalar_tensor_tensor(out=JUNK[:, 0:128], in0=LAMW[:, 0:128], scalar=1.0, in1=LAMW[:, 128:256],
                                                         op0=ALU.mult, op1=ALU.mult, accum_out=SM[:, 1:2]),
                 reads=[CONSTt], writes=[JUNKt, SMt])
            P.op("dve", lambda e: e.scalar_tensor_tensor(out=JUNK[:, 128:256], in0=LAMW[:, 256:384], scalar=1.0, in1=LAMW[:, 384:512],
                                                         op0=ALU.mult, op1=ALU.mult, accum_out=SM[:, 2:3]),
                 reads=[CONSTt, SMt], writes=[JUNKt, SMt])
            P.op("act", lambda e: e.activation(out=SM[:, 3:5], in_=SM[:, 1:3], func=AF.Exp), reads=[SMt], writes=[SMt])
            P.op("dve", lambda e: e.tensor_tensor(out=SM[:, 5:6], in0=SM[:, 3:4], in1=SM[:, 4:5], op=ALU.subtract), reads=[SMt], writes=[SMt])
            P.op("dve", lambda e: e.tensor_scalar(out=LAM, in0=SM[:, 5:6], scalar1=float(LAMBDA_INIT), scalar2=None, op0=ALU.add),
                 reads=[SMt], writes=[SMt])
            P.op("dve", lambda e: e.tensor_scalar(out=SUBG[:, :], in0=SUBG[:, :], scalar1=float(1.0 - LAMBDA_INIT), scalar2=None, op0=ALU.mult),
                 reads=[CONSTt], writes=[CONSTt])

            for i in range(4):
                P.op("dve", lambda e, i=i: e.memset(VB[i].rearrange("p (k e) -> p k e", e=257)[:, :, 256:257], 1.0), writes=[VBt[i]])

            def evac_copy(n, out_ap, in_ap, reads, writes):
                if n % 2 == 0:
                    P.op("act", lambda e: e.activation(out=out_ap, in_=in_ap, func=AF.Copy), reads=reads, writes=writes)
                else:
                    P.op("dve", lambda e: e.tensor_copy(out=out_ap, in_=in_ap), reads=reads, writes=writes)

            def proj_fm(dst_rows, dtoks, t):
                if True:
                    ti = wi[0]
                    for ocl in range(4):
                        oc = t * 4 + ocl
                        b = stn[0] % NSTG
                        stn[0] += 1
                        for hf in range(2):
                            t0, t1 = hslice(hf)
                            bk = nb()
                            for k in range(KC):
                                off = (ocl * 16 + k) * 128
                                P.op("pe", lambda e, bk=bk, off=off, k=k, t0=t0, t1=t1, ti=ti: e.matmul(
                                    PSa[bk][:, :], lhsT=wslot(ti, off), rhs=Ak(k, t0, t1), start=(k == 0), stop=(k == KC - 1)),
                                     reads=[Wt[ti % NSLOT], At[k][hf]], writes=[PSt[bk]], signal=(k == KC - 1))
                            evac_copy(hf, STG[b][:, t0:t1], PSa[bk][:, :], [PSt[bk]], [STGt[b]])
                        P.op("sp", lambda e, b=b, oc=oc: e.dma_start(out=dst_rows(oc), in_=STG[b][:, :]),
                             reads=[STGt[b]], writes=[dtoks[oc]], dma=f"st{b}")
                    w_consume_done(ti)
                    wi[0] += 1

            def gather_k(t):
                for h in (2 * t, 2 * t + 1):
                    P.op("pool", lambda e, h=h: e.collective_compute("AllGather", ALU.bypass, replica_groups=GROUPS, ins=[ccik[h]], outs=[ccok[h]]),
                         reads=[Kd[2 * h], Kd[2 * h + 1]], writes=[CCKt[h]], dma=f"cck{h}", inc=1)

            def gather_v(cg):
                for h in (2 * cg, 2 * cg + 1):
                    P.op("pool", lambda e, h=h: e.collective_compute("AllGather", ALU.bypass, replica_groups=GROUPS, ins=[cciv[h]], outs=[ccov[h]]),
                         reads=[Vd[tt][h] for tt in range(8)], writes=[CCVt[h]], dma=f"ccv{h}", inc=1)

            def v_tile(cg):
                ti = wi[0]
                for tt in range(8):
                    b = stn[0] % NSTG
                    stn[0] += 1
                    bk = nb()
                    for k in range(KC):
                        P.op("pe", lambda e, bk=bk, k=k, tt=tt, ti=ti: e.matmul(
                            PSa[bk][:, :], lhsT=Ak(k, tt * 128, (tt + 1) * 128), rhs=wslot(ti, k * 512, 512), start=(k == 0), stop=(k == KC - 1)),
                             reads=[Wt[ti % NSLOT], At[k][tt // 4]], writes=[PSt[bk]], signal=(k == KC - 1))
                    evac_copy(tt, STG[b][:, 0:512], PSa[bk][:, :], [PSt[bk]], [STGt[b]])
                    for hh in range(2):
                        h = 2 * cg + hh
                        P.op("sp", lambda e, b=b, tt=tt, h=h, hh=hh: e.dma_start(out=cciv[h][tt * 128:(tt + 1) * 128, :], in_=STG[b][:, hh * 256:(hh + 1) * 256]),
                             reads=[STGt[b]], writes=[Vd[tt][h]], dma=(f"st{b}" if hh == 0 else f"su{b}"))
                w_consume_done(ti)
                wi[0] += 1

            for t in range(4):
                proj_fm(lambda oc: ccik[oc // 2][(oc % 2) * 128:(oc % 2 + 1) * 128, :], Kd, t)
                gather_k(t)
                v_tile(t)
                gather_v(t)
            for t in range(4):
                proj_fm(lambda oc: qT_d[oc * 128:(oc + 1) * 128, :], Qd, t)

            ACC = [0, 1, 2, 3]
            SCB = [4, 5, 6, 7]
            scn = [0]
            ptn = [0]
            qn = [0]
            onn = [0]
            LOOK = 3
            pending_pe = []
            pending_now = []
            OC2 = LA["OC2"]
            OC2t = [Tok(f"oc2_{i}") for i in range(4)]
            ONB = [LA[f"ON{i}"] for i in range(4)]
            ONBt = [Tok(f"on{i}") for i in range(4)]

            for h in range(NH):
                nsub = 512 // QBH[h]
                def load_v(h=h):
                    for blk in range(4):
                        vsrc = cciv[h] if blk == 3 else ccov[h][blk * 1024:(blk + 1) * 1024, :]
                        vsrc = vsrc.rearrange("(k p) e -> p k e", p=128)
                        rd = [CCVt[h]] if blk < 3 else [Vd[tt][h] for tt in range(8)]
                        P.op("sp", lambda e, blk=blk, vsrc=vsrc: e.dma_start(out=VB[blk].rearrange("p (k e) -> p k e", e=257)[:, :, 0:256], in_=vsrc),
                             reads=rd, writes=[VBt[blk]], dma=f"v{blk}")
                for qb in range(2):
                    for c in range(2):
                        ch = 2 * h + c
                        qi = qn[0] % 2
                        qn[0] += 1
                        P.op("sp", lambda e, qi=qi, ch=ch, qb=qb: e.dma_start(out=QB[qi][:, :], in_=qT_d[ch * 128:(ch + 1) * 128, qb * 512:(qb + 1) * 512]),
                             reads=[Qd[ch]], writes=[QBt[qi]], dma=f"q{qi}")
                        for blk in range(4):
                            rd = [CCKt[h]] if blk < 3 else [Kd[ch]]
                            ksrc = ccik[h][c * 128:(c + 1) * 128, :] if blk == 3 else ccok[h][blk * 256 + c * 128:blk * 256 + (c + 1) * 128, :]
                            P.op("sp", lambda e, blk=blk, ksrc=ksrc: e.dma_start(out=KB[blk][:, :], in_=ksrc),
                                 reads=rd, writes=[KBt[blk]], dma=f"k{blk}")
                        if qb == 0 and c == 0:
                            load_v()
                        work = []
                        for blk in range(3):
                            for kt in range(8):
                                work.append((blk, kt, 0, False))
                        for kt in range(8):
                            if kt < 4 * qb:
                                work.append((3, kt, 0, False))
                            elif kt <= 4 * qb + 3:
                                work.append((3, kt, kt - 4 * qb, True))
                        nW = len(work)
                        pis = [None] * nW

                        def emit_score(i):
                            blk, kt, m, diag = work[i]
                            c0 = 128 * m
                            sb = SCB[scn[0] % 4]
                            scn[0] += 1
                            pi = ptn[0] % 4
                            ptn[0] += 1
                            pis[i] = pi
                            P.op("pe", lambda e, sb=sb, blk=blk, kt=kt, qi=qi, c0=c0: e.matmul(
                                PSa[sb][:, c0:512], lhsT=KB[blk][:, kt * 128:(kt + 1) * 128], rhs=QB[qi][:, c0:512], start=True, stop=True),
                                 reads=[KBt[blk], QBt[qi]], writes=[PSt[sb]], signal=True)
                            for sub in range(nsub):
                                a0 = max(c0, sub * QBH[h])
                                a1 = (sub + 1) * QBH[h]
                                if a0 >= a1:
                                    continue
                                bc = BIDX[(h, blk, qb, kt, sub)]
                                P.op("act", lambda e, pi=pi, sb=sb, a0=a0, a1=a1, bc=bc: e.activation(
                                    out=PT[pi][:, a0:a1], in_=PSa[sb][:, a0:a1], func=AF.Exp, bias=BIAS[:, bc:bc + 1], scale=float(SCALE)),
                                     reads=[PSt[sb], CONSTt], writes=[PTt[pi]])
                            if diag:
                                P.op("dve", lambda e, pi=pi, c0=c0: e.tensor_tensor(out=PT[pi][:, c0:c0 + 128], in0=PT[pi][:, c0:c0 + 128], in1=TRI[:, :], op=ALU.mult),
                                     reads=[PTt[pi], CONSTt], writes=[PTt[pi]])

                        def emit_av(i):
                            blk, kt, m, diag = work[i]
                            pi = pis[i]
                            for qs in range(m, 4):
                                first = (i == 0)
                                lastk = (blk == 3 and kt == 4 * qb + qs)
                                P.op("pe", lambda e, qs=qs, pi=pi, blk=blk, kt=kt, first=first, lastk=lastk: e.matmul(
                                    PSa[ACC[qs]][:, 0:257], lhsT=PT[pi][:, qs * 128:(qs + 1) * 128], rhs=VB[blk][:, kt * 257:(kt + 1) * 257],
                                    start=first, stop=lastk),
                                     reads=[PTt[pi], VBt[blk]], writes=[PSt[ACC[qs]]], signal=(lastk or qs == 3))

                        for idx in range(nW + LOOK):
                            if idx < nW:
                                emit_score(idx)
                            if idx == 8 and pending_pe:
                                for fn_ in pending_pe:
                                    fn_()
                                pending_pe.clear()
                            if idx >= LOOK:
                                emit_av(idx - LOOK)
                        dstb = OC if c == 0 else OC2
                        dstt = OCt if c == 0 else OC2t
                        alias = [STGt[2], STGt[3]] if c == 0 else [STGt[4], STGt[5]]
                        for qs in range(4):
                            P.op("dve", lambda e, qs=qs, dstb=dstb: e.tensor_copy(out=dstb[:, qs * 257:(qs + 1) * 257], in_=PSa[ACC[qs]][:, 0:257]),
                                 reads=[PSt[ACC[qs]]], writes=[dstt[qs]] + alias)
                        if c == 1:
                            OC3 = OC.rearrange("p (q e) -> p q e", e=257)
                            OC23 = OC2.rearrange("p (q e) -> p q e", e=257)
                            P.op("dve", lambda e: e.reciprocal(out=SS[:, 0:4], in_=OC3[:, :, 256]), reads=OCt, writes=[SSt])
                            P.op("dve", lambda e: e.reciprocal(out=SS[:, 4:8], in_=OC23[:, :, 256]), reads=OC2t + [SSt], writes=[SSt])
                            P.op("dve", lambda e: e.tensor_scalar(out=SS[:, 4:8], in0=SS[:, 4:8], scalar1=LAM, scalar2=None, op0=ALU.mult),
                                 reads=[SSt, SMt], writes=[SSt])
                            for qs in range(4):
                                ocq = OC[:, qs * 257:qs * 257 + 256]
                                oc2q = OC2[:, qs * 257:qs * 257 + 256]
                                P.op("dve", lambda e, oc2q=oc2q, qs=qs: e.tensor_scalar(out=OT[:, :], in0=oc2q, scalar1=SS[:, 4 + qs:5 + qs], scalar2=None, op0=ALU.mult),
                                     reads=[OC2t[qs], SSt], writes=[OTt])
                                P.op("dve", lambda e, ocq=ocq, qs=qs: e.scalar_tensor_tensor(out=ocq, in0=ocq, scalar=SS[:, qs:qs + 1], in1=OT[:, :],
                                                                                            op0=ALU.mult, op1=ALU.subtract),
                                     reads=[OCt[qs], OTt, SSt], writes=[OCt[qs]])
                                P.op("dve", lambda e, ocq=ocq, qs=qs: e.scalar_tensor_tensor(out=JUNK[:, :], in0=ocq, scalar=1.0, in1=ocq, op0=ALU.mult, op1=ALU.mult,
                                                                                            accum_out=SS[:, 8 + qs:9 + qs]),
                                     reads=[OCt[qs], SSt], writes=[JUNKt, SSt])
                            P.op("act", lambda e: e.activation(out=SS[:, 12:16], in_=SS[:, 8:12], func=AF.Ln, bias=EPSC[:, 0:1], scale=1.0 / 256),
                                 reads=[SSt, ONESt], writes=[SSt])
                            P.op("act", lambda e: e.activation(out=SS[:, 12:16], in_=SS[:, 12:16], func=AF.Exp, scale=-0.5),
                                 reads=[SSt], writes=[SSt])
                            for qs in range(4):
                                tq = (qb * 4 + qs) * 128
                                ocq = OC[:, qs * 257:qs * 257 + 256]
                                oi = onn[0] % 4
                                onn[0] += 1
                                P.op("dve", lambda e, ocq=ocq, qs=qs, oi=oi: e.scalar_tensor_tensor(out=ONB[oi][:, :], in0=ocq, scalar=SS[:, 12 + qs:13 + qs], in1=SUBG[:, :],
                                                                                                   op0=ALU.mult, op1=ALU.mult),
                                     reads=[OCt[qs], SSt, CONSTt], writes=[ONBt[oi]])

                                def tr_fn(oi=oi, tq=tq, h=h):
                                    for j in range(2):
                                        sb = SCB[scn[0] % 4]
                                        scn[0] += 1
                                        pv = PSa[sb].bitcast(BF16)
                                        P.op("pe", lambda e, pv=pv, j=j, oi=oi: e.transpose(out=pv[:, 0:128], in_=ONB[oi][:, j * 128:(j + 1) * 128], identity=IDENT[:, :]),
                                             reads=[ONBt[oi], CONSTt], writes=[PSt[sb]], signal=True)
                                        kk = 2 * h + j
                                        P.op("dve", lambda e, pv=pv, kk=kk, tq=tq: e.tensor_copy(out=Ak(kk, tq, tq + 128), in_=pv[:, 0:128]),
                                             reads=[PSt[sb]], writes=[At[kk][tq // 512]])
                                if qs % 2 == 1:
                                    pending_now.append(tr_fn)
                                else:
                                    pending_now.append(tr_fn)
                            pending_pe.extend(pending_now)
                            pending_now.clear()
            for fn_ in pending_pe:
                fn_()
            pending_pe.clear()

            for t in range(4):
                ti = wi[0]
                for ocl in range(4):
                    oc = t * 4 + ocl
                    for hf in range(2):
                        t0, t1 = hslice(hf)
                        bk = nb()
                        for k in range(KC):
                            off = (ocl * 16 + k) * 128
                            P.op("pe", lambda e, bk=bk, off=off, k=k, t0=t0, t1=t1, ti=ti: e.matmul(
                                PSa[bk][:, :], lhsT=wslot(ti, off), rhs=Ak(k, t0, t1), start=(k == 0), stop=(k == KC - 1)),
                                 reads=[Wt[ti % NSLOT], At[k][hf]], writes=[PSt[bk]], signal=(k == KC - 1))
                        P.op("dve", lambda e, bk=bk, oc=oc, t0=t0, t1=t1: e.tensor_tensor(out=Hk(oc, t0, t1), in0=Hk(oc, t0, t1), in1=PSa[bk][:, :], op=ALU.add),
                             reads=[PSt[bk], Ht[oc][hf]], writes=[Ht[oc][hf]])
                w_consume_done(ti)
                wi[0] += 1

        pool_mixer()
        dump(0)
        P.barrier()
        if stop >= 2:
            ffn(C_NFFN0)
            dump(1)
            P.barrier()
        if stop >= 3:
            attention()
            dump(2)
            P.barrier()
        if stop >= 4:
            ffn(C_NFFN1)
            dump(3)
            P.barrier()
        norm_stats(False)
        OUT = [LO["OUT0"], LO["OUT1"]]
        OUTt = [Tok("out0"), Tok("out1")]
        for k in range(KC):
            b = k % 2
            P.op("dve", lambda e, k=k, b=b: e.scalar_tensor_tensor(out=OUT[b][:, :], in0=Hk(k), scalar=col(C_FINAL + k), in1=RSTD[:, HALO:TE],
                                                                  op0=ALU.mult, op1=ALU.mult),
                 reads=[Ht[k][0], Ht[k][1], RSTDt, CONSTt], writes=[OUTt[b]])
            P.op("sp", lambda e, k=k, b=b: e.dma_start(out=out_d[:, k * T:(k + 1) * T], in_=OUT[b][:, :]), reads=[OUTt[b]], dma=f"o{b}")
        assert wi[0] == NT, (wi[0], NT)
        P.final_wait("sp", ["o0", "o1", "dbg"])

        sems = {}
        for key in P.semkeys:
            sems[key] = es.enter_context(nc.semaphore(f"s_{key}"))
        block = es.enter_context(nc.Block())

        def run(e, name):
            for item in P.q[name]:
                kind = item[0]
                if kind == "wait":
                    e.wait_ge(sems[item[1]], item[2])
                elif kind == "dma":
                    item[1](e).then_inc(sems[item[2]], item[3])
                elif kind == "sig":
                    item[1](e).then_inc(sems[item[2]], 1)
                else:
                    item[1](e)

        @block.tensor
        def _(e):
            run(e, "pe")

        @block.scalar
        def _(e):
            run(e, "act")

        @block.vector
        def _(e):
            run(e, "dve")

        @block.gpsimd
        def _(e):
            run(e, "pool")

        @block.sync
        def _(e):
            run(e, "sp")
    return nc


def _fm(a, n):
    return np.ascontiguousarray(a.T.reshape(KC, 128, n).transpose(1, 0, 2).reshape(128, KC * n))


def _colvec(v):
    return v.reshape(KC, 128).T


DEBUG = False
STOP = 4
_LAST = {}


def kernel(x, norm_mix, norm_ffn, pool_w, pool_scale, w_qkv, lambda_q1, lambda_k1, lambda_q2, lambda_k2,
           subln_g, w_o, w_gate, w_up, w_down, final_norm):
    inp = dict(x=x, norm_mix=norm_mix, norm_ffn=norm_ffn, pool_w=pool_w, pool_scale=pool_scale, w_qkv=w_qkv,
               w_o=w_o, w_gate=w_gate, w_up=w_up, w_down=w_down)
    inp = {k: np.asarray(v, dtype=np.float32) for k, v in inp.items()}
    x = inp["x"]
    wpack = pack_weights(inp)
    cols = np.zeros((128, NCOLS), np.float32)
    cols[:, C_NMIX0:C_NMIX0 + 16] = _colvec(inp["norm_mix"][0])
    cols[:, C_NFFN0:C_NFFN0 + 16] = _colvec(inp["norm_ffn"][0])
    cols[:, C_PSCALE:C_PSCALE + 16] = _colvec(inp["pool_scale"][0])
    cols[:, C_NMIX1:C_NMIX1 + 16] = _colvec(inp["norm_mix"][1])
    cols[:, C_NFFN1:C_NFFN1 + 16] = _colvec(inp["norm_ffn"][1])
    cols[:, C_FINAL:C_FINAL + 16] = _colvec(np.asarray(final_norm, np.float32))
    lamw = np.concatenate([np.asarray(v, np.float32).reshape(1, 128) for v in (lambda_q1, lambda_k1, lambda_q2, lambda_k2)], axis=1)
    lamw = np.ascontiguousarray(np.broadcast_to(lamw, (128, 512)))
    subg = np.ascontiguousarray(np.broadcast_to(np.asarray(subln_g, np.float32).reshape(1, 256), (128, 256)))
    tri = (np.arange(128)[None, :] >= np.arange(128)[:, None]).astype(ml_dtypes.bfloat16)
    ident = np.eye(128).astype(ml_dtypes.bfloat16)

    in_maps = []
    for c in range(NCORES):
        b, r = c // 4, c % 4
        xs = x[b, r * T:(r + 1) * T, :]
        if r == 0:
            xh = np.zeros((HALO, D), np.float32)
        else:
            xh = x[b, r * T - HALO:r * T, :]
        inv16 = np.zeros((128, 64), np.float32)
        for g in range(4):
            w = 2 << g
            tpos = r * T + np.arange(16)
            inv16[:, g * 16:(g + 1) * 16] = (1.0 / np.minimum(tpos + 1, w)).astype(np.float32)[None, :]
        in_maps.append({
            "xT": _fm(xs, T), "xh": _fm(xh, HALO), "cols": cols, "bias": make_bias(r), "inv16": inv16,
            "tri": tri, "ident": ident, "lamw": lamw, "subg": subg, "w": wpack,
        })
    nc = build_program(debug=DEBUG, stop=STOP)
    for m in in_maps:
        m["w"] = wpack[:NEED_TILES[STOP]]
    res = run_bass_kernel_spmd(nc, in_maps, core_ids=list(range(NCORES)))
    out = np.zeros((2, 4096, D), np.float32)
    for c in range(NCORES):
        b, r = c // 4, c % 4
        o = np.asarray(res.results[c]["out"]).reshape(128, KC, T).transpose(2, 1, 0).reshape(T, D)
        out[b, r * T:(r + 1) * T, :] = o
    if DEBUG:
        _LAST["dbg"] = [np.asarray(res.results[c]["dbg"]) for c in range(NCORES)]
    return out
```
